# Optimizing a Trainium2 kernel written in Bass

```python
import math
import jax, jax.numpy as jnp
from jax import lax
import numpy as np

D_MODEL = 1024
BATCH = 16
SEQ = 2048
DEPTH = 4

N_MIXERS = 2
N_A = (DEPTH + 1) // 2
N_B = DEPTH // 2
RMS_EPS = 1e-6

A_GROUPS = ((128, 1), (512, 4), (2048, 16))
A_N_GROUPS = len(A_GROUPS)
A_HEADS_PER_GROUP = 8
A_HEAD_DIM = 64
A_HEADS = A_N_GROUPS * A_HEADS_PER_GROUP
A_QKV_COLS = 3 * A_HEADS * A_HEAD_DIM
A_MERGED = A_HEADS_PER_GROUP * A_HEAD_DIM
ROPE_THETA = 10000.0
NEG_INF = -1e30

B_HEADS = 4
B_KEY_DIM = D_MODEL // 2
B_VAL_DIM = D_MODEL
B_DK = B_KEY_DIM // B_HEADS
B_DV = B_VAL_DIM // B_HEADS
B_GATE_RANK = 16
B_GATE_NORMALIZER = 16.0
B_CHUNK = 64
B_IN_COLS = 2 * B_KEY_DIM + 2 * B_VAL_DIM + B_GATE_RANK

FFN_DIM = 2816
CONV_WIDTH = 3

kernel_name = "hybrid_dilated_attn_gla_convffn"


def rms_norm(x, g):
    xf = x.astype(jnp.float32)
    y = xf * lax.rsqrt(jnp.mean(xf * xf, axis=-1, keepdims=True) + RMS_EPS)
    return (y * g.astype(jnp.float32)).astype(x.dtype)


def rope_tables(positions, dim):
    inv = ROPE_THETA ** (-jnp.arange(0, dim, 2, dtype=jnp.float32) / dim)
    ang = positions.astype(jnp.float32)[..., None] * inv
    return jnp.cos(ang), jnp.sin(ang)


def apply_rope(x, cos, sin):
    x1, x2 = jnp.split(x.astype(jnp.float32), 2, axis=-1)
    c = cos[:, :, None, :]
    s = sin[:, :, None, :]
    return jnp.concatenate([x1 * c - x2 * s, x2 * c + x1 * s], axis=-1).astype(x.dtype)


def dilated_window_attention(q, k, v, span, dilation):
    B, S, H, dh = q.shape
    L = S // dilation
    nb = -(-L // span)
    Lp = nb * span

    def to_blocks(t):
        t = t.reshape(B, L, dilation, H, dh).transpose(0, 2, 1, 3, 4)
        t = jnp.pad(t, ((0, 0), (0, 0), (0, Lp - L), (0, 0), (0, 0)))
        return t.reshape(B, dilation, nb, span, H, dh)

    qb, kb, vb = to_blocks(q), to_blocks(k), to_blocks(v)

    def with_prev(t):
        prev = jnp.pad(t, ((0, 0), (0, 0), (1, 0), (0, 0), (0, 0), (0, 0)))[:, :, :-1]
        return jnp.concatenate([prev, t], axis=3)

    kk, vv = with_prev(kb), with_prev(vb)
    s = jnp.einsum('brnqhd,brnkhd->brnhqk', qb, kk,
                   preferred_element_type=jnp.float32)
    i = jnp.arange(span)[:, None]
    c = jnp.arange(2 * span)[None, :]
    dist = span + i - c
    band = (dist >= 0) & (dist <= span)
    not_first = (jnp.arange(nb) > 0)[:, None, None]
    mask = band[None] & (not_first | (c >= span)[None])
    s = jnp.where(mask[None, None, :, None], s, NEG_INF)
    lse = jax.nn.logsumexp(s, axis=-1)
    p = jnp.exp(s - lse[..., None])
    o = jnp.einsum('brnhqk,brnkhd->brnqhd', p.astype(v.dtype), vv)
    o = o.reshape(B, dilation, Lp, H, dh)[:, :, :L]
    o = o.transpose(0, 2, 1, 3, 4).reshape(B, S, H, dh)
    lse = lse.transpose(0, 1, 2, 4, 3).reshape(B, dilation, Lp, H)[:, :, :L]
    lse = lse.transpose(0, 2, 1, 3).reshape(B, S, H)
    return o, lse


def mixer_dilated(h, w_qkv, w_o, cos, sin):
    B, S, _ = h.shape
    qkv = (h @ w_qkv).reshape(B, S, 3, A_HEADS, A_HEAD_DIM)
    q = apply_rope(qkv[:, :, 0], cos, sin) * (A_HEAD_DIM ** -0.5)
    k = apply_rope(qkv[:, :, 1], cos, sin)
    v = qkv[:, :, 2]
    outs, lses = [], []
    for g, (window, dilation) in enumerate(A_GROUPS):
        sl = slice(g * A_HEADS_PER_GROUP, (g + 1) * A_HEADS_PER_GROUP)
        o, lse = dilated_window_attention(q[:, :, sl], k[:, :, sl], v[:, :, sl],
                                          window // dilation, dilation)
        outs.append(o)
        lses.append(lse)
    o = jnp.stack(outs, axis=0)
    lse = jnp.stack(lses, axis=0)
    wgt = jax.nn.softmax(lse, axis=0)
    merged = jnp.einsum('gbsh,gbshd->bshd', wgt.astype(o.dtype), o)
    return merged.reshape(B, S, A_MERGED) @ w_o


def gla_chunked(q, k, v, glog):
    B, S, H, dk = q.shape
    dv = v.shape[-1]
    C = B_CHUNK
    n = S // C

    def chunks(t):
        return t.astype(jnp.float32).reshape(B, n, C, H, -1).transpose(1, 0, 3, 2, 4)

    qc, kc, vc, gc = chunks(q), chunks(k), chunks(v), chunks(glog)
    b = jnp.cumsum(gc, axis=3)
    b_mid = b[:, :, :, C // 2 - 1:C // 2]
    b_last = b[:, :, :, -1:]
    a = jnp.einsum('nbhid,nbhjd->nbhij', qc * jnp.exp(b - b_mid), kc * jnp.exp(b_mid - b))
    causal = jnp.tril(jnp.ones((C, C), dtype=bool))
    a = jnp.where(causal, a, 0.0)
    o_intra = jnp.einsum('nbhij,nbhjv->nbhiv', a, vc)
    q_inter = qc * jnp.exp(b)
    k_state = kc * jnp.exp(b_last - b)
    decay = jnp.exp(b_last[:, :, :, 0])

    def step(state, xs):
        qi, ks, vi, dec = xs
        o = jnp.einsum('bhid,bhdv->bhiv', qi, state)
        state = dec[..., None] * state + jnp.einsum('bhjd,bhjv->bhdv', ks, vi)
        return state, o

    state0 = jnp.zeros((B, H, dk, dv), jnp.float32)
    _, o_inter = lax.scan(step, state0, (q_inter, k_state, vc, decay))
    o = o_intra + o_inter
    return o.transpose(1, 0, 3, 2, 4).reshape(B, S, H, dv)


def mixer_gla(h, w_in, w_gate_up, b_gate_up, g_norm, w_o):
    B, S, _ = h.shape
    proj = h @ w_in
    splits = [B_KEY_DIM, 2 * B_KEY_DIM, 2 * B_KEY_DIM + B_VAL_DIM, 2 * B_KEY_DIM + 2 * B_VAL_DIM]
    q, k, v, r, gdown = jnp.split(proj, splits, axis=-1)
    glog = jax.nn.log_sigmoid((gdown @ w_gate_up + b_gate_up).astype(jnp.float32)) / B_GATE_NORMALIZER
    q = q.reshape(B, S, B_HEADS, B_DK) * (B_DK ** -0.5)
    k = k.reshape(B, S, B_HEADS, B_DK)
    v = v.reshape(B, S, B_HEADS, B_DV)
    o = gla_chunked(q, k, v, glog.reshape(B, S, B_HEADS, B_DK))
    o = rms_norm(o, g_norm).astype(h.dtype).reshape(B, S, B_VAL_DIM)
    return (o * jax.nn.silu(r)) @ w_o


def conv_ffn(h, w_in, conv_w, conv_b, w_down):
    a, u = jnp.split(h @ w_in, 2, axis=-1)
    a = lax.conv_general_dilated(a, conv_w[:, None, :].astype(a.dtype), window_strides=(1,),
                                 padding=[(CONV_WIDTH - 1, 0)],
                                 dimension_numbers=('NWC', 'WIO', 'NWC'),
                                 feature_group_count=FFN_DIM) + conv_b
    return (jax.nn.silu(a) * u) @ w_down


def setup_inputs(seed: int = 0) -> dict:
    key = jax.random.key(seed)
    ks = jax.random.split(key, 20)
    f32 = jnp.float32
    res_scale = (2.0 * DEPTH) ** -0.5

    def nrm(k, shape, scale):
        return jax.random.normal(k, shape, f32) * scale

    x = jax.random.normal(ks[0], (BATCH, SEQ, D_MODEL), f32)
    positions = jnp.broadcast_to(jnp.arange(SEQ, dtype=jnp.int32), (BATCH, SEQ))
    return {
        'x': x,
        'positions': positions,
        'norm_mix': 1.0 + nrm(ks[1], (DEPTH, D_MODEL), 0.02),
        'norm_ffn': 1.0 + nrm(ks[2], (DEPTH, D_MODEL), 0.02),
        'a_w_qkv': nrm(ks[3], (N_A, D_MODEL, A_QKV_COLS), D_MODEL ** -0.5),
        'a_w_o': nrm(ks[4], (N_A, A_MERGED, D_MODEL), A_MERGED ** -0.5 * res_scale),
        'b_w_in': nrm(ks[5], (N_B, D_MODEL, B_IN_COLS), D_MODEL ** -0.5),
        'b_w_gate_up': nrm(ks[6], (N_B, B_GATE_RANK, B_KEY_DIM), B_GATE_RANK ** -0.5),
        'b_b_gate_up': nrm(ks[7], (N_B, B_KEY_DIM), 0.1),
        'b_g_norm': 1.0 + nrm(ks[8], (N_B, B_DV), 0.02),
        'b_w_o': nrm(ks[9], (N_B, B_VAL_DIM, D_MODEL), B_VAL_DIM ** -0.5 * res_scale),
        'f_w_in': nrm(ks[10], (DEPTH, D_MODEL, 2 * FFN_DIM), D_MODEL ** -0.5),
        'f_conv_w': nrm(ks[11], (DEPTH, CONV_WIDTH, FFN_DIM), CONV_WIDTH ** -0.5),
        'f_conv_b': nrm(ks[12], (DEPTH, FFN_DIM), 0.02),
        'f_w_down': nrm(ks[13], (DEPTH, FFN_DIM, D_MODEL), FFN_DIM ** -0.5 * res_scale),
        'norm_final': 1.0 + nrm(ks[14], (D_MODEL,), 0.02),
    }


def reference(x, positions, norm_mix, norm_ffn, a_w_qkv, a_w_o, b_w_in, b_w_gate_up,
              b_b_gate_up, b_g_norm, b_w_o, f_w_in, f_conv_w, f_conv_b, f_w_down, norm_final):
    cos, sin = rope_tables(positions, A_HEAD_DIM)
    for i in range(DEPTH):
        h = rms_norm(x, norm_mix[i])
        j = i // N_MIXERS
        if i % N_MIXERS == 0:
            x = x + mixer_dilated(h, a_w_qkv[j], a_w_o[j], cos, sin)
        else:
            x = x + mixer_gla(h, b_w_in[j], b_w_gate_up[j], b_b_gate_up[j], b_g_norm[j], b_w_o[j])
        h = rms_norm(x, norm_ffn[i])
        x = x + conv_ffn(h, f_w_in[i], f_conv_w[i], f_conv_b[i], f_w_down[i])
    return rms_norm(x, norm_final)
```

```python
from contextlib import ExitStack
import numpy as np
import concourse.bass as bass
import concourse.mybir as mybir
from concourse.bass_utils import run_bass_kernel_spmd

F32 = mybir.dt.float32
BF16 = mybir.dt.bfloat16
I32 = mybir.dt.int32
AF = mybir.ActivationFunctionType
ALU = mybir.AluOpType

S = 2048
D = 1024
NCH = 8
DEPTH = 4
FFN = 2816
NFC = 22
FGRP = 11
NEG = -30000.0
TWO_PI = 6.283185307179586

ENGS = ["pe", "act", "dve", "pool", "sp"]
RECENT = 4
EMBED = True
PUSHBACK = False

C_ID = 0
C_TRI = 128
C_TRS = 256
C_MASK = 384
C_ONES = 896
C_INV = 1024
C_SGN = 1025
C_EPS = 1026
C_2PI = 1027
C_ONE = 1028
C_PERM = 1032
NCONST = 1160


class MK:
    def __init__(self, nc, semstack):
        self.nc = nc
        self.semstack = semstack
        self.streams = {e: [] for e in ENGS}
        self.nops = {e: 0 for e in ENGS}
        self.know = {e: {} for e in ENGS}
        self.pushable = set()
        self.tokclock = {}
        self.lastw = {}
        self.readers = {}
        self.dma_issued = {}
        self.sems = {}
        self.n_waits = 0
        self.n_instr = 0

    def _sem(self, key):
        s = self.sems.get(key)
        if s is None:
            s = self.semstack.enter_context(self.nc.semaphore("s%d" % len(self.sems)))
            self.sems[key] = s
        return s

    def _deps(self, reads, writes, rhs=()):
        deps = set()
        nonbank = set()
        lw = self.lastw
        for k in reads:
            t = lw.get(k)
            if t is not None:
                deps.add(t)
                nonbank.add(t)
        for k in rhs:
            t = lw.get(k)
            if t is not None:
                deps.add(t)
        for k in writes:
            isbank = isinstance(k, tuple) and k[0] == "ps"
            t = lw.get(k)
            if t is not None:
                deps.add(t)
                if not isbank:
                    nonbank.add(t)
            r = self.readers.get(k)
            if r:
                deps.update(r)
                if not isbank:
                    nonbank.update(r)
        self._nonbank = nonbank
        return deps

    def _reduce(self, eng, deps, noself=False):
        need = {}
        nb = getattr(self, "_nonbank", ())
        nbsem = set()
        for tok in deps:
            (semkey, val) = tok
            if semkey[0] == "dma":
                val = self.dma_issued[semkey[1]]
            elif semkey[1] == eng:
                if eng == "pe" or noself:
                    continue
                if self.nops[eng] - (val - 1) > RECENT:
                    continue
            if tok in nb:
                nbsem.add(semkey)
            if need.get(semkey, 0) < val:
                need[semkey] = val
        self._nbsem = nbsem
        k = self.know[eng]
        tc = self.tokclock
        items = [(sk, v) for sk, v in need.items() if k.get(sk, 0) < v]
        if len(items) > 1:
            items.sort(key=lambda kv: -sum(tc.get(kv, {}).values()))
        out = []
        for semkey, val in items:
            if k.get(semkey, 0) >= val:
                continue
            out.append((semkey, val))
            k[semkey] = val
            clk = tc.get((semkey, val))
            if clk:
                for s2, v2 in clk.items():
                    if k.get(s2, 0) < v2:
                        k[s2] = v2
        self.n_waits += len(out)
        return out

    def _commit(self, tok, reads, writes):
        rd = self.readers
        for k in reads:
            l = rd.get(k)
            if l is None:
                rd[k] = [tok]
            else:
                l.append(tok)
        for k in writes:
            self.lastw[k] = tok
            rd[k] = []

    def op(self, eng, fn, reads=(), writes=(), noself=False, rhs=()):
        waits = self._reduce(eng, self._deps(reads, writes, rhs), noself)
        emb = None
        if EMBED and waits:
            cand = list(waits)
            if cand:
                emb = cand[-1]
                waits = [w for w in waits if w is not emb]
        n = self.nops[eng]
        semkey = ("eng", eng)
        tok = (semkey, n + 1)
        clk = dict(self.know[eng])
        clk[semkey] = n + 1
        self.tokclock[tok] = clk
        if PUSHBACK and eng == "pe" and waits:
            for w in waits:
                if w[0] not in self._nbsem:
                    self.pushable.add((eng, n + 1, w))
        self.streams[eng].append((waits, fn, (semkey, 1), emb, n + 1))
        self.nops[eng] = n + 1
        self._commit(tok, tuple(reads) + tuple(rhs), writes)
        self.n_instr += 1
        return tok

    def dma(self, eng, fn, slot, reads=(), writes=()):
        waits = self._reduce(eng, self._deps(reads, writes))
        semkey = ("dma", slot)
        tot = self.dma_issued.get(slot, 0) + 16
        self.dma_issued[slot] = tot
        tok = (semkey, tot)
        clk = dict(self.know[eng])
        clk[semkey] = tot
        self.tokclock[tok] = clk
        self.streams[eng].append((waits, fn, (semkey, 16), None, None))
        self._commit(tok, reads, writes)
        self.n_instr += 1
        return tok

    def wait(self, eng, toks):
        waits = self._reduce(eng, set(toks))
        if waits:
            self.streams[eng].append((waits, None, None, None, None))

    def barrier(self):
        toks = [(("eng", e), self.nops[e]) for e in ENGS if self.nops[e] > 0]
        toks += [(("dma", s), v) for s, v in self.dma_issued.items()]
        for e in ENGS:
            self.wait(e, [t for t in toks if t[0] != ("eng", e)])

    def _pushback(self, name, st):
        me = ("eng", name)
        tc = self.tokclock
        for i in range(1, len(st)):
            waits, fn, inc, emb, idx = st[i]
            if not waits:
                continue
            keep = []
            for w in waits:
                clk = tc.get(w)
                placed = False
                if clk is not None and (name, idx, w) in self.pushable:
                    lim = clk.get(me, 0)
                    for j in range(i - 1, max(i - 4, -1), -1):
                        wj, fnj, incj, embj, idxj = st[j]
                        if fnj is None or idxj is None:
                            continue
                        if idxj <= lim:
                            break
                        if embj is None:
                            st[j] = (wj, fnj, incj, w, idxj)
                            placed = True
                            break
                if not placed:
                    keep.append(w)
            st[i] = (keep, fn, inc, emb, idx)
            self.n_pushed = getattr(self, "n_pushed", 0) + len(waits) - len(keep)

    def emit_block(self):
        nc = self.nc
        streams = self.streams
        if PUSHBACK:
            for name in ENGS:
                self._pushback(name, streams[name])
        for e in ENGS:
            for waits, fn, inc, emb, _idx in streams[e]:
                for semkey, _ in waits:
                    self._sem(semkey)
                if emb is not None:
                    self._sem(emb[0])
                if inc is not None:
                    self._sem(inc[0])
        sems = self.sems
        with nc.Block() as block:
            def run(name, e):
                for waits, fn, inc, emb, _idx in streams[name]:
                    for semkey, val in waits:
                        e.wait_ge(sems[semkey], val)
                    if fn is not None:
                        ins = fn(e)
                        if emb is not None:
                            ins._wait_ge(sems[emb[0]], emb[1])
                        ins.then_inc(sems[inc[0]], inc[1])

            @block.tensor
            def _(e):
                run("pe", e)

            @block.scalar
            def _(e):
                run("act", e)

            @block.vector
            def _(e):
                run("dve", e)

            @block.gpsimd
            def _(e):
                run("pool", e)

            @block.sync
            def _(e):
                run("sp", e)
        self.streams = {e: [] for e in ENGS}


class Builder:
    def __init__(self, nseq, plan, final_norm=True):
        self.nseq = nseq
        self.plan = plan
        self.final_norm = final_norm
        nc = bass.Bass("TRN2", target_bir_lowering=False)
        self.nc = nc
        dt = nc.dram_tensor
        self.x = dt("x", [nseq, S, D], F32, kind="ExternalInput").ap()
        self.pos = dt("pos", [nseq, S], I32, kind="ExternalInput").ap()
        self.a_w_qkv = dt("a_w_qkv", [2, D, 4608], F32, kind="ExternalInput").ap()
        self.a_w_o = dt("a_w_o", [2, 512, D], F32, kind="ExternalInput").ap()
        self.b_w_in = dt("b_w_in", [2, D, 3088], F32, kind="ExternalInput").ap()
        self.b_w_o = dt("b_w_o", [2, D, D], F32, kind="ExternalInput").ap()
        self.f_w_in = dt("f_w_in", [4, D, 2 * FFN], F32, kind="ExternalInput").ap()
        self.f_w_down = dt("f_w_down", [4, FFN, D], F32, kind="ExternalInput").ap()
        self.cst = dt("cst", [128, NCONST], F32, kind="ExternalInput").ap()
        self.gains_d = dt("gains", [128, 72], F32, kind="ExternalInput").ap()
        self.convp_d = dt("convp", [128, 4 * 4 * NFC], F32, kind="ExternalInput").ap()
        self.gnorm_d = dt("gnorm", [128, 4], F32, kind="ExternalInput").ap()
        self.wgu_d = dt("wgu", [32, 2 * 512], F32, kind="ExternalInput").ap()
        self.y = dt("y", [nseq, S, D], F32, kind="ExternalOutput").ap()
        self.nb = 0

    def bank(self):
        b = self.nb % 8
        self.nb += 1
        return b

    def build(self):
        nc = self.nc
        with ExitStack() as outer:
            mk = MK(nc, outer)
            self.mk = mk
            T = lambda name, shape, dt_: outer.enter_context(nc.sbuf_tensor(name, shape, dt_))
            self.xT = T("xT", [128, NCH, S], F32)
            self.hT = T("hT", [128, NCH, S], BF16)
            self.cf = T("cf", [128, NCONST], F32)
            self.cb = T("cb", [128, NCONST], BF16)
            self.gains = T("gains_s", [128, 72], F32)
            self.convp = T("convp_s", [128, 4 * 4 * NFC], F32)
            self.gnorm = T("gnorm_s", [128, 4], F32)
            self.wgu = T("wgu_s", [32, 2 * 512], F32)
            self.ps = [outer.enter_context(nc.psum_tensor("ps%d" % i, [128, 512], F32)) for i in range(8)]
            self.nsq = [T("nsq%d" % i, [128, 512], BF16) for i in range(2)]
            self.nrs = [T("nrs%d" % i, [128, 512], F32) for i in range(2)]
            self.nq = 0
            mk.dma("sp", lambda e: e.dma_start(out=self.cf[:], in_=self.cst), slot="c0", writes=["cf"])
            mk.dma("pool", lambda e: e.dma_start(out=self.cb[:], in_=self.cst), slot="c1", writes=["cb"])
            mk.dma("sp", lambda e: e.dma_start(out=self.gains[:], in_=self.gains_d), slot="c2", writes=["gains"])
            mk.dma("sp", lambda e: e.dma_start(out=self.convp[:], in_=self.convp_d), slot="c3", writes=["convp"])
            mk.dma("sp", lambda e: e.dma_start(out=self.gnorm[:], in_=self.gnorm_d), slot="c4", writes=["gnorm"])
            mk.dma("sp", lambda e: e.dma_start(out=self.wgu[:], in_=self.wgu_d), slot="c5", writes=["wgu"])
            mk.emit_block()
            for pi, ph in enumerate(self.plan):
                kind = ph[0]
                nxt = self.plan[pi + 1] if pi + 1 < len(self.plan) else None
                self.next_phase = nxt
                self.was_prefetched = getattr(self, "prefetched", False)
                self.prefetched = False
                if nxt is None or nxt[0] in ("store", "load"):
                    self.next_goff = None
                elif nxt[0] == "F":
                    self.next_goff = 32 + nxt[2] * 8
                else:
                    self.next_goff = nxt[2] * 8
                with ExitStack() as st:
                    self.st = st
                    mk.barrier()
                    if kind == "load":
                        self.phase_load(ph[1])
                    elif kind == "A":
                        self.phase_A(ph[1], ph[2])
                    elif kind == "B":
                        self.phase_B(ph[1], ph[2])
                    elif kind == "F":
                        self.phase_F(ph[1], ph[2])
                    elif kind == "store":
                        self.phase_store(ph[1])
                    mk.emit_block()
        return nc

    def alloc_pf(self):
        self.pf = self.T("pf", [128, 4096], BF16)
        return self.pf

    def prefetch_next(self, cur_keys):
        nxt = self.next_phase
        self.prefetched = False
        if nxt is None or nxt[0] not in ("A", "B", "F"):
            return
        mk, pf = self.mk, self.pf
        l = nxt[2]
        if nxt[0] == "A":
            w3 = self.a_w_qkv[l // 2].rearrange("(c p) n -> p c n", p=128)
            dst = pf[:, 0:3072].rearrange("p (c n) -> p c n", c=NCH)
            for kind in range(3):
                col0 = kind * 1536 + 2 * 512 + 0 * 128
                mk.dma("pool", lambda e, kind=kind, col0=col0: e.dma_start(out=dst[:, :, kind * 128:(kind + 1) * 128], in_=w3[:, :, col0:col0 + 128]),
                       slot=("wA", 0), writes=[("wA", 0)] + cur_keys)
        elif nxt[0] == "F":
            w3 = self.f_w_in[l].rearrange("(c p) n -> p c n", p=128)
            dst = pf[:, 0:2048].rearrange("p (c n) -> p c n", c=NCH)
            mk.dma("pool", lambda e: e.dma_start(out=dst[:, :, 0:128], in_=w3[:, :, 0:128]), slot=("win", 0), writes=[("win", 0)] + cur_keys)
            mk.dma("pool", lambda e: e.dma_start(out=dst[:, :, 128:256], in_=w3[:, :, FFN:FFN + 128]), slot=("win", 0), writes=[("win", 0)] + cur_keys)
        else:
            w3 = self.b_w_in[l // 2].rearrange("(c p) n -> p c n", p=128)
            dq = pf[:, 0:1024].rearrange("p (c n) -> p c n", c=NCH)
            dk = pf[:, 1024:2048].rearrange("p (c n) -> p c n", c=NCH)
            dv = pf[:, 2048:4096].rearrange("p (c n) -> p c n", c=NCH)
            mk.dma("pool", lambda e: e.dma_start(out=dq, in_=w3[:, :, 0:128]), slot=("wq", 0), writes=[("wq", 0)] + cur_keys)
            mk.dma("pool", lambda e: e.dma_start(out=dk, in_=w3[:, :, 512:640]), slot=("wk", 0), writes=[("wk", 0)] + cur_keys)
            mk.dma("pool", lambda e: e.dma_start(out=dv, in_=w3[:, :, 1024:1280]), slot="wv", writes=["wv"] + cur_keys)
        self.prefetched = True

    def T(self, name, shape, dt_):
        self.tcount = getattr(self, "tcount", 0) + 1
        return self.st.enter_context(self.nc.sbuf_tensor("%s_%d" % (name, self.tcount), shape, dt_))

    def phase_load(self, s):
        mk, ps, xT, cf = self.mk, self.ps, self.xT, self.cf
        xs = [self.T("xs%d" % i, [128, D], F32) for i in range(3)]
        for t16 in range(16):
            sl = t16 % 3
            mk.dma("sp", lambda e, t16=t16, sl=sl: e.dma_start(out=xs[sl][:], in_=self.x[s, t16 * 128:(t16 + 1) * 128, :]),
                   slot=("xs", sl), writes=[("xs", sl)])
            tt = t16 // 4
            for half in range(2):
                b = self.bank()
                for j in range(4):
                    c = half * 4 + j
                    mk.op("pe", lambda e, b=b, j=j, c=c, sl=sl: e.transpose(out=ps[b][:, j * 128:(j + 1) * 128], in_=xs[sl][:, c * 128:(c + 1) * 128], identity=cf[:, C_ID:C_ID + 128]),
                          reads=[("xs", sl), "cf"], writes=[("ps", b)])
                mk.op("act", lambda e, b=b, half=half, t16=t16: e.activation(
                    out=xT[:, half * 4:(half + 1) * 4, t16 * 128:(t16 + 1) * 128],
                    in_=ps[b][:, :].rearrange("p (a b) -> p a b", a=4), func=AF.Copy),
                    writes=[("ps", b)] + [("x", half * 4 + j, tt) for j in range(4)])
            if t16 % 4 == 3:
                self.norm_tile(tt)

    def norm_tile(self, tt):
        self.norm_S(tt)
        self.norm_H(tt)

    def defer_norm(self, stages):
        goff = self.next_goff
        if goff is None:
            return
        self.pending_norm = [(k, tt, goff) for (k, tt) in stages]

    def flush_pending(self):
        pend = getattr(self, "pending_norm", [])
        self.pending_norm = []
        for k, tt, goff in pend:
            if k == "S":
                self.norm_S(tt, goff)
            else:
                self.norm_H(tt, goff)

    def norm_S(self, tt, goff=-1):
        if goff == -1:
            goff = self.next_goff
        if goff is None:
            return
        mk, ps, xT, cf, cb = self.mk, self.ps, self.xT, self.cf, self.cb
        sq, rs = self.nsq, self.nrs
        tsl = slice(tt * 512, (tt + 1) * 512)
        b = self.bank()
        for c in range(NCH):
            q = self.nq % 2
            self.nq += 1
            mk.op("act", lambda e, c=c, q=q, tsl=tsl: e.activation(out=sq[q][:], in_=xT[:, c, tsl], func=AF.Square),
                  reads=[("x", c, tt)], writes=[("nsq", q)])
            mk.op("pe", lambda e, b=b, c=c, q=q: e.matmul(ps[b][:, :], lhsT=cb[:, C_ONES:C_ONES + 128], rhs=sq[q][:], start=(c == 0), stop=(c == NCH - 1)),
                  reads=["cb"], rhs=[("nsq", q)], writes=[("ps", b)])
        r = tt % 2
        mk.op("act", lambda e, b=b, r=r: e.activation(out=rs[r][:], in_=ps[b][:, :], func=AF.Ln, scale=1.0 / D, bias=cf[:, C_EPS:C_EPS + 1]),
              reads=["cf"], writes=[("ps", b), ("nrs", r)])
        mk.op("act", lambda e, r=r: e.activation(out=rs[r][:], in_=rs[r][:], func=AF.Exp, scale=-0.5), writes=[("nrs", r)])

    def norm_H(self, tt, goff=-1):
        if goff == -1:
            goff = self.next_goff
        if goff is None:
            return
        mk, xT, hT, gains, rs = self.mk, self.xT, self.hT, self.gains, self.nrs
        tsl = slice(tt * 512, (tt + 1) * 512)
        r = tt % 2
        for c in range(NCH):
            mk.op("dve", lambda e, c=c, r=r, tsl=tsl: e.scalar_tensor_tensor(
                out=hT[:, c, tsl], in0=xT[:, c, tsl], scalar=gains[:, goff + c:goff + c + 1], in1=rs[r][:], op0=ALU.mult, op1=ALU.mult),
                reads=[("x", c, tt), ("nrs", r), "gains"], writes=[("hT", c, tt)])

    def hT_reads(self, c):
        return [("hT", c, t) for t in range(4)]

    def phase_A(self, s, l):
        j = l // 2
        mk, ps, xT, hT, cf, cb = self.mk, self.ps, self.xT, self.hT, self.cf, self.cb
        T = self.T
        pf = self.alloc_pf()
        cosT = T("cosT", [128, S], F32)
        sinT = T("sinT", [128, S], F32)
        tA = [T("tA%d" % i, [128, 512], F32) for i in range(2)]
        tB = [T("tB%d" % i, [128, 512], F32) for i in range(2)]
        ti = T("ti", [128, 512], I32)
        qd = T("qd", [128, S], BF16)
        kdA = T("kdA", [128, S], BF16)
        kdB = T("kdB", [128, S], BF16)
        vA = T("vA", [128, 16, 128], BF16)
        vB = T("vB", [128, 16, 128], BF16)
        accA = T("accA", [128, S], F32)
        accB = T("accB", [128, S], F32)
        mT = T("mT", [128, 4, S], BF16)
        PT = [T("PT%d" % i, [128, 512], BF16) for i in range(3)]
        wsl = [pf[:, 0:3072].rearrange("p (c n) -> p c n", c=NCH)]
        qb = [T("qb%d" % i, [128, 512], BF16) for i in range(2)]
        wo = [T("woA%d" % i, [128, 4, 128], BF16) for i in range(2)]

        wv3 = self.a_w_qkv[j].rearrange("(c p) n -> p c n", p=128)
        ws = wsl[0]
        wkey = ("wA", 0)

        def load_wA(p_, g_):
            for kind in range(3):
                col0 = kind * 1536 + g_ * 512 + p_ * 128
                mk.dma("pool", lambda e, kind=kind, col0=col0: e.dma_start(out=ws[:, :, kind * 128:(kind + 1) * 128], in_=wv3[:, :, col0:col0 + 128]),
                       slot=wkey, writes=[wkey])
        if not self.was_prefetched:
            load_wA(0, 2)
        self.flush_pending()
        def gen_table(tt):
            tsl = slice(tt * 512, (tt + 1) * 512)
            mk.dma("sp", lambda e, tsl=tsl: e.dma_start(out=ti[:], in_=self.pos[s, tsl].partition_broadcast(128)),
                   slot="ti", writes=["ti"])
            u, kf = accA[:, 0:512], accB[:, 0:512]
            mk.op("dve", lambda e: e.tensor_copy(out=u, in_=ti[:]), reads=["ti"], writes=[("accA", 0)])
            mk.op("dve", lambda e: e.tensor_scalar(out=u, in0=u, scalar1=cf[:, C_INV:C_INV + 1], scalar2=None, op0=ALU.mult),
                  reads=["cf"], writes=[("accA", 0)])
            for which in range(2):
                if which == 1:
                    mk.op("dve", lambda e: e.tensor_scalar(out=u, in0=u, scalar1=0.25, scalar2=None, op0=ALU.add), writes=[("accA", 0)])
                mk.op("dve", lambda e: e.tensor_copy(out=ti[:], in_=u), reads=[("accA", 0)], writes=["ti"])
                mk.op("dve", lambda e: e.tensor_copy(out=kf, in_=ti[:]), reads=["ti"], writes=[("accB", 0)])
                mk.op("dve", lambda e: e.tensor_tensor(out=kf, in0=u, in1=kf, op=ALU.subtract), reads=[("accA", 0)], writes=[("accB", 0)])
                if which == 0:
                    mk.op("act", lambda e, tsl=tsl: e.activation(out=sinT[:, tsl], in_=kf, func=AF.Sin, scale=cf[:, C_SGN:C_SGN + 1]),
                          reads=[("accB", 0), "cf"], writes=[("sinT", tt)])
                else:
                    mk.op("act", lambda e, tsl=tsl: e.activation(out=cosT[:, tsl], in_=kf, func=AF.Sin, scale=cf[:, C_2PI:C_2PI + 1]),
                          reads=[("accB", 0), "cf"], writes=[("cosT", tt)])
        gen_table(0)
        mk.op("pool", lambda e: e.memset(kdA[64:128, :], 0.0), writes=[("kd", t_) for t_ in range(4)])
        mk.op("pool", lambda e: e.memset(kdB[0:64, :], 0.0), writes=[("kd", t_) for t_ in range(4)])
        mk.op("pool", lambda e: e.memset(vA[:, :, 64:128], 1.0), writes=["v"])
        mk.op("pool", lambda e: e.memset(vB[:, :, 0:64], 1.0), writes=["v"])


        def normalise_tile(p, tt):
            if True:
                tsl = slice(tt * 512, (tt + 1) * 512)
                r = tt % 2
                mk.op("act", lambda e, r=r, tsl=tsl: e.activation(out=tA[r][0:64, :], in_=accA[64:128, tsl], func=AF.Ln), reads=[("accA", tt)], writes=[("tA", r)])
                mk.op("act", lambda e, r=r: e.activation(out=tA[r][0:64, :], in_=tA[r][0:64, :], func=AF.Exp, scale=-1.0), writes=[("tA", r)])
                mk.op("dve", lambda e, r=r, tsl=tsl, p=p: e.tensor_tensor(out=mT[0:64, p, tsl], in0=accA[0:64, tsl], in1=tA[r][0:64, :], op=ALU.mult),
                      reads=[("accA", tt), ("tA", r)], writes=[("mT", p, tt)])
                mk.op("act", lambda e, r=r, tsl=tsl: e.activation(out=tB[r][64:128, :], in_=accB[0:64, tsl], func=AF.Ln), reads=[("accB", tt)], writes=[("tB", r)])
                mk.op("act", lambda e, r=r: e.activation(out=tB[r][64:128, :], in_=tB[r][64:128, :], func=AF.Exp, scale=-1.0), writes=[("tB", r)])
                mk.op("dve", lambda e, r=r, tsl=tsl, p=p: e.tensor_tensor(out=mT[64:128, p, tsl], in0=accB[64:128, tsl], in1=tB[r][64:128, :], op=ALU.mult),
                      reads=[("accB", tt), ("tB", r)], writes=[("mT", p, tt)])
        for p in range(4):
            for gi in range(3):
                g = 2 - gi
                dil = (1, 4, 16)[g]
                pending = []

                def post_tile(kind, tt, r):
                    tsl = slice(tt * 512, (tt + 1) * 512)
                    A, B = tA[r], tB[r]
                    b2 = self.bank()
                    mk.op("pe", lambda e, b2=b2, r=r: e.matmul(ps[b2][:, :], lhsT=cb[:, C_PERM:C_PERM + 128], rhs=qb[r][:], start=True, stop=True),
                          reads=["cb"], rhs=[("qb", r)], writes=[("ps", b2)])
                    mk.op("pool", lambda e, A=A, r=r, tsl=tsl: e.tensor_tensor(out=A[:], in0=qb[r][:], in1=cosT[:, tsl], op=ALU.mult),
                          reads=[("cosT", tt), ("qb", r)], writes=[("tA", r)])
                    mk.op("dve", lambda e, b2=b2, B=B, tsl=tsl: e.tensor_tensor(out=B[:], in0=ps[b2][:, :], in1=sinT[:, tsl], op=ALU.mult),
                          reads=[("sinT", tt)], writes=[("ps", b2), ("tB", r)])

                    def views(dst, Asrc, Bsrc, prt):
                        if dil == 1:
                            return dst[prt, tsl], Asrc[prt, :], Bsrc[prt, :]
                        n_r = dil
                        m = 512 // dil
                        o = dst[prt, :].rearrange("p (r m) -> p r m", r=n_r)[:, :, tt * m:(tt + 1) * m]
                        a = Asrc[prt, :].rearrange("p (m r) -> p r m", r=n_r)
                        bb = Bsrc[prt, :].rearrange("p (m r) -> p r m", r=n_r)
                        return o, a, bb
                    if kind == 0:
                        o, a, bb = views(qd, A, B, slice(0, 128))
                        mk.op("dve", lambda e, o=o, a=a, bb=bb: e.tensor_tensor(out=o, in0=a, in1=bb, op=ALU.add),
                              reads=[("tA", r), ("tB", r)], writes=[("qd", tt)])
                        if p == 0 and gi == 0 and tt + 1 < 4:
                            gen_table(tt + 1)
                    else:
                        for (dst, prt) in ((kdA, slice(0, 64)), (kdB, slice(64, 128))):
                            o, a, bb = views(dst, A, B, prt)
                            mk.op("dve", lambda e, o=o, a=a, bb=bb: e.tensor_tensor(out=o, in0=a, in1=bb, op=ALU.add),
                                  reads=[("tA", r), ("tB", r)], writes=[("kd", tt)])

                for kind in range(2):
                    for tt in range(4):
                        tsl = slice(tt * 512, (tt + 1) * 512)
                        b = self.bank()
                        for c in range(NCH):
                            mk.op("pe", lambda e, b=b, c=c, ws=ws, kind=kind, tsl=tsl: e.matmul(
                                ps[b][:, :], lhsT=ws[:, c, kind * 128:(kind + 1) * 128], rhs=hT[:, c, tsl], start=(c == 0), stop=(c == NCH - 1)),
                                reads=[wkey, ("hT", c, tt)], writes=[("ps", b)])
                        r = (kind * 4 + tt) % 2
                        mk.op("act", lambda e, b=b, r=r: e.activation(out=qb[r][:], in_=ps[b][:, :], func=AF.Copy),
                              writes=[("ps", b), ("qb", r)])
                        if pending:
                            post_tile(*pending.pop())
                        pending.append((kind, tt, r))
                def tok_slice(blk):
                    if dil == 1:
                        return slice(blk * 128, (blk + 1) * 128)
                    if dil == 4:
                        r_, n_ = blk // 4, blk % 4
                        st_ = r_ + 512 * n_
                        return slice(st_, st_ + 4 * 127 + 1, 4)
                    return slice(blk, S, 16)
                for b4 in range(4):
                    bk = self.bank()
                    for bi in range(4):
                        blk = b4 * 4 + bi
                        tk = tok_slice(blk)
                        for c in range(NCH):
                            mk.op("pe", lambda e, bk=bk, bi=bi, c=c, tk=tk, ws=ws: e.matmul(
                                ps[bk][:, bi * 128:(bi + 1) * 128], lhsT=hT[:, c, tk], rhs=ws[:, c, 256:384], start=(c == 0), stop=(c == NCH - 1)),
                                reads=[wkey] + self.hT_reads(c), writes=[("ps", bk)])
                    if pending:
                        post_tile(*pending.pop())
                    pv = ps[bk][:, :].rearrange("p (a b) -> p a b", a=4)
                    mk.op("act", lambda e, pv=pv, b4=b4: e.activation(out=vA[:, b4 * 4:(b4 + 1) * 4, 0:64], in_=pv[:, :, 0:64], func=AF.Copy),
                          writes=[("ps", bk), "v"])
                    mk.op("act", lambda e, pv=pv, b4=b4: e.activation(out=vB[:, b4 * 4:(b4 + 1) * 4, 64:128], in_=pv[:, :, 64:128], func=AF.Copy),
                          writes=[("ps", bk), "v"])
                nxt = p * 3 + gi + 1
                if nxt < 12:
                    load_wA(nxt // 3, 2 - (nxt % 3))
                else:
                    self.prefetch_next([("wA", 0)])
                LA = 2
                ctx = {}
                obanks = {}

                def tiles_of(blk):
                    if dil == 1:
                        return [blk // 4]
                    if dil == 4:
                        return [blk % 4]
                    return [0, 1, 2, 3]

                def qk_stage(blk):
                    n_ = blk if dil == 1 else (blk % 4 if dil == 4 else 0)
                    hp = n_ > 0
                    W = 512 if hp else 256
                    bs = self.bank()
                    bsl = slice(blk * 128, (blk + 1) * 128)
                    psl = slice((blk - 1) * 128, blk * 128)
                    mk.op("pe", lambda e, bs=bs, W=W: e.matmul(ps[bs][:, 0:W], lhsT=cb[:, C_ID:C_ID + 128], rhs=cb[:, C_MASK:C_MASK + W], start=True, stop=False),
                          reads=["cb"], writes=[("ps", bs)])
                    for h, kd in ((0, kdA), (1, kdB)):
                        mk.op("pe", lambda e, bs=bs, h=h, kd=kd, bsl=bsl: e.matmul(
                            ps[bs][:, h * 128:(h + 1) * 128], lhsT=kd[:, bsl], rhs=qd[:, bsl], start=False, stop=True, skip_group_check=True),
                            reads=[("kd", t_) for t_ in tiles_of(blk)], rhs=[("qd", t_) for t_ in tiles_of(blk)], writes=[("ps", bs)])
                        if hp:
                            mk.op("pe", lambda e, bs=bs, h=h, kd=kd, bsl=bsl, psl=psl: e.matmul(
                                ps[bs][:, 256 + h * 128:256 + (h + 1) * 128], lhsT=kd[:, psl], rhs=qd[:, bsl], start=False, stop=True, skip_group_check=True),
                                reads=[("kd", t_) for t_ in tiles_of(blk - 1)], rhs=[("qd", t_) for t_ in tiles_of(blk)], writes=[("ps", bs)])
                    pq = blk % 3
                    mk.op("act", lambda e, bs=bs, W=W, pq=pq: e.activation(out=PT[pq][:, 0:W], in_=ps[bs][:, 0:W], func=AF.Exp, scale=0.125),
                          writes=[("ps", bs), ("PT", pq)])
                    ctx[blk] = (hp, pq)

                def pv_stage(blk):
                    hp, pq = ctx[blk]
                    q4 = blk % 4
                    if q4 == 0:
                        obanks[0], obanks[1] = self.bank(), self.bank()
                    oa, ob = obanks[0], obanks[1]
                    osl = slice(q4 * 128, (q4 + 1) * 128)
                    for ob_, vv, c0 in ((oa, vA, 0), (ob, vB, 128)):
                        mk.op("pe", lambda e, ob_=ob_, vv=vv, c0=c0, osl=osl, blk=blk, pq=pq, hp=hp: e.matmul(
                            ps[ob_][:, osl], lhsT=vv[:, blk, :], rhs=PT[pq][:, c0:c0 + 128], start=True, stop=(not hp)),
                            reads=["v"], rhs=[("PT", pq)], writes=[("ps", ob_)])
                        if hp:
                            mk.op("pe", lambda e, ob_=ob_, vv=vv, c0=c0, osl=osl, blk=blk, pq=pq: e.matmul(
                                ps[ob_][:, osl], lhsT=vv[:, blk - 1, :], rhs=PT[pq][:, 256 + c0:256 + c0 + 128], start=False, stop=True),
                                reads=["v"], rhs=[("PT", pq)], writes=[("ps", ob_)])
                    if q4 == 3:
                        b0 = blk - 3
                        for ob_, acc, akey in ((oa, accA, "accA"), (ob, accB, "accB")):
                            if dil == 1:
                                o = acc[:, b0 * 128:b0 * 128 + 512]
                                src = ps[ob_][:, :]
                            elif dil == 4:
                                o = acc[:, slice(b0 // 4, S, 4)]
                                src = ps[ob_][:, :]
                            else:
                                o = acc[:, :].rearrange("p (m r) -> p r m", r=16)[:, b0:b0 + 4, :]
                                src = ps[ob_][:, :].rearrange("p (a b) -> p a b", a=4)
                            akeys = [(akey, b0 // 4)] if dil == 1 else [(akey, t) for t in range(4)]
                            if gi == 0:
                                mk.op("act", lambda e, o=o, src=src: e.activation(out=o, in_=src, func=AF.Copy),
                                      writes=[("ps", ob_)] + akeys)
                            else:
                                mk.op("dve", lambda e, o=o, src=src: e.tensor_tensor(out=o, in0=src, in1=o, op=ALU.add),
                                      writes=[("ps", ob_)] + akeys)
                        if gi == 2:
                            normalise_tile(p, b0 // 4)

                for step in range(16 + LA):
                    if step < 16:
                        qk_stage(step)
                    if step >= LA:
                        pv_stage(step - LA)
        wo3 = self.a_w_o[j].rearrange("(c p) n -> p c n", p=128)
        it = 0
        for half in range(2):
            for dc in range(NCH):
                w = wo[it % 2]
                wk = ("woA", it % 2)
                it += 1
                mk.dma("pool", lambda e, w=w, dc=dc: e.dma_start(out=w[:], in_=wo3[:, :, dc * 128:(dc + 1) * 128]), slot=wk, writes=[wk])
                for tt in (2 * half, 2 * half + 1):
                    tsl = slice(tt * 512, (tt + 1) * 512)
                    b = self.bank()
                    for kc in range(4):
                        mk.op("pe", lambda e, b=b, kc=kc, w=w, tsl=tsl: e.matmul(ps[b][:, :], lhsT=w[:, kc, :], rhs=mT[:, kc, tsl], start=(kc == 0), stop=(kc == 3)),
                              reads=[wk], rhs=[("mT", kc, tt)], writes=[("ps", b)])
                    mk.op("dve", lambda e, b=b, dc=dc, tsl=tsl: e.tensor_tensor(out=xT[:, dc, tsl], in0=ps[b][:, :], in1=xT[:, dc, tsl], op=ALU.add),
                          writes=[("ps", b), ("x", dc, tt)])
                if half == 1:
                    if dc == 0:
                        self.norm_S(0)
                    elif dc == 2:
                        self.norm_S(1)
                    elif dc == 4:
                        self.norm_H(0)
                    elif dc == 6:
                        self.norm_H(1)
        self.defer_norm([("S", 2), ("S", 3), ("H", 2), ("H", 3)])

    def phase_B(self, s, l):
        jb = l // 2
        mk, ps, xT, hT, cf, cb = self.mk, self.ps, self.xT, self.hT, self.cf, self.cb
        gnorm, wgu = self.gnorm, self.wgu
        T = self.T
        pf = self.alloc_pf()
        gd = T("gd", [32, S], F32)
        Lt = T("Lt", [128, 512], F32)
        EB = T("EB", [128, 512], F32)
        EBi = T("EBi", [128, 512], F32)
        Er = T("Er", [128, 512], F32)
        dec = T("dec", [128, 16], F32)
        qi = T("qi", [128, S], BF16)
        kinv = T("kinv", [128, S], BF16)
        ks = T("ks", [128, 16, 128], BF16)
        vv = T("vv", [128, 16, 256], BF16)
        oTh = T("oTh", [128, 2, S], BF16)
        S32 = T("S32", [128, 256], F32)
        Sbf = [T("Sbf%d" % i, [128, 256], BF16) for i in range(2)]
        aTm = [T("aTm%d" % i, [128, 128], BF16) for i in range(3)]
        sq, rs = self.nsq, self.nrs
        sr = [T("sr%d" % i, [128, 512], BF16) for i in range(2)]
        t1 = [T("t1%d" % i, [128, 512], F32) for i in range(2)]
        yT = T("yT", [128, 4, S], BF16)
        wq = [pf[:, 0:1024].rearrange("p (c n) -> p c n", c=NCH), T("wq1", [128, NCH, 128], BF16)[:]]
        wk_ = [pf[:, 1024:2048].rearrange("p (c n) -> p c n", c=NCH), T("wk1", [128, NCH, 128], BF16)[:]]
        wv = pf[:, 2048:4096].rearrange("p (c n) -> p c n", c=NCH)
        wr = [T("wr%d" % i, [128, NCH, 128], BF16) for i in range(2)]
        wg = T("wg", [128, NCH, 16], BF16)
        wo = [T("woB%d" % i, [128, 4, 128], BF16) for i in range(2)]

        win3 = self.b_w_in[jb].rearrange("(c p) n -> p c n", p=128)
        wo3 = self.b_w_o[jb].rearrange("(c p) n -> p c n", p=128)
        mk.dma("pool", lambda e: e.dma_start(out=wg[:], in_=win3[:, :, 3072:3088]), slot="wg", writes=["wg"])
        if not self.was_prefetched:
            mk.dma("pool", lambda e: e.dma_start(out=wq[0], in_=win3[:, :, 0:128]), slot=("wq", 0), writes=[("wq", 0)])
            mk.dma("pool", lambda e: e.dma_start(out=wk_[0], in_=win3[:, :, 512:640]), slot=("wk", 0), writes=[("wk", 0)])
            mk.dma("pool", lambda e: e.dma_start(out=wv, in_=win3[:, :, 1024:1280]), slot="wv", writes=["wv"])
        self.flush_pending()
        mk.op("pool", lambda e: e.memset(gd[:, :], 1.0), writes=["gd"])
        for tt in range(4):
            tsl = slice(tt * 512, (tt + 1) * 512)
            b = self.bank()
            for c in range(NCH):
                mk.op("pe", lambda e, b=b, c=c, tsl=tsl: e.matmul(ps[b][0:16, :], lhsT=wg[:, c, :], rhs=hT[:, c, tsl], start=(c == 0), stop=(c == NCH - 1)),
                      reads=["wg", ("hT", c, tt)], writes=[("ps", b)])
            mk.op("act", lambda e, b=b, tsl=tsl: e.activation(out=gd[0:16, tsl], in_=ps[b][0:16, :], func=AF.Copy), writes=[("ps", b), "gd"])

        for h in range(4):
            hs = h % 2
            if h > 0:
                mk.dma("pool", lambda e, h=h, hs=hs: e.dma_start(out=wq[hs], in_=win3[:, :, h * 128:(h + 1) * 128]), slot=("wq", hs), writes=[("wq", hs)])
                mk.dma("pool", lambda e, h=h, hs=hs: e.dma_start(out=wk_[hs], in_=win3[:, :, 512 + h * 128:512 + (h + 1) * 128]), slot=("wk", hs), writes=[("wk", hs)])
                mk.dma("pool", lambda e, h=h: e.dma_start(out=wv, in_=win3[:, :, 1024 + h * 256:1024 + (h + 1) * 256]), slot="wv", writes=["wv"])
            for tt in range(4):
                tsl = slice(tt * 512, (tt + 1) * 512)
                bl = self.bank()
                for j4 in range(4):
                    t16 = tt * 4 + j4
                    mk.op("pe", lambda e, bl=bl, j4=j4, t16=t16, h=h: e.matmul(
                        ps[bl][:, j4 * 128:(j4 + 1) * 128], lhsT=gd[0:17, t16 * 128:(t16 + 1) * 128],
                        rhs=wgu[0:17, jb * 512 + h * 128:jb * 512 + (h + 1) * 128], start=True, stop=True),
                        reads=["gd", "wgu"], writes=[("ps", bl)])
                mk.op("act", lambda e, bl=bl: e.activation(out=Lt[:], in_=ps[bl][:, :], func=AF.Exp, scale=-1.0), writes=[("ps", bl), "Lt"])
                mk.op("act", lambda e: e.activation(out=Lt[:], in_=Lt[:], func=AF.Ln, bias=cf[:, C_ONE:C_ONE + 1]), reads=["cf"], writes=["Lt"])
                bq = self.bank()
                for c in range(NCH):
                    mk.op("pe", lambda e, bq=bq, c=c, hs=hs, tsl=tsl: e.matmul(ps[bq][:, :], lhsT=wq[hs][:, c, :], rhs=hT[:, c, tsl], start=(c == 0), stop=(c == NCH - 1)),
                          reads=[("wq", hs), ("hT", c, tt)], writes=[("ps", bq)])
                bk = self.bank()
                for c in range(NCH):
                    mk.op("pe", lambda e, bk=bk, c=c, hs=hs, tsl=tsl: e.matmul(ps[bk][:, :], lhsT=wk_[hs][:, c, :], rhs=hT[:, c, tsl], start=(c == 0), stop=(c == NCH - 1)),
                          reads=[("wk", hs), ("hT", c, tt)], writes=[("ps", bk)])
                bc, br = self.bank(), self.bank()
                for j4 in range(4):
                    jsl = slice(j4 * 128, (j4 + 1) * 128)
                    mk.op("pe", lambda e, bc=bc, jsl=jsl: e.matmul(ps[bc][:, jsl], lhsT=Lt[:, jsl], rhs=cf[:, C_TRI:C_TRI + 128], start=True, stop=True),
                          reads=["Lt", "cf"], writes=[("ps", bc)])
                    mk.op("pe", lambda e, br=br, jsl=jsl: e.matmul(ps[br][:, jsl], lhsT=cf[:, C_TRS:C_TRS + 128], rhs=Lt[:, jsl], start=True, stop=True),
                          reads=["cf"], rhs=["Lt"], writes=[("ps", br)])
                mk.op("act", lambda e, bc=bc: e.activation(out=EB[:], in_=ps[bc][:, :], func=AF.Exp, scale=-1.0 / 16), writes=[("ps", bc), "EB"])
                mk.op("act", lambda e, bc=bc: e.activation(out=EBi[:], in_=ps[bc][:, :], func=AF.Exp, scale=1.0 / 16), writes=[("ps", bc), "EBi"])
                mk.op("act", lambda e, br=br: e.activation(out=Er[:], in_=ps[br][:, :], func=AF.Exp, scale=-1.0 / 16), writes=[("ps", br), "Er"])
                mk.op("dve", lambda e, tt=tt: e.tensor_copy(out=dec[:, tt * 4:(tt + 1) * 4], in_=EB[:, slice(127, 512, 128)]), reads=["EB"], writes=["dec"])
                mk.op("dve", lambda e, bq=bq, tsl=tsl: e.scalar_tensor_tensor(out=qi[:, tsl], in0=ps[bq][:, :], scalar=128.0 ** -0.5, in1=EB[:], op0=ALU.mult, op1=ALU.mult),
                      reads=["EB"], writes=[("ps", bq), "qi"])
                mk.op("dve", lambda e, bk=bk, tsl=tsl: e.tensor_tensor(out=kinv[:, tsl], in0=ps[bk][:, :], in1=EBi[:], op=ALU.mult),
                      reads=["EBi"], writes=[("ps", bk), "kinv"])
                bkt = self.bank()
                for j4 in range(4):
                    t16 = tt * 4 + j4
                    for c in range(NCH):
                        mk.op("pe", lambda e, bkt=bkt, j4=j4, t16=t16, c=c, hs=hs: e.matmul(
                            ps[bkt][:, j4 * 128:(j4 + 1) * 128], lhsT=hT[:, c, t16 * 128:(t16 + 1) * 128], rhs=wk_[hs][:, c, :], start=(c == 0), stop=(c == NCH - 1)),
                            reads=[("wk", hs), ("hT", c, tt)], writes=[("ps", bkt)])
                mk.op("dve", lambda e, bkt=bkt, tt=tt: e.tensor_tensor(
                    out=ks[:, tt * 4:(tt + 1) * 4, :], in0=ps[bkt][:, :].rearrange("p (a b) -> p a b", a=4), in1=Er[:, :].rearrange("p (a b) -> p a b", a=4), op=ALU.mult),
                    reads=["Er"], writes=[("ps", bkt), "ks"])
                for half in range(2):
                    bv = self.bank()
                    for jj in range(2):
                        t16 = tt * 4 + half * 2 + jj
                        for c in range(NCH):
                            mk.op("pe", lambda e, bv=bv, jj=jj, t16=t16, c=c: e.matmul(
                                ps[bv][:, jj * 256:(jj + 1) * 256], lhsT=hT[:, c, t16 * 128:(t16 + 1) * 128], rhs=wv[:, c, :], start=(c == 0), stop=(c == NCH - 1)),
                                reads=["wv", ("hT", c, tt)], writes=[("ps", bv)])
                    t0 = tt * 4 + half * 2
                    mk.op("act", lambda e, bv=bv, t0=t0: e.activation(out=vv[:, t0:t0 + 2, :], in_=ps[bv][:, :].rearrange("p (a b) -> p a b", a=2), func=AF.Copy),
                          writes=[("ps", bv), "vv"])
            if h == 3:
                self.prefetch_next([("wq", 0), ("wk", 0), "wv"])
            LA = 2
            b1s = {}

            def rec_stage1(c):
                csl = slice(c * 128, (c + 1) * 128)
                b1 = self.bank()
                b1s[c] = b1
                mk.op("pe", lambda e, b1=b1, csl=csl: e.matmul(ps[b1][:, 0:128], lhsT=kinv[:, csl], rhs=qi[:, csl], start=True, stop=True),
                      reads=["kinv", "qi"], writes=[("ps", b1)])
                if c < 15:
                    mk.op("pe", lambda e, b1=b1, c=c: e.matmul(ps[b1][:, 128:384], lhsT=ks[:, c, :], rhs=vv[:, c, :], start=True, stop=True),
                          reads=["ks", "vv"], writes=[("ps", b1)])
                aq = c % 3
                mk.op("dve", lambda e, b1=b1, aq=aq: e.tensor_tensor(out=aTm[aq][:], in0=ps[b1][:, 0:128], in1=cf[:, C_TRI:C_TRI + 128], op=ALU.mult),
                      reads=["cf"], writes=[("ps", b1), ("aTm", aq)])

            def rec_stage2(c):
                csl = slice(c * 128, (c + 1) * 128)
                b1 = b1s[c]
                aq = c % 3
                bo = self.bank()
                sqn = c % 2
                for vc in range(2):
                    vsl = slice(vc * 128, (vc + 1) * 128)
                    if c > 0:
                        mk.op("pe", lambda e, bo=bo, vsl=vsl, sqn=sqn, csl=csl: e.matmul(ps[bo][:, vsl], lhsT=Sbf[sqn][:, vsl], rhs=qi[:, csl], start=True, stop=False),
                              reads=[("Sbf", sqn), "qi"], writes=[("ps", bo)])
                    mk.op("pe", lambda e, bo=bo, vsl=vsl, aq=aq, c=c: e.matmul(ps[bo][:, vsl], lhsT=vv[:, c, vsl], rhs=aTm[aq][:], start=(c == 0), stop=True),
                          reads=["vv"], rhs=[("aTm", aq)], writes=[("ps", bo)])
                if c < 15:
                    if c == 0:
                        mk.op("dve", lambda e, b1=b1: e.tensor_copy(out=S32[:], in_=ps[b1][:, 128:384]), writes=[("ps", b1), "S32"])
                    else:
                        mk.op("dve", lambda e, b1=b1, c=c: e.scalar_tensor_tensor(out=S32[:], in0=S32[:], scalar=dec[:, c:c + 1], in1=ps[b1][:, 128:384], op0=ALU.mult, op1=ALU.add),
                              reads=["dec"], writes=[("ps", b1), "S32"])
                    nq = (c + 1) % 2
                    mk.op("act", lambda e, nq=nq: e.activation(out=Sbf[nq][:], in_=S32[:], func=AF.Copy), reads=["S32"], writes=[("Sbf", nq)])
                mk.op("act", lambda e, bo=bo, csl=csl: e.activation(out=oTh[:, :, csl], in_=ps[bo][:, 0:256].rearrange("p (a b) -> p a b", a=2), func=AF.Copy),
                      writes=[("ps", bo), "oTh"])

            for step in range(16 + LA):
                if step < 16:
                    rec_stage1(step)
                if step >= LA:
                    rec_stage2(step - LA)
            hh = h % 2
            for vc in range(2):
                mk.dma("pool", lambda e, h=h, vc=vc: e.dma_start(out=wr[vc][:], in_=win3[:, :, 2048 + h * 256 + vc * 128:2048 + h * 256 + (vc + 1) * 128]),
                       slot=("wr", vc), writes=[("wr", vc)])
            for tt in range(4):
                tsl = slice(tt * 512, (tt + 1) * 512)
                bn = self.bank()
                brrs = [self.bank(), self.bank()]
                for vc in range(2):
                    mk.op("dve", lambda e, vc=vc, tsl=tsl: e.tensor_tensor(out=sq[vc][:], in0=oTh[:, vc, tsl], in1=oTh[:, vc, tsl], op=ALU.mult),
                          reads=["oTh"], writes=[("nsq", vc)])
                for vc in range(2):
                    brr = brrs[vc]
                    for c in range(NCH):
                        mk.op("pe", lambda e, brr=brr, c=c, vc=vc, tsl=tsl: e.matmul(ps[brr][:, :], lhsT=wr[vc][:, c, :], rhs=hT[:, c, tsl], start=(c == 0), stop=(c == NCH - 1)),
                              reads=[("wr", vc), ("hT", c, tt)], writes=[("ps", brr)])
                for vc in range(2):
                    mk.op("pe", lambda e, bn=bn, vc=vc: e.matmul(ps[bn][:, :], lhsT=cb[:, C_ONES:C_ONES + 128], rhs=sq[vc][:], start=(vc == 0), stop=(vc == 1)),
                          reads=["cb"], rhs=[("nsq", vc)], writes=[("ps", bn)])
                r = tt % 2
                mk.op("act", lambda e, bn=bn, r=r: e.activation(out=rs[r][:], in_=ps[bn][:, :], func=AF.Ln, scale=1.0 / 256, bias=cf[:, C_EPS:C_EPS + 1]),
                      reads=["cf"], writes=[("ps", bn), ("nrs", r)])
                mk.op("act", lambda e, r=r: e.activation(out=rs[r][:], in_=rs[r][:], func=AF.Exp, scale=-0.5), writes=[("nrs", r)])
                for vc in range(2):
                    brr = brrs[vc]
                    mk.op("act", lambda e, brr=brr, vc=vc: e.activation(out=sr[vc][:], in_=ps[brr][:, :], func=AF.Silu), writes=[("ps", brr), ("sr", vc)])
                    mk.op("dve", lambda e, vc=vc, r=r, tsl=tsl: e.scalar_tensor_tensor(
                        out=t1[vc][:], in0=oTh[:, vc, tsl], scalar=gnorm[:, jb * 2 + vc:jb * 2 + vc + 1], in1=rs[r][:], op0=ALU.mult, op1=ALU.mult),
                        reads=["oTh", ("nrs", r), "gnorm"], writes=[("t1", vc)])
                    mk.op("dve", lambda e, vc=vc, tsl=tsl, hh=hh: e.tensor_tensor(out=yT[:, hh * 2 + vc, tsl], in0=t1[vc][:], in1=sr[vc][:], op=ALU.mult),
                          reads=[("t1", vc), ("sr", vc)], writes=[("yT", hh * 2 + vc, tt)])
            if hh == 1:
                pair = h // 2
                if pair == 0:
                    for dc in range(NCH):
                        w = wo[dc % 2]
                        wkk = ("woB", dc % 2)
                        mk.dma("pool", lambda e, w=w, dc=dc, pair=pair: e.dma_start(out=w[:], in_=wo3[:, pair * 4:(pair + 1) * 4, dc * 128:(dc + 1) * 128]), slot=wkk, writes=[wkk])
                        for tt in range(4):
                            tsl = slice(tt * 512, (tt + 1) * 512)
                            b = self.bank()
                            for kc in range(4):
                                mk.op("pe", lambda e, b=b, kc=kc, w=w, tsl=tsl: e.matmul(ps[b][:, :], lhsT=w[:, kc, :], rhs=yT[:, kc, tsl], start=(kc == 0), stop=(kc == 3)),
                                      reads=[wkk], rhs=[("yT", kc, tt)], writes=[("ps", b)])
                            mk.op("dve", lambda e, b=b, dc=dc, tsl=tsl: e.tensor_tensor(out=xT[:, dc, tsl], in0=ps[b][:, :], in1=xT[:, dc, tsl], op=ALU.add),
                                  writes=[("ps", b), ("x", dc, tt)])
                else:
                    it = 0
                    for half in range(2):
                        for dc in range(NCH):
                            w = wo[it % 2]
                            wkk = ("woB", it % 2)
                            it += 1
                            mk.dma("pool", lambda e, w=w, dc=dc, pair=pair: e.dma_start(out=w[:], in_=wo3[:, pair * 4:(pair + 1) * 4, dc * 128:(dc + 1) * 128]), slot=wkk, writes=[wkk])
                            for tt in (2 * half, 2 * half + 1):
                                tsl = slice(tt * 512, (tt + 1) * 512)
                                b = self.bank()
                                for kc in range(4):
                                    mk.op("pe", lambda e, b=b, kc=kc, w=w, tsl=tsl: e.matmul(ps[b][:, :], lhsT=w[:, kc, :], rhs=yT[:, kc, tsl], start=(kc == 0), stop=(kc == 3)),
                                          reads=[wkk], rhs=[("yT", kc, tt)], writes=[("ps", b)])
                                mk.op("dve", lambda e, b=b, dc=dc, tsl=tsl: e.tensor_tensor(out=xT[:, dc, tsl], in0=ps[b][:, :], in1=xT[:, dc, tsl], op=ALU.add),
                                      writes=[("ps", b), ("x", dc, tt)])
                            if half == 1:
                                if dc == 0:
                                    self.norm_S(0)
                                elif dc == 2:
                                    self.norm_S(1)
                                elif dc == 4:
                                    self.norm_H(0)
                                elif dc == 6:
                                    self.norm_H(1)
                    self.defer_norm([("S", 2), ("S", 3), ("H", 2), ("H", 3)])

    def phase_F(self, s, l):
        mk, ps, xT, hT, cf, cb, convp = self.mk, self.ps, self.xT, self.hT, self.cf, self.cb, self.convp
        T = self.T
        pf = self.alloc_pf()
        yT = T("yF", [128, FGRP, S], BF16)
        asb = [T("asb%d" % i, [128, 514], F32) for i in range(2)]
        ct = [T("ct%d" % i, [128, 512], F32) for i in range(2)]
        sb = [T("sb%d" % i, [128, 512], BF16) for i in range(2)]
        win = [pf[:, 0:2048].rearrange("p (c n) -> p c n", c=NCH), pf[:, 2048:4096].rearrange("p (c n) -> p c n", c=NCH),
               T("win2", [128, NCH, 256], BF16)[:]]
        wdr = T("wdr", [128, NCH, FGRP, 128], BF16)
        wd = [wdr[:, i, :, :] for i in range(2)]

        win3 = self.f_w_in[l].rearrange("(c p) n -> p c n", p=128)
        wd3 = self.f_w_down[l].rearrange("(c p) n -> p c n", p=128)

        def cp(k, fc):
            o = (l * 4 + k) * NFC + fc
            return convp[:, o:o + 1]
        it = 0
        nt = 0

        def load_win(fc):
            w = win[fc % 3]
            wk = ("win", fc % 3)
            mk.dma("pool", lambda e, w=w, fc=fc: e.dma_start(out=w[:, :, 0:128], in_=win3[:, :, fc * 128:(fc + 1) * 128]), slot=wk, writes=[wk])
            mk.dma("pool", lambda e, w=w, fc=fc: e.dma_start(out=w[:, :, 128:256], in_=win3[:, :, FFN + fc * 128:FFN + (fc + 1) * 128]), slot=wk, writes=[wk])
        if not self.was_prefetched:
            load_win(0)
        load_win(1)
        self.flush_pending()
        for grp in range(2):
            for fi in range(FGRP):
                if grp == 0 and fi == 8:
                    for dc in range(2):
                        mk.dma("pool", lambda e, dc=dc: e.dma_start(out=wd[dc], in_=wd3[:, 0:FGRP, dc * 128:(dc + 1) * 128]), slot=("wd", dc), writes=[("wd", dc)])
                if grp == 1 and 1 <= fi <= 8:
                    dcl = fi - 1
                    mk.dma("pool", lambda e, dc=dcl: e.dma_start(out=wdr[:, dc, :, :], in_=wd3[:, FGRP:2 * FGRP, dc * 128:(dc + 1) * 128]), slot=("wdr", dcl), writes=[("wdr", dcl)] + ([("wd", dcl)] if dcl < 2 else []))
                fc = grp * FGRP + fi
                w = win[it % 3]
                wk = ("win", it % 3)
                it += 1
                if fc + 2 < NFC:
                    load_win(fc + 2)
                for tt in range(4):
                    tsl = slice(tt * 512, (tt + 1) * 512)
                    q = nt % 2
                    nt += 1
                    a_cur, a_prev = asb[q], asb[1 - q]
                    ba, bu = self.bank(), self.bank()
                    for c in range(NCH):
                        mk.op("pe", lambda e, ba=ba, c=c, w=w, tsl=tsl: e.matmul(ps[ba][:, :], lhsT=w[:, c, 0:128], rhs=hT[:, c, tsl], start=(c == 0), stop=(c == NCH - 1)),
                              reads=[wk, ("hT", c, tt)], writes=[("ps", ba)] + ([("ps", bu)] if c == 0 else []))
                    for c in range(NCH):
                        mk.op("pe", lambda e, bu=bu, c=c, w=w, tsl=tsl: e.matmul(ps[bu][:, :], lhsT=w[:, c, 128:256], rhs=hT[:, c, tsl], start=(c == 0), stop=(c == NCH - 1)),
                              reads=[wk, ("hT", c, tt)], writes=[("ps", bu)])
                    if tt == 0:
                        mk.op("pool", lambda e, a_cur=a_cur: e.memset(a_cur[:, 0:2], 0.0), writes=[("asb", q)])
                    else:
                        mk.op("pool", lambda e, a_cur=a_cur, a_prev=a_prev: e.tensor_copy(out=a_cur[:, 0:2], in_=a_prev[:, 512:514]),
                              reads=[("asb", 1 - q)], writes=[("asb", q)])
                    mk.op("act", lambda e, ba=ba, a_cur=a_cur: e.activation(out=a_cur[:, 2:514], in_=ps[ba][:, :], func=AF.Copy), writes=[("ps", ba), ("asb", q)])
                    mk.op("act", lambda e, ba=ba, q=q, fc=fc: e.activation(out=ct[q][:], in_=ps[ba][:, :], func=AF.Identity, scale=cp(2, fc), bias=cp(3, fc)),
                          reads=["convp"], writes=[("ps", ba), ("ct", q)])
                    mk.op("dve", lambda e, q=q, a_cur=a_cur, fc=fc: e.scalar_tensor_tensor(out=ct[q][:], in0=a_cur[:, 1:513], scalar=cp(1, fc), in1=ct[q][:], op0=ALU.mult, op1=ALU.add),
                          reads=[("asb", q), "convp"], writes=[("ct", q)])
                    mk.op("dve", lambda e, q=q, a_cur=a_cur, fc=fc: e.scalar_tensor_tensor(out=ct[q][:], in0=a_cur[:, 0:512], scalar=cp(0, fc), in1=ct[q][:], op0=ALU.mult, op1=ALU.add),
                          reads=[("asb", q), "convp"], writes=[("ct", q)])
                    mk.op("act", lambda e, q=q: e.activation(out=sb[q][:], in_=ct[q][:], func=AF.Silu), reads=[("ct", q)], writes=[("sb", q)])
                    mk.op("dve", lambda e, q=q, bu=bu, fi=fi, tsl=tsl: e.tensor_tensor(out=yT[:, fi, tsl], in0=ps[bu][:, :], in1=sb[q][:], op=ALU.mult),
                          reads=[("sb", q)], writes=[("ps", bu), ("yF", fi, tt)])
            if grp == 1:
                self.prefetch_next([("win", 0), ("win", 1)])
            if grp == 0:
                for dc in range(NCH):
                    w = wd[dc % 2]
                    wkk = ("wd", dc % 2)
                    if dc >= 2:
                        mk.dma("pool", lambda e, w=w, dc=dc, grp=grp: e.dma_start(out=w, in_=wd3[:, grp * FGRP:(grp + 1) * FGRP, dc * 128:(dc + 1) * 128]), slot=wkk, writes=[wkk])
                    for tt in range(4):
                        tsl = slice(tt * 512, (tt + 1) * 512)
                        b = self.bank()
                        for fi in range(FGRP):
                            mk.op("pe", lambda e, b=b, fi=fi, w=w, tsl=tsl: e.matmul(ps[b][:, :], lhsT=w[:, fi, :], rhs=yT[:, fi, tsl], start=(fi == 0), stop=(fi == FGRP - 1)),
                                  reads=[wkk], rhs=[("yF", fi, tt)], writes=[("ps", b)])
                        mk.op("dve", lambda e, b=b, dc=dc, tsl=tsl: e.tensor_tensor(out=xT[:, dc, tsl], in0=ps[b][:, :], in1=xT[:, dc, tsl], op=ALU.add),
                              writes=[("ps", b), ("x", dc, tt)])
            else:
                for tt in range(4):
                    tsl = slice(tt * 512, (tt + 1) * 512)
                    for dc in range(NCH):
                        b = self.bank()
                        for fi in range(FGRP):
                            mk.op("pe", lambda e, b=b, fi=fi, dc=dc, tsl=tsl: e.matmul(ps[b][:, :], lhsT=wdr[:, dc, fi, :], rhs=yT[:, fi, tsl], start=(fi == 0), stop=(fi == FGRP - 1)),
                                  reads=[("wdr", dc)], rhs=[("yF", fi, tt)], writes=[("ps", b)])
                        mk.op("dve", lambda e, b=b, dc=dc, tsl=tsl: e.tensor_tensor(out=xT[:, dc, tsl], in0=ps[b][:, :], in1=xT[:, dc, tsl], op=ALU.add),
                              writes=[("ps", b), ("x", dc, tt)])
                        if tt >= 1 and dc == 1:
                            self.norm_S(tt - 1)
                        if tt >= 1 and dc == 5:
                            self.norm_H(tt - 1)
                self.defer_norm([("S", 3), ("H", 3)])

    def phase_store(self, s):
        mk, ps, xT, cf, cb, gains = self.mk, self.ps, self.xT, self.cf, self.cb, self.gains
        T = self.T
        self.flush_pending()
        sq, rs = self.nsq, self.nrs
        of = [T("of%d" % i, [128, 512], F32) for i in range(2)]
        ost = [T("ost%d" % i, [128, 4, D], F32) for i in range(2)]
        toks = []
        for tt in range(4):
            tsl = slice(tt * 512, (tt + 1) * 512)
            r = tt % 2
            if self.final_norm:
                b = self.bank()
                for c in range(NCH):
                    q = c % 2
                    mk.op("act", lambda e, c=c, q=q, tsl=tsl: e.activation(out=sq[q][:], in_=xT[:, c, tsl], func=AF.Square),
                          reads=[("x", c, tt)], writes=[("nsq", q)])
                    mk.op("pe", lambda e, b=b, c=c, q=q: e.matmul(ps[b][:, :], lhsT=cb[:, C_ONES:C_ONES + 128], rhs=sq[q][:], start=(c == 0), stop=(c == NCH - 1)),
                          reads=["cb"], rhs=[("nsq", q)], writes=[("ps", b)])
                mk.op("act", lambda e, b=b, r=r: e.activation(out=rs[r][:], in_=ps[b][:, :], func=AF.Ln, scale=1.0 / D, bias=cf[:, C_EPS:C_EPS + 1]),
                      reads=["cf"], writes=[("ps", b), ("nrs", r)])
                mk.op("act", lambda e, r=r: e.activation(out=rs[r][:], in_=rs[r][:], func=AF.Exp, scale=-0.5), writes=[("nrs", r)])
            o = ost[r]
            for c in range(NCH):
                q = c % 2
                if self.final_norm:
                    mk.op("dve", lambda e, c=c, q=q, r=r, tsl=tsl: e.scalar_tensor_tensor(
                        out=of[q][:], in0=xT[:, c, tsl], scalar=gains[:, 64 + c:65 + c], in1=rs[r][:], op0=ALU.mult, op1=ALU.mult),
                        reads=[("x", c, tt), ("nrs", r), "gains"], writes=[("of", q)])
                    src = of[q]
                    srck = [("of", q)]
                b2 = self.bank()
                for j in range(4):
                    if self.final_norm:
                        inp = src[:, j * 128:(j + 1) * 128]
                    else:
                        inp = xT[:, c, tt * 512 + j * 128:tt * 512 + (j + 1) * 128]
                        srck = [("x", c, tt)]
                    mk.op("pe", lambda e, b2=b2, j=j, inp=inp: e.transpose(out=ps[b2][:, j * 128:(j + 1) * 128], in_=inp, identity=cf[:, C_ID:C_ID + 128]),
                          reads=srck + ["cf"], writes=[("ps", b2)])
                mk.op("act", lambda e, b2=b2, c=c, o=o: e.activation(out=o[:, :, c * 128:(c + 1) * 128], in_=ps[b2][:, :].rearrange("p (a b) -> p a b", a=4), func=AF.Copy),
                      writes=[("ps", b2), ("ost", r)])
            toks.append(mk.dma("sp", lambda e, o=o, tsl=tsl: e.dma_start(out=self.y[s, tsl, :].rearrange("(j p) n -> p j n", p=128), in_=o[:]),
                               slot=("ost", r), reads=[("ost", r)]))
        mk.wait("sp", toks)


def make_consts():
    c = np.zeros((128, NCONST), np.float32)
    idx = np.arange(128)
    c[:, C_ID:C_ID + 128] = np.eye(128, dtype=np.float32)
    c[:, C_TRI:C_TRI + 128] = (idx[:, None] <= idx[None, :]).astype(np.float32)
    c[:, C_TRS:C_TRS + 128] = (idx[:, None] > idx[None, :]).astype(np.float32)
    mc = np.where(idx[:, None] <= idx[None, :], 0.0, NEG).astype(np.float32)
    mp = np.where(idx[:, None] >= idx[None, :], 0.0, NEG).astype(np.float32)
    c[:, C_MASK:C_MASK + 512] = np.concatenate([mc, mc, mp, mp], axis=1)
    c[:, C_ONES:C_ONES + 128] = 1.0
    fi = (idx % 32).astype(np.float64)
    inv = 10000.0 ** (-(2.0 * fi) / 64.0)
    c[:, C_INV] = (inv / TWO_PI).astype(np.float32)
    sc = TWO_PI * (1.0 - 1e-6)
    c[:, C_SGN] = np.where((idx % 64) < 32, -sc, sc).astype(np.float32)
    c[:, C_EPS] = 1e-6
    c[:, C_2PI] = sc
    c[:, C_ONE] = 1.0
    c[:, C_PERM:C_PERM + 128] = (idx[:, None] == (idx[None, :] ^ 32)).astype(np.float32)
    return c


def full_plan(nseq):
    plan = []
    for s in range(nseq):
        plan.append(("load", s))
        for l in range(DEPTH):
            plan.append(("A" if l % 2 == 0 else "B", s, l))
            plan.append(("F", s, l))
        plan.append(("store", s))
    return plan


def layout_params(inp):
    f32 = np.float32
    gains = np.zeros((128, 72), f32)
    gains[:, 0:32] = np.asarray(inp["norm_mix"], f32).reshape(4, 8, 128).transpose(2, 0, 1).reshape(128, 32)
    gains[:, 32:64] = np.asarray(inp["norm_ffn"], f32).reshape(4, 8, 128).transpose(2, 0, 1).reshape(128, 32)
    gains[:, 64:72] = np.asarray(inp["norm_final"], f32).reshape(8, 128).T
    cw = np.asarray(inp["f_conv_w"], f32).reshape(4, 3, NFC, 128)
    cbias = np.asarray(inp["f_conv_b"], f32).reshape(4, 1, NFC, 128)
    convp = np.concatenate([cw, cbias], axis=1).transpose(3, 0, 1, 2).reshape(128, 4 * 4 * NFC)
    gnorm = np.asarray(inp["b_g_norm"], f32).reshape(2, 2, 128).transpose(2, 0, 1).reshape(128, 4)
    wgu = np.zeros((32, 2, 512), f32)
    wgu[0:16] = np.asarray(inp["b_w_gate_up"], f32).transpose(1, 0, 2)
    wgu[16] = np.asarray(inp["b_b_gate_up"], f32)
    return dict(gains=np.ascontiguousarray(gains), convp=np.ascontiguousarray(convp),
                gnorm=np.ascontiguousarray(gnorm), wgu=np.ascontiguousarray(wgu.reshape(32, 1024)))


def kernel(**inputs):
    n = 8
    nseq = 2
    bld = Builder(nseq, full_plan(nseq))
    nc = bld.build()
    shared = layout_params(inputs)
    shared["cst"] = make_consts()
    for k in ("a_w_qkv", "a_w_o", "b_w_in", "b_w_o", "f_w_in", "f_w_down"):
        shared[k] = np.ascontiguousarray(np.asarray(inputs[k], np.float32))
    x = np.asarray(inputs["x"], np.float32)
    pos = np.asarray(inputs["positions"], np.int32)
    in_maps = []
    for i in range(n):
        m = dict(shared)
        m["x"] = np.ascontiguousarray(x[i * nseq:(i + 1) * nseq])
        m["pos"] = np.ascontiguousarray(pos[i * nseq:(i + 1) * nseq])
        in_maps.append(m)
    res = run_bass_kernel_spmd(nc, in_maps, core_ids=list(range(n)))
    return np.concatenate([r["y"] for r in res.results], axis=0)
```

```python
from contextlib import ExitStack
import numpy as np
import concourse.bass as bass
import concourse.mybir as mybir
from concourse.bass_utils import run_bass_kernel_spmd

F32 = mybir.dt.float32
BF16 = mybir.dt.bfloat16
I32 = mybir.dt.int32
AF = mybir.ActivationFunctionType
ALU = mybir.AluOpType

S = 2048
D = 1024
NCH = 8
DEPTH = 4
FFN = 2816
NFC = 22
FGRP = 11
NEG = -30000.0
TWO_PI = 6.283185307179586

ENGS = ["pe", "act", "dve", "pool", "sp"]
RECENT = 4
EMBED = True
PUSHBACK = False

C_ID = 0
C_TRI = 128
C_TRS = 256
C_MASK = 384
C_ONES = 896
C_INV = 1024
C_SGN = 1025
C_EPS = 1026
C_2PI = 1027
C_ONE = 1028
C_PERM = 1032
NCONST = 1160


class MK:
    def __init__(self, nc, semstack):
        self.nc = nc
        self.semstack = semstack
        self.streams = {e: [] for e in ENGS}
        self.nops = {e: 0 for e in ENGS}
        self.know = {e: {} for e in ENGS}
        self.pushable = set()
        self.tokclock = {}
        self.lastw = {}
        self.readers = {}
        self.dma_issued = {}
        self.sems = {}
        self.n_waits = 0
        self.n_instr = 0

    def _sem(self, key):
        s = self.sems.get(key)
        if s is None:
            s = self.semstack.enter_context(self.nc.semaphore("s%d" % len(self.sems)))
            self.sems[key] = s
        return s

    def _deps(self, reads, writes, rhs=()):
        deps = set()
        nonbank = set()
        lw = self.lastw
        for k in reads:
            t = lw.get(k)
            if t is not None:
                deps.add(t)
                nonbank.add(t)
        for k in rhs:
            t = lw.get(k)
            if t is not None:
                deps.add(t)
        for k in writes:
            isbank = isinstance(k, tuple) and k[0] == "ps"
            t = lw.get(k)
            if t is not None:
                deps.add(t)
                if not isbank:
                    nonbank.add(t)
            r = self.readers.get(k)
            if r:
                deps.update(r)
                if not isbank:
                    nonbank.update(r)
        self._nonbank = nonbank
        return deps

    def _reduce(self, eng, deps, noself=False):
        need = {}
        nb = getattr(self, "_nonbank", ())
        nbsem = set()
        for tok in deps:
            (semkey, val) = tok
            if semkey[0] == "dma":
                val = self.dma_issued[semkey[1]]
            elif semkey[1] == eng:
                if eng == "pe" or noself:
                    continue
                if self.nops[eng] - (val - 1) > RECENT:
                    continue
            if tok in nb:
                nbsem.add(semkey)
            if need.get(semkey, 0) < val:
                need[semkey] = val
        self._nbsem = nbsem
        k = self.know[eng]
        tc = self.tokclock
        items = [(sk, v) for sk, v in need.items() if k.get(sk, 0) < v]
        if len(items) > 1:
            items.sort(key=lambda kv: -sum(tc.get(kv, {}).values()))
        out = []
        for semkey, val in items:
            if k.get(semkey, 0) >= val:
                continue
            out.append((semkey, val))
            k[semkey] = val
            clk = tc.get((semkey, val))
            if clk:
                for s2, v2 in clk.items():
                    if k.get(s2, 0) < v2:
                        k[s2] = v2
        self.n_waits += len(out)
        return out

    def _commit(self, tok, reads, writes):
        rd = self.readers
        for k in reads:
            l = rd.get(k)
            if l is None:
                rd[k] = [tok]
            else:
                l.append(tok)
        for k in writes:
            self.lastw[k] = tok
            rd[k] = []

    def op(self, eng, fn, reads=(), writes=(), noself=False, rhs=()):
        waits = self._reduce(eng, self._deps(reads, writes, rhs), noself)
        emb = None
        if EMBED and waits:
            cand = list(waits)
            if cand:
                emb = cand[-1]
                waits = [w for w in waits if w is not emb]
        n = self.nops[eng]
        semkey = ("eng", eng)
        tok = (semkey, n + 1)
        clk = dict(self.know[eng])
        clk[semkey] = n + 1
        self.tokclock[tok] = clk
        if PUSHBACK and eng == "pe" and waits:
            for w in waits:
                if w[0] not in self._nbsem:
                    self.pushable.add((eng, n + 1, w))
        self.streams[eng].append((waits, fn, (semkey, 1), emb, n + 1))
        self.nops[eng] = n + 1
        self._commit(tok, tuple(reads) + tuple(rhs), writes)
        self.n_instr += 1
        return tok

    def dma(self, eng, fn, slot, reads=(), writes=()):
        waits = self._reduce(eng, self._deps(reads, writes))
        semkey = ("dma", slot)
        tot = self.dma_issued.get(slot, 0) + 16
        self.dma_issued[slot] = tot
        tok = (semkey, tot)
        clk = dict(self.know[eng])
        clk[semkey] = tot
        self.tokclock[tok] = clk
        self.streams[eng].append((waits, fn, (semkey, 16), None, None))
        self._commit(tok, reads, writes)
        self.n_instr += 1
        return tok

    def wait(self, eng, toks):
        waits = self._reduce(eng, set(toks))
        if waits:
            self.streams[eng].append((waits, None, None, None, None))

    def barrier(self):
        toks = [(("eng", e), self.nops[e]) for e in ENGS if self.nops[e] > 0]
        toks += [(("dma", s), v) for s, v in self.dma_issued.items()]
        for e in ENGS:
            self.wait(e, [t for t in toks if t[0] != ("eng", e)])

    def _pushback(self, name, st):
        me = ("eng", name)
        tc = self.tokclock
        for i in range(1, len(st)):
            waits, fn, inc, emb, idx = st[i]
            if not waits:
                continue
            keep = []
            for w in waits:
                clk = tc.get(w)
                placed = False
                if clk is not None and (name, idx, w) in self.pushable:
                    lim = clk.get(me, 0)
                    for j in range(i - 1, max(i - 4, -1), -1):
                        wj, fnj, incj, embj, idxj = st[j]
                        if fnj is None or idxj is None:
                            continue
                        if idxj <= lim:
                            break
                        if embj is None:
                            st[j] = (wj, fnj, incj, w, idxj)
                            placed = True
                            break
                if not placed:
                    keep.append(w)
            st[i] = (keep, fn, inc, emb, idx)
            self.n_pushed = getattr(self, "n_pushed", 0) + len(waits) - len(keep)

    def emit_block(self):
        nc = self.nc
        streams = self.streams
        if PUSHBACK:
            for name in ENGS:
                self._pushback(name, streams[name])
        for e in ENGS:
            for waits, fn, inc, emb, _idx in streams[e]:
                for semkey, _ in waits:
                    self._sem(semkey)
                if emb is not None:
                    self._sem(emb[0])
                if inc is not None:
                    self._sem(inc[0])
        sems = self.sems
        with nc.Block() as block:
            def run(name, e):
                for waits, fn, inc, emb, _idx in streams[name]:
                    for semkey, val in waits:
                        e.wait_ge(sems[semkey], val)
                    if fn is not None:
                        ins = fn(e)
                        if emb is not None:
                            ins._wait_ge(sems[emb[0]], emb[1])
                        ins.then_inc(sems[inc[0]], inc[1])

            @block.tensor
            def _(e):
                run("pe", e)

            @block.scalar
            def _(e):
                run("act", e)

            @block.vector
            def _(e):
                run("dve", e)

            @block.gpsimd
            def _(e):
                run("pool", e)

            @block.sync
            def _(e):
                run("sp", e)
        self.streams = {e: [] for e in ENGS}


class Builder:
    def __init__(self, nseq, plan, final_norm=True):
        self.nseq = nseq
        self.plan = plan
        self.final_norm = final_norm
        nc = bass.Bass("TRN2", target_bir_lowering=False)
        self.nc = nc
        dt = nc.dram_tensor
        self.x = dt("x", [nseq, S, D], F32, kind="ExternalInput").ap()
        self.pos = dt("pos", [nseq, S], I32, kind="ExternalInput").ap()
        self.a_w_qkv = dt("a_w_qkv", [2, D, 4608], F32, kind="ExternalInput").ap()
        self.a_w_o = dt("a_w_o", [2, 512, D], F32, kind="ExternalInput").ap()
        self.b_w_in = dt("b_w_in", [2, D, 3088], F32, kind="ExternalInput").ap()
        self.b_w_o = dt("b_w_o", [2, D, D], F32, kind="ExternalInput").ap()
        self.f_w_in = dt("f_w_in", [4, D, 2 * FFN], F32, kind="ExternalInput").ap()
        self.f_w_down = dt("f_w_down", [4, FFN, D], F32, kind="ExternalInput").ap()
        self.cst = dt("cst", [128, NCONST], F32, kind="ExternalInput").ap()
        self.gains_d = dt("gains", [128, 72], F32, kind="ExternalInput").ap()
        self.convp_d = dt("convp", [128, 4 * 4 * NFC], F32, kind="ExternalInput").ap()
        self.gnorm_d = dt("gnorm", [128, 4], F32, kind="ExternalInput").ap()
        self.wgu_d = dt("wgu", [32, 2 * 512], F32, kind="ExternalInput").ap()
        self.y = dt("y", [nseq, S, D], F32, kind="ExternalOutput").ap()
        self.nb = 0

    def bank(self):
        b = self.nb % 8
        self.nb += 1
        return b

    def build(self):
        nc = self.nc
        with ExitStack() as outer:
            mk = MK(nc, outer)
            self.mk = mk
            T = lambda name, shape, dt_: outer.enter_context(nc.sbuf_tensor(name, shape, dt_))
            self.xT = T("xT", [128, NCH, S], F32)
            self.hT = T("hT", [128, NCH, S], BF16)
            self.cf = T("cf", [128, NCONST], F32)
            self.cb = T("cb", [128, NCONST], BF16)
            self.gains = T("gains_s", [128, 72], F32)
            self.convp = T("convp_s", [128, 4 * 4 * NFC], F32)
            self.gnorm = T("gnorm_s", [128, 4], F32)
            self.wgu = T("wgu_s", [32, 2 * 512], F32)
            self.ps = [outer.enter_context(nc.psum_tensor("ps%d" % i, [128, 512], F32)) for i in range(8)]
            self.nsq = [T("nsq%d" % i, [128, 512], BF16) for i in range(2)]
            self.nrs = [T("nrs%d" % i, [128, 512], F32) for i in range(2)]
            self.nq = 0
            mk.dma("sp", lambda e: e.dma_start(out=self.cf[:], in_=self.cst), slot="c0", writes=["cf"])
            mk.dma("pool", lambda e: e.dma_start(out=self.cb[:], in_=self.cst), slot="c1", writes=["cb"])
            mk.dma("sp", lambda e: e.dma_start(out=self.gains[:], in_=self.gains_d), slot="c2", writes=["gains"])
            mk.dma("sp", lambda e: e.dma_start(out=self.convp[:], in_=self.convp_d), slot="c3", writes=["convp"])
            mk.dma("sp", lambda e: e.dma_start(out=self.gnorm[:], in_=self.gnorm_d), slot="c4", writes=["gnorm"])
            mk.dma("sp", lambda e: e.dma_start(out=self.wgu[:], in_=self.wgu_d), slot="c5", writes=["wgu"])
            mk.emit_block()
            for pi, ph in enumerate(self.plan):
                kind = ph[0]
                nxt = self.plan[pi + 1] if pi + 1 < len(self.plan) else None
                self.next_phase = nxt
                self.was_prefetched = getattr(self, "prefetched", False)
                self.prefetched = False
                if nxt is None or nxt[0] in ("store", "load", "storeload"):
                    self.next_goff = None
                elif nxt[0] == "F":
                    self.next_goff = 32 + nxt[2] * 8
                else:
                    self.next_goff = nxt[2] * 8
                with ExitStack() as st:
                    self.st = st
                    mk.barrier()
                    if kind == "load":
                        self.phase_load(ph[1])
                    elif kind == "A":
                        self.phase_A(ph[1], ph[2])
                    elif kind == "B":
                        self.phase_B(ph[1], ph[2])
                    elif kind == "F":
                        self.phase_F(ph[1], ph[2])
                    elif kind == "store":
                        self.phase_store(ph[1])
                    elif kind == "storeload":
                        self.phase_storeload(ph[1], ph[2])
                    mk.emit_block()
        return nc

    def alloc_pf(self):
        self.pf = self.T("pf", [128, 4096], BF16)
        return self.pf

    def prefetch_next(self, cur_keys):
        nxt = self.next_phase
        self.prefetched = False
        if nxt is None or nxt[0] not in ("A", "B", "F"):
            return
        mk, pf = self.mk, self.pf
        l = nxt[2]
        if nxt[0] == "A":
            w3 = self.a_w_qkv[l // 2].rearrange("(c p) n -> p c n", p=128)
            dst = pf[:, 0:3072].rearrange("p (c n) -> p c n", c=NCH)
            for kind in range(3):
                col0 = kind * 1536 + 2 * 512 + 0 * 128
                mk.dma("pool", lambda e, kind=kind, col0=col0: e.dma_start(out=dst[:, :, kind * 128:(kind + 1) * 128], in_=w3[:, :, col0:col0 + 128]),
                       slot=("wA", 0), writes=[("wA", 0)] + cur_keys)
        elif nxt[0] == "F":
            w3 = self.f_w_in[l].rearrange("(c p) n -> p c n", p=128)
            dst = pf[:, 0:2048].rearrange("p (c n) -> p c n", c=NCH)
            mk.dma("pool", lambda e: e.dma_start(out=dst[:, :, 0:128], in_=w3[:, :, 0:128]), slot=("win", 0), writes=[("win", 0)] + cur_keys)
            mk.dma("pool", lambda e: e.dma_start(out=dst[:, :, 128:256], in_=w3[:, :, FFN:FFN + 128]), slot=("win", 0), writes=[("win", 0)] + cur_keys)
        else:
            w3 = self.b_w_in[l // 2].rearrange("(c p) n -> p c n", p=128)
            dq = pf[:, 0:1024].rearrange("p (c n) -> p c n", c=NCH)
            dk = pf[:, 1024:2048].rearrange("p (c n) -> p c n", c=NCH)
            dv = pf[:, 2048:4096].rearrange("p (c n) -> p c n", c=NCH)
            mk.dma("pool", lambda e: e.dma_start(out=dq, in_=w3[:, :, 0:128]), slot=("wq", 0), writes=[("wq", 0)] + cur_keys)
            mk.dma("pool", lambda e: e.dma_start(out=dk, in_=w3[:, :, 512:640]), slot=("wk", 0), writes=[("wk", 0)] + cur_keys)
            mk.dma("pool", lambda e: e.dma_start(out=dv, in_=w3[:, :, 1024:1280]), slot="wv", writes=["wv"] + cur_keys)
        self.prefetched = True

    def T(self, name, shape, dt_):
        self.tcount = getattr(self, "tcount", 0) + 1
        return self.st.enter_context(self.nc.sbuf_tensor("%s_%d" % (name, self.tcount), shape, dt_))

    def phase_load(self, s):
        xs = [self.T("xs%d" % i, [128, D], F32) for i in range(3)]
        for tt in range(4):
            self.load_tiles(s, tt, xs, "sp")

    def load_tiles(self, s, tt, xs, queue):
        mk, ps, xT, cf = self.mk, self.ps, self.xT, self.cf
        for t16 in range(tt * 4, tt * 4 + 4):
            sl = t16 % 3
            mk.dma(queue, lambda e, t16=t16, sl=sl: e.dma_start(out=xs[sl][:], in_=self.x[s, t16 * 128:(t16 + 1) * 128, :]),
                   slot=("xs", sl), writes=[("xs", sl)])
            for half in range(2):
                b = self.bank()
                for j in range(4):
                    c = half * 4 + j
                    mk.op("pe", lambda e, b=b, j=j, c=c, sl=sl: e.transpose(out=ps[b][:, j * 128:(j + 1) * 128], in_=xs[sl][:, c * 128:(c + 1) * 128], identity=cf[:, C_ID:C_ID + 128]),
                          reads=[("xs", sl), "cf"], writes=[("ps", b)])
                mk.op("act", lambda e, b=b, half=half, t16=t16: e.activation(
                    out=xT[:, half * 4:(half + 1) * 4, t16 * 128:(t16 + 1) * 128],
                    in_=ps[b][:, :].rearrange("p (a b) -> p a b", a=4), func=AF.Copy),
                    writes=[("ps", b)] + [("x", half * 4 + j, tt) for j in range(4)])
        self.norm_tile(tt)

    def norm_tile(self, tt):
        self.norm_S(tt)
        self.norm_H(tt)

    def defer_norm(self, stages):
        goff = self.next_goff
        if goff is None:
            return
        self.pending_norm = [(k, tt, goff) for (k, tt) in stages]

    def flush_pending(self):
        pend = getattr(self, "pending_norm", [])
        self.pending_norm = []
        for k, tt, goff in pend:
            if k == "S":
                self.norm_S(tt, goff)
            else:
                self.norm_H(tt, goff)

    def norm_S(self, tt, goff=-1):
        if goff == -1:
            goff = self.next_goff
        if goff is None:
            return
        mk, ps, xT, cf, cb = self.mk, self.ps, self.xT, self.cf, self.cb
        sq, rs = self.nsq, self.nrs
        tsl = slice(tt * 512, (tt + 1) * 512)
        b = self.bank()
        for c in range(NCH):
            q = self.nq % 2
            self.nq += 1
            mk.op("act", lambda e, c=c, q=q, tsl=tsl: e.activation(out=sq[q][:], in_=xT[:, c, tsl], func=AF.Square),
                  reads=[("x", c, tt)], writes=[("nsq", q)])
            mk.op("pe", lambda e, b=b, c=c, q=q: e.matmul(ps[b][:, :], lhsT=cb[:, C_ONES:C_ONES + 128], rhs=sq[q][:], start=(c == 0), stop=(c == NCH - 1)),
                  reads=["cb"], rhs=[("nsq", q)], writes=[("ps", b)])
        r = tt % 2
        mk.op("act", lambda e, b=b, r=r: e.activation(out=rs[r][:], in_=ps[b][:, :], func=AF.Ln, scale=1.0 / D, bias=cf[:, C_EPS:C_EPS + 1]),
              reads=["cf"], writes=[("ps", b), ("nrs", r)])
        mk.op("act", lambda e, r=r: e.activation(out=rs[r][:], in_=rs[r][:], func=AF.Exp, scale=-0.5), writes=[("nrs", r)])

    def norm_H(self, tt, goff=-1):
        if goff == -1:
            goff = self.next_goff
        if goff is None:
            return
        mk, xT, hT, gains, rs = self.mk, self.xT, self.hT, self.gains, self.nrs
        tsl = slice(tt * 512, (tt + 1) * 512)
        r = tt % 2
        for c in range(NCH):
            mk.op("dve", lambda e, c=c, r=r, tsl=tsl: e.scalar_tensor_tensor(
                out=hT[:, c, tsl], in0=xT[:, c, tsl], scalar=gains[:, goff + c:goff + c + 1], in1=rs[r][:], op0=ALU.mult, op1=ALU.mult),
                reads=[("x", c, tt), ("nrs", r), "gains"], writes=[("hT", c, tt)])

    def hT_reads(self, c):
        return [("hT", c, t) for t in range(4)]

    def phase_A(self, s, l):
        j = l // 2
        mk, ps, xT, hT, cf, cb = self.mk, self.ps, self.xT, self.hT, self.cf, self.cb
        T = self.T
        pf = self.alloc_pf()
        cosT = T("cosT", [128, S], F32)
        sinT = T("sinT", [128, S], F32)
        tA = [T("tA%d" % i, [128, 512], F32) for i in range(2)]
        tB = [T("tB%d" % i, [128, 512], F32) for i in range(2)]
        ti = T("ti", [128, 512], I32)
        qd = T("qd", [128, S], BF16)
        kdA = T("kdA", [128, S], BF16)
        kdB = T("kdB", [128, S], BF16)
        vA = T("vA", [128, 16, 128], BF16)
        vB = T("vB", [128, 16, 128], BF16)
        accA = T("accA", [128, S], F32)
        accB = T("accB", [128, S], F32)
        mT = T("mT", [128, 4, S], BF16)
        PT = [T("PT%d" % i, [128, 512], BF16) for i in range(3)]
        wsl = [pf[:, 0:3072].rearrange("p (c n) -> p c n", c=NCH)]
        qb = [T("qb%d" % i, [128, 512], BF16) for i in range(2)]
        wo = [T("woA%d" % i, [128, 4, 128], BF16) for i in range(2)]

        wv3 = self.a_w_qkv[j].rearrange("(c p) n -> p c n", p=128)
        ws = wsl[0]
        wkey = ("wA", 0)

        def load_wA(p_, g_):
            for kind in range(3):
                col0 = kind * 1536 + g_ * 512 + p_ * 128
                mk.dma("pool", lambda e, kind=kind, col0=col0: e.dma_start(out=ws[:, :, kind * 128:(kind + 1) * 128], in_=wv3[:, :, col0:col0 + 128]),
                       slot=wkey, writes=[wkey])
        if not self.was_prefetched:
            load_wA(0, 2)
        self.flush_pending()
        def gen_table(tt):
            tsl = slice(tt * 512, (tt + 1) * 512)
            mk.dma("sp", lambda e, tsl=tsl: e.dma_start(out=ti[:], in_=self.pos[s, tsl].partition_broadcast(128)),
                   slot="ti", writes=["ti"])
            u, kf = accA[:, 0:512], accB[:, 0:512]
            mk.op("dve", lambda e: e.tensor_copy(out=u, in_=ti[:]), reads=["ti"], writes=[("accA", 0)])
            mk.op("dve", lambda e: e.tensor_scalar(out=u, in0=u, scalar1=cf[:, C_INV:C_INV + 1], scalar2=None, op0=ALU.mult),
                  reads=["cf"], writes=[("accA", 0)])
            for which in range(2):
                if which == 1:
                    mk.op("dve", lambda e: e.tensor_scalar(out=u, in0=u, scalar1=0.25, scalar2=None, op0=ALU.add), writes=[("accA", 0)])
                mk.op("dve", lambda e: e.tensor_copy(out=ti[:], in_=u), reads=[("accA", 0)], writes=["ti"])
                mk.op("dve", lambda e: e.tensor_copy(out=kf, in_=ti[:]), reads=["ti"], writes=[("accB", 0)])
                mk.op("dve", lambda e: e.tensor_tensor(out=kf, in0=u, in1=kf, op=ALU.subtract), reads=[("accA", 0)], writes=[("accB", 0)])
                if which == 0:
                    mk.op("act", lambda e, tsl=tsl: e.activation(out=sinT[:, tsl], in_=kf, func=AF.Sin, scale=cf[:, C_SGN:C_SGN + 1]),
                          reads=[("accB", 0), "cf"], writes=[("sinT", tt)])
                else:
                    mk.op("act", lambda e, tsl=tsl: e.activation(out=cosT[:, tsl], in_=kf, func=AF.Sin, scale=cf[:, C_2PI:C_2PI + 1]),
                          reads=[("accB", 0), "cf"], writes=[("cosT", tt)])
        gen_table(0)
        mk.op("pool", lambda e: e.memset(kdA[64:128, :], 0.0), writes=[("kd", t_) for t_ in range(4)])
        mk.op("pool", lambda e: e.memset(kdB[0:64, :], 0.0), writes=[("kd", t_) for t_ in range(4)])
        mk.op("pool", lambda e: e.memset(vA[:, :, 64:128], 1.0), writes=["v"])
        mk.op("pool", lambda e: e.memset(vB[:, :, 0:64], 1.0), writes=["v"])


        def normalise_tile(p, tt):
            if True:
                tsl = slice(tt * 512, (tt + 1) * 512)
                r = tt % 2
                mk.op("act", lambda e, r=r, tsl=tsl: e.activation(out=tA[r][0:64, :], in_=accA[64:128, tsl], func=AF.Ln), reads=[("accA", tt)], writes=[("tA", r)])
                mk.op("act", lambda e, r=r: e.activation(out=tA[r][0:64, :], in_=tA[r][0:64, :], func=AF.Exp, scale=-1.0), writes=[("tA", r)])
                mk.op("dve", lambda e, r=r, tsl=tsl, p=p: e.tensor_tensor(out=mT[0:64, p, tsl], in0=accA[0:64, tsl], in1=tA[r][0:64, :], op=ALU.mult),
                      reads=[("accA", tt), ("tA", r)], writes=[("mT", p, tt)])
                mk.op("act", lambda e, r=r, tsl=tsl: e.activation(out=tB[r][64:128, :], in_=accB[0:64, tsl], func=AF.Ln), reads=[("accB", tt)], writes=[("tB", r)])
                mk.op("act", lambda e, r=r: e.activation(out=tB[r][64:128, :], in_=tB[r][64:128, :], func=AF.Exp, scale=-1.0), writes=[("tB", r)])
                mk.op("dve", lambda e, r=r, tsl=tsl, p=p: e.tensor_tensor(out=mT[64:128, p, tsl], in0=accB[64:128, tsl], in1=tB[r][64:128, :], op=ALU.mult),
                      reads=[("accB", tt), ("tB", r)], writes=[("mT", p, tt)])
        for p in range(4):
            for gi in range(3):
                g = 2 - gi
                dil = (1, 4, 16)[g]
                pending = []

                def post_tile(kind, tt, r):
                    tsl = slice(tt * 512, (tt + 1) * 512)
                    A, B = tA[r], tB[r]
                    b2 = self.bank()
                    mk.op("pe", lambda e, b2=b2, r=r: e.matmul(ps[b2][:, :], lhsT=cb[:, C_PERM:C_PERM + 128], rhs=qb[r][:], start=True, stop=True),
                          reads=["cb"], rhs=[("qb", r)], writes=[("ps", b2)])
                    mk.op("pool", lambda e, A=A, r=r, tsl=tsl: e.tensor_tensor(out=A[:], in0=qb[r][:], in1=cosT[:, tsl], op=ALU.mult),
                          reads=[("cosT", tt), ("qb", r)], writes=[("tA", r)])
                    mk.op("dve", lambda e, b2=b2, B=B, tsl=tsl: e.tensor_tensor(out=B[:], in0=ps[b2][:, :], in1=sinT[:, tsl], op=ALU.mult),
                          reads=[("sinT", tt)], writes=[("ps", b2), ("tB", r)])

                    def views(dst, Asrc, Bsrc, prt):
                        if dil == 1:
                            return dst[prt, tsl], Asrc[prt, :], Bsrc[prt, :]
                        n_r = dil
                        m = 512 // dil
                        o = dst[prt, :].rearrange("p (r m) -> p r m", r=n_r)[:, :, tt * m:(tt + 1) * m]
                        a = Asrc[prt, :].rearrange("p (m r) -> p r m", r=n_r)
                        bb = Bsrc[prt, :].rearrange("p (m r) -> p r m", r=n_r)
                        return o, a, bb
                    if kind == 0:
                        o, a, bb = views(qd, A, B, slice(0, 128))
                        mk.op("dve", lambda e, o=o, a=a, bb=bb: e.tensor_tensor(out=o, in0=a, in1=bb, op=ALU.add),
                              reads=[("tA", r), ("tB", r)], writes=[("qd", tt)])
                        if p == 0 and gi == 0 and tt + 1 < 4:
                            gen_table(tt + 1)
                    else:
                        for (dst, prt) in ((kdA, slice(0, 64)), (kdB, slice(64, 128))):
                            o, a, bb = views(dst, A, B, prt)
                            mk.op("dve", lambda e, o=o, a=a, bb=bb: e.tensor_tensor(out=o, in0=a, in1=bb, op=ALU.add),
                                  reads=[("tA", r), ("tB", r)], writes=[("kd", tt)])

                for kind in range(2):
                    for tt in range(4):
                        tsl = slice(tt * 512, (tt + 1) * 512)
                        b = self.bank()
                        for c in range(NCH):
                            mk.op("pe", lambda e, b=b, c=c, ws=ws, kind=kind, tsl=tsl: e.matmul(
                                ps[b][:, :], lhsT=ws[:, c, kind * 128:(kind + 1) * 128], rhs=hT[:, c, tsl], start=(c == 0), stop=(c == NCH - 1)),
                                reads=[wkey, ("hT", c, tt)], writes=[("ps", b)])
                        r = (kind * 4 + tt) % 2
                        mk.op("act", lambda e, b=b, r=r: e.activation(out=qb[r][:], in_=ps[b][:, :], func=AF.Copy),
                              writes=[("ps", b), ("qb", r)])
                        if pending:
                            post_tile(*pending.pop())
                        pending.append((kind, tt, r))
                def tok_slice(blk):
                    if dil == 1:
                        return slice(blk * 128, (blk + 1) * 128)
                    if dil == 4:
                        r_, n_ = blk // 4, blk % 4
                        st_ = r_ + 512 * n_
                        return slice(st_, st_ + 4 * 127 + 1, 4)
                    return slice(blk, S, 16)
                for b4 in range(4):
                    bk = self.bank()
                    for bi in range(4):
                        blk = b4 * 4 + bi
                        tk = tok_slice(blk)
                        for c in range(NCH):
                            mk.op("pe", lambda e, bk=bk, bi=bi, c=c, tk=tk, ws=ws: e.matmul(
                                ps[bk][:, bi * 128:(bi + 1) * 128], lhsT=hT[:, c, tk], rhs=ws[:, c, 256:384], start=(c == 0), stop=(c == NCH - 1)),
                                reads=[wkey] + self.hT_reads(c), writes=[("ps", bk)])
                    if pending:
                        post_tile(*pending.pop())
                    pv = ps[bk][:, :].rearrange("p (a b) -> p a b", a=4)
                    mk.op("act", lambda e, pv=pv, b4=b4: e.activation(out=vA[:, b4 * 4:(b4 + 1) * 4, 0:64], in_=pv[:, :, 0:64], func=AF.Copy),
                          writes=[("ps", bk), "v"])
                    mk.op("act", lambda e, pv=pv, b4=b4: e.activation(out=vB[:, b4 * 4:(b4 + 1) * 4, 64:128], in_=pv[:, :, 64:128], func=AF.Copy),
                          writes=[("ps", bk), "v"])
                nxt = p * 3 + gi + 1
                if nxt < 12:
                    load_wA(nxt // 3, 2 - (nxt % 3))
                else:
                    self.prefetch_next([("wA", 0)])
                LA = 2
                ctx = {}
                obanks = {}

                def tiles_of(blk):
                    if dil == 1:
                        return [blk // 4]
                    if dil == 4:
                        return [blk % 4]
                    return [0, 1, 2, 3]

                def qk_stage(blk):
                    n_ = blk if dil == 1 else (blk % 4 if dil == 4 else 0)
                    hp = n_ > 0
                    W = 512 if hp else 256
                    bs = self.bank()
                    bsl = slice(blk * 128, (blk + 1) * 128)
                    psl = slice((blk - 1) * 128, blk * 128)
                    mk.op("pe", lambda e, bs=bs, W=W: e.matmul(ps[bs][:, 0:W], lhsT=cb[:, C_ID:C_ID + 128], rhs=cb[:, C_MASK:C_MASK + W], start=True, stop=False),
                          reads=["cb"], writes=[("ps", bs)])
                    for h, kd in ((0, kdA), (1, kdB)):
                        mk.op("pe", lambda e, bs=bs, h=h, kd=kd, bsl=bsl: e.matmul(
                            ps[bs][:, h * 128:(h + 1) * 128], lhsT=kd[:, bsl], rhs=qd[:, bsl], start=False, stop=True, skip_group_check=True),
                            reads=[("kd", t_) for t_ in tiles_of(blk)], rhs=[("qd", t_) for t_ in tiles_of(blk)], writes=[("ps", bs)])
                        if hp:
                            mk.op("pe", lambda e, bs=bs, h=h, kd=kd, bsl=bsl, psl=psl: e.matmul(
                                ps[bs][:, 256 + h * 128:256 + (h + 1) * 128], lhsT=kd[:, psl], rhs=qd[:, bsl], start=False, stop=True, skip_group_check=True),
                                reads=[("kd", t_) for t_ in tiles_of(blk - 1)], rhs=[("qd", t_) for t_ in tiles_of(blk)], writes=[("ps", bs)])
                    pq = blk % 3
                    mk.op("act", lambda e, bs=bs, W=W, pq=pq: e.activation(out=PT[pq][:, 0:W], in_=ps[bs][:, 0:W], func=AF.Exp, scale=0.125),
                          writes=[("ps", bs), ("PT", pq)])
                    ctx[blk] = (hp, pq)

                def pv_stage(blk):
                    hp, pq = ctx[blk]
                    q4 = blk % 4
                    if q4 == 0:
                        obanks[0], obanks[1] = self.bank(), self.bank()
                    oa, ob = obanks[0], obanks[1]
                    osl = slice(q4 * 128, (q4 + 1) * 128)
                    for ob_, vv, c0 in ((oa, vA, 0), (ob, vB, 128)):
                        mk.op("pe", lambda e, ob_=ob_, vv=vv, c0=c0, osl=osl, blk=blk, pq=pq, hp=hp: e.matmul(
                            ps[ob_][:, osl], lhsT=vv[:, blk, :], rhs=PT[pq][:, c0:c0 + 128], start=True, stop=(not hp)),
                            reads=["v"], rhs=[("PT", pq)], writes=[("ps", ob_)])
                        if hp:
                            mk.op("pe", lambda e, ob_=ob_, vv=vv, c0=c0, osl=osl, blk=blk, pq=pq: e.matmul(
                                ps[ob_][:, osl], lhsT=vv[:, blk - 1, :], rhs=PT[pq][:, 256 + c0:256 + c0 + 128], start=False, stop=True),
                                reads=["v"], rhs=[("PT", pq)], writes=[("ps", ob_)])
                    if q4 == 3:
                        b0 = blk - 3
                        for ob_, acc, akey in ((oa, accA, "accA"), (ob, accB, "accB")):
                            if dil == 1:
                                o = acc[:, b0 * 128:b0 * 128 + 512]
                                src = ps[ob_][:, :]
                            elif dil == 4:
                                o = acc[:, slice(b0 // 4, S, 4)]
                                src = ps[ob_][:, :]
                            else:
                                o = acc[:, :].rearrange("p (m r) -> p r m", r=16)[:, b0:b0 + 4, :]
                                src = ps[ob_][:, :].rearrange("p (a b) -> p a b", a=4)
                            akeys = [(akey, b0 // 4)] if dil == 1 else [(akey, t) for t in range(4)]
                            if gi == 0:
                                mk.op("act", lambda e, o=o, src=src: e.activation(out=o, in_=src, func=AF.Copy),
                                      writes=[("ps", ob_)] + akeys)
                            else:
                                mk.op("dve", lambda e, o=o, src=src: e.tensor_tensor(out=o, in0=src, in1=o, op=ALU.add),
                                      writes=[("ps", ob_)] + akeys)
                        if gi == 2:
                            normalise_tile(p, b0 // 4)

                for step in range(16 + LA):
                    if step < 16:
                        qk_stage(step)
                    if step >= LA:
                        pv_stage(step - LA)
        wo3 = self.a_w_o[j].rearrange("(c p) n -> p c n", p=128)
        it = 0
        for half in range(2):
            for dc in range(NCH):
                w = wo[it % 2]
                wk = ("woA", it % 2)
                it += 1
                mk.dma("pool", lambda e, w=w, dc=dc: e.dma_start(out=w[:], in_=wo3[:, :, dc * 128:(dc + 1) * 128]), slot=wk, writes=[wk])
                for tt in (2 * half, 2 * half + 1):
                    tsl = slice(tt * 512, (tt + 1) * 512)
                    b = self.bank()
                    for kc in range(4):
                        mk.op("pe", lambda e, b=b, kc=kc, w=w, tsl=tsl: e.matmul(ps[b][:, :], lhsT=w[:, kc, :], rhs=mT[:, kc, tsl], start=(kc == 0), stop=(kc == 3)),
                              reads=[wk], rhs=[("mT", kc, tt)], writes=[("ps", b)])
                    mk.op("dve", lambda e, b=b, dc=dc, tsl=tsl: e.tensor_tensor(out=xT[:, dc, tsl], in0=ps[b][:, :], in1=xT[:, dc, tsl], op=ALU.add),
                          writes=[("ps", b), ("x", dc, tt)])
                if half == 1:
                    if dc == 0:
                        self.norm_S(0)
                    elif dc == 2:
                        self.norm_S(1)
                    elif dc == 4:
                        self.norm_H(0)
                    elif dc == 6:
                        self.norm_H(1)
        self.defer_norm([("S", 2), ("S", 3), ("H", 2), ("H", 3)])

    def phase_B(self, s, l):
        jb = l // 2
        mk, ps, xT, hT, cf, cb = self.mk, self.ps, self.xT, self.hT, self.cf, self.cb
        gnorm, wgu = self.gnorm, self.wgu
        T = self.T
        pf = self.alloc_pf()
        gd = T("gd", [32, S], F32)
        Lt = T("Lt", [128, 512], F32)
        EB = T("EB", [128, 512], F32)
        EBi = T("EBi", [128, 512], F32)
        Er = T("Er", [128, 512], F32)
        dec = T("dec", [128, 16], F32)
        qi = T("qi", [128, S], BF16)
        kinv = T("kinv", [128, S], BF16)
        ks = T("ks", [128, 16, 128], BF16)
        vv = T("vv", [128, 16, 256], BF16)
        oTh = T("oTh", [128, 2, S], BF16)
        S32 = T("S32", [128, 256], F32)
        Sbf = [T("Sbf%d" % i, [128, 256], BF16) for i in range(2)]
        aTm = [T("aTm%d" % i, [128, 128], BF16) for i in range(3)]
        sq, rs = self.nsq, self.nrs
        sr = [T("sr%d" % i, [128, 512], BF16) for i in range(2)]
        t1 = [T("t1%d" % i, [128, 512], F32) for i in range(2)]
        yT = T("yT", [128, 4, S], BF16)
        wq = [pf[:, 0:1024].rearrange("p (c n) -> p c n", c=NCH), T("wq1", [128, NCH, 128], BF16)[:]]
        wk_ = [pf[:, 1024:2048].rearrange("p (c n) -> p c n", c=NCH), T("wk1", [128, NCH, 128], BF16)[:]]
        wv = pf[:, 2048:4096].rearrange("p (c n) -> p c n", c=NCH)
        wr = [T("wr%d" % i, [128, NCH, 128], BF16) for i in range(2)]
        wg = T("wg", [128, NCH, 16], BF16)
        wo = [T("woB%d" % i, [128, 4, 128], BF16) for i in range(2)]

        win3 = self.b_w_in[jb].rearrange("(c p) n -> p c n", p=128)
        wo3 = self.b_w_o[jb].rearrange("(c p) n -> p c n", p=128)
        mk.dma("pool", lambda e: e.dma_start(out=wg[:], in_=win3[:, :, 3072:3088]), slot="wg", writes=["wg"])
        if not self.was_prefetched:
            mk.dma("pool", lambda e: e.dma_start(out=wq[0], in_=win3[:, :, 0:128]), slot=("wq", 0), writes=[("wq", 0)])
            mk.dma("pool", lambda e: e.dma_start(out=wk_[0], in_=win3[:, :, 512:640]), slot=("wk", 0), writes=[("wk", 0)])
            mk.dma("pool", lambda e: e.dma_start(out=wv, in_=win3[:, :, 1024:1280]), slot="wv", writes=["wv"])
        self.flush_pending()
        mk.op("pool", lambda e: e.memset(gd[:, :], 1.0), writes=["gd"])
        for tt in range(4):
            tsl = slice(tt * 512, (tt + 1) * 512)
            b = self.bank()
            for c in range(NCH):
                mk.op("pe", lambda e, b=b, c=c, tsl=tsl: e.matmul(ps[b][0:16, :], lhsT=wg[:, c, :], rhs=hT[:, c, tsl], start=(c == 0), stop=(c == NCH - 1)),
                      reads=["wg", ("hT", c, tt)], writes=[("ps", b)])
            mk.op("act", lambda e, b=b, tsl=tsl: e.activation(out=gd[0:16, tsl], in_=ps[b][0:16, :], func=AF.Copy), writes=[("ps", b), "gd"])

        for h in range(4):
            hs = h % 2
            if h > 0:
                mk.dma("pool", lambda e, h=h, hs=hs: e.dma_start(out=wq[hs], in_=win3[:, :, h * 128:(h + 1) * 128]), slot=("wq", hs), writes=[("wq", hs)])
                mk.dma("pool", lambda e, h=h, hs=hs: e.dma_start(out=wk_[hs], in_=win3[:, :, 512 + h * 128:512 + (h + 1) * 128]), slot=("wk", hs), writes=[("wk", hs)])
                mk.dma("pool", lambda e, h=h: e.dma_start(out=wv, in_=win3[:, :, 1024 + h * 256:1024 + (h + 1) * 256]), slot="wv", writes=["wv"])
            for tt in range(4):
                tsl = slice(tt * 512, (tt + 1) * 512)
                bl = self.bank()
                for j4 in range(4):
                    t16 = tt * 4 + j4
                    mk.op("pe", lambda e, bl=bl, j4=j4, t16=t16, h=h: e.matmul(
                        ps[bl][:, j4 * 128:(j4 + 1) * 128], lhsT=gd[0:17, t16 * 128:(t16 + 1) * 128],
                        rhs=wgu[0:17, jb * 512 + h * 128:jb * 512 + (h + 1) * 128], start=True, stop=True),
                        reads=["gd", "wgu"], writes=[("ps", bl)])
                mk.op("act", lambda e, bl=bl: e.activation(out=Lt[:], in_=ps[bl][:, :], func=AF.Exp, scale=-1.0), writes=[("ps", bl), "Lt"])
                mk.op("act", lambda e: e.activation(out=Lt[:], in_=Lt[:], func=AF.Ln, bias=cf[:, C_ONE:C_ONE + 1]), reads=["cf"], writes=["Lt"])
                bq = self.bank()
                for c in range(NCH):
                    mk.op("pe", lambda e, bq=bq, c=c, hs=hs, tsl=tsl: e.matmul(ps[bq][:, :], lhsT=wq[hs][:, c, :], rhs=hT[:, c, tsl], start=(c == 0), stop=(c == NCH - 1)),
                          reads=[("wq", hs), ("hT", c, tt)], writes=[("ps", bq)])
                bk = self.bank()
                for c in range(NCH):
                    mk.op("pe", lambda e, bk=bk, c=c, hs=hs, tsl=tsl: e.matmul(ps[bk][:, :], lhsT=wk_[hs][:, c, :], rhs=hT[:, c, tsl], start=(c == 0), stop=(c == NCH - 1)),
                          reads=[("wk", hs), ("hT", c, tt)], writes=[("ps", bk)])
                bc, br = self.bank(), self.bank()
                for j4 in range(4):
                    jsl = slice(j4 * 128, (j4 + 1) * 128)
                    mk.op("pe", lambda e, bc=bc, jsl=jsl: e.matmul(ps[bc][:, jsl], lhsT=Lt[:, jsl], rhs=cf[:, C_TRI:C_TRI + 128], start=True, stop=True),
                          reads=["Lt", "cf"], writes=[("ps", bc)])
                    mk.op("pe", lambda e, br=br, jsl=jsl: e.matmul(ps[br][:, jsl], lhsT=cf[:, C_TRS:C_TRS + 128], rhs=Lt[:, jsl], start=True, stop=True),
                          reads=["cf"], rhs=["Lt"], writes=[("ps", br)])
                mk.op("act", lambda e, bc=bc: e.activation(out=EB[:], in_=ps[bc][:, :], func=AF.Exp, scale=-1.0 / 16), writes=[("ps", bc), "EB"])
                mk.op("act", lambda e, bc=bc: e.activation(out=EBi[:], in_=ps[bc][:, :], func=AF.Exp, scale=1.0 / 16), writes=[("ps", bc), "EBi"])
                mk.op("act", lambda e, br=br: e.activation(out=Er[:], in_=ps[br][:, :], func=AF.Exp, scale=-1.0 / 16), writes=[("ps", br), "Er"])
                mk.op("dve", lambda e, tt=tt: e.tensor_copy(out=dec[:, tt * 4:(tt + 1) * 4], in_=EB[:, slice(127, 512, 128)]), reads=["EB"], writes=["dec"])
                mk.op("dve", lambda e, bq=bq, tsl=tsl: e.scalar_tensor_tensor(out=qi[:, tsl], in0=ps[bq][:, :], scalar=128.0 ** -0.5, in1=EB[:], op0=ALU.mult, op1=ALU.mult),
                      reads=["EB"], writes=[("ps", bq), "qi"])
                mk.op("dve", lambda e, bk=bk, tsl=tsl: e.tensor_tensor(out=kinv[:, tsl], in0=ps[bk][:, :], in1=EBi[:], op=ALU.mult),
                      reads=["EBi"], writes=[("ps", bk), "kinv"])
                bkt = self.bank()
                for j4 in range(4):
                    t16 = tt * 4 + j4
                    for c in range(NCH):
                        mk.op("pe", lambda e, bkt=bkt, j4=j4, t16=t16, c=c, hs=hs: e.matmul(
                            ps[bkt][:, j4 * 128:(j4 + 1) * 128], lhsT=hT[:, c, t16 * 128:(t16 + 1) * 128], rhs=wk_[hs][:, c, :], start=(c == 0), stop=(c == NCH - 1)),
                            reads=[("wk", hs), ("hT", c, tt)], writes=[("ps", bkt)])
                mk.op("dve", lambda e, bkt=bkt, tt=tt: e.tensor_tensor(
                    out=ks[:, tt * 4:(tt + 1) * 4, :], in0=ps[bkt][:, :].rearrange("p (a b) -> p a b", a=4), in1=Er[:, :].rearrange("p (a b) -> p a b", a=4), op=ALU.mult),
                    reads=["Er"], writes=[("ps", bkt), "ks"])
                for half in range(2):
                    bv = self.bank()
                    for jj in range(2):
                        t16 = tt * 4 + half * 2 + jj
                        for c in range(NCH):
                            mk.op("pe", lambda e, bv=bv, jj=jj, t16=t16, c=c: e.matmul(
                                ps[bv][:, jj * 256:(jj + 1) * 256], lhsT=hT[:, c, t16 * 128:(t16 + 1) * 128], rhs=wv[:, c, :], start=(c == 0), stop=(c == NCH - 1)),
                                reads=["wv", ("hT", c, tt)], writes=[("ps", bv)])
                    t0 = tt * 4 + half * 2
                    mk.op("act", lambda e, bv=bv, t0=t0: e.activation(out=vv[:, t0:t0 + 2, :], in_=ps[bv][:, :].rearrange("p (a b) -> p a b", a=2), func=AF.Copy),
                          writes=[("ps", bv), "vv"])
            if h == 3:
                self.prefetch_next([("wq", 0), ("wk", 0), "wv"])
            LA = 2
            b1s = {}

            def rec_stage1(c):
                csl = slice(c * 128, (c + 1) * 128)
                b1 = self.bank()
                b1s[c] = b1
                mk.op("pe", lambda e, b1=b1, csl=csl: e.matmul(ps[b1][:, 0:128], lhsT=kinv[:, csl], rhs=qi[:, csl], start=True, stop=True),
                      reads=["kinv", "qi"], writes=[("ps", b1)])
                if c < 15:
                    mk.op("pe", lambda e, b1=b1, c=c: e.matmul(ps[b1][:, 128:384], lhsT=ks[:, c, :], rhs=vv[:, c, :], start=True, stop=True),
                          reads=["ks", "vv"], writes=[("ps", b1)])
                aq = c % 3
                mk.op("dve", lambda e, b1=b1, aq=aq: e.tensor_tensor(out=aTm[aq][:], in0=ps[b1][:, 0:128], in1=cf[:, C_TRI:C_TRI + 128], op=ALU.mult),
                      reads=["cf"], writes=[("ps", b1), ("aTm", aq)])

            def rec_stage2(c):
                csl = slice(c * 128, (c + 1) * 128)
                b1 = b1s[c]
                aq = c % 3
                bo = self.bank()
                sqn = c % 2
                for vc in range(2):
                    vsl = slice(vc * 128, (vc + 1) * 128)
                    if c > 0:
                        mk.op("pe", lambda e, bo=bo, vsl=vsl, sqn=sqn, csl=csl: e.matmul(ps[bo][:, vsl], lhsT=Sbf[sqn][:, vsl], rhs=qi[:, csl], start=True, stop=False),
                              reads=[("Sbf", sqn), "qi"], writes=[("ps", bo)])
                    mk.op("pe", lambda e, bo=bo, vsl=vsl, aq=aq, c=c: e.matmul(ps[bo][:, vsl], lhsT=vv[:, c, vsl], rhs=aTm[aq][:], start=(c == 0), stop=True),
                          reads=["vv"], rhs=[("aTm", aq)], writes=[("ps", bo)])
                if c < 15:
                    if c == 0:
                        mk.op("dve", lambda e, b1=b1: e.tensor_copy(out=S32[:], in_=ps[b1][:, 128:384]), writes=[("ps", b1), "S32"])
                    else:
                        mk.op("dve", lambda e, b1=b1, c=c: e.scalar_tensor_tensor(out=S32[:], in0=S32[:], scalar=dec[:, c:c + 1], in1=ps[b1][:, 128:384], op0=ALU.mult, op1=ALU.add),
                              reads=["dec"], writes=[("ps", b1), "S32"])
                    nq = (c + 1) % 2
                    mk.op("act", lambda e, nq=nq: e.activation(out=Sbf[nq][:], in_=S32[:], func=AF.Copy), reads=["S32"], writes=[("Sbf", nq)])
                mk.op("act", lambda e, bo=bo, csl=csl: e.activation(out=oTh[:, :, csl], in_=ps[bo][:, 0:256].rearrange("p (a b) -> p a b", a=2), func=AF.Copy),
                      writes=[("ps", bo), "oTh"])

            for step in range(16 + LA):
                if step < 16:
                    rec_stage1(step)
                if step >= LA:
                    rec_stage2(step - LA)
            hh = h % 2
            for vc in range(2):
                mk.dma("pool", lambda e, h=h, vc=vc: e.dma_start(out=wr[vc][:], in_=win3[:, :, 2048 + h * 256 + vc * 128:2048 + h * 256 + (vc + 1) * 128]),
                       slot=("wr", vc), writes=[("wr", vc)])
            for tt in range(4):
                tsl = slice(tt * 512, (tt + 1) * 512)
                bn = self.bank()
                brrs = [self.bank(), self.bank()]
                for vc in range(2):
                    mk.op("dve", lambda e, vc=vc, tsl=tsl: e.tensor_tensor(out=sq[vc][:], in0=oTh[:, vc, tsl], in1=oTh[:, vc, tsl], op=ALU.mult),
                          reads=["oTh"], writes=[("nsq", vc)])
                for vc in range(2):
                    brr = brrs[vc]
                    for c in range(NCH):
                        mk.op("pe", lambda e, brr=brr, c=c, vc=vc, tsl=tsl: e.matmul(ps[brr][:, :], lhsT=wr[vc][:, c, :], rhs=hT[:, c, tsl], start=(c == 0), stop=(c == NCH - 1)),
                              reads=[("wr", vc), ("hT", c, tt)], writes=[("ps", brr)])
                for vc in range(2):
                    mk.op("pe", lambda e, bn=bn, vc=vc: e.matmul(ps[bn][:, :], lhsT=cb[:, C_ONES:C_ONES + 128], rhs=sq[vc][:], start=(vc == 0), stop=(vc == 1)),
                          reads=["cb"], rhs=[("nsq", vc)], writes=[("ps", bn)])
                r = tt % 2
                mk.op("act", lambda e, bn=bn, r=r: e.activation(out=rs[r][:], in_=ps[bn][:, :], func=AF.Ln, scale=1.0 / 256, bias=cf[:, C_EPS:C_EPS + 1]),
                      reads=["cf"], writes=[("ps", bn), ("nrs", r)])
                mk.op("act", lambda e, r=r: e.activation(out=rs[r][:], in_=rs[r][:], func=AF.Exp, scale=-0.5), writes=[("nrs", r)])
                for vc in range(2):
                    brr = brrs[vc]
                    mk.op("act", lambda e, brr=brr, vc=vc: e.activation(out=sr[vc][:], in_=ps[brr][:, :], func=AF.Silu), writes=[("ps", brr), ("sr", vc)])
                    mk.op("dve", lambda e, vc=vc, r=r, tsl=tsl: e.scalar_tensor_tensor(
                        out=t1[vc][:], in0=oTh[:, vc, tsl], scalar=gnorm[:, jb * 2 + vc:jb * 2 + vc + 1], in1=rs[r][:], op0=ALU.mult, op1=ALU.mult),
                        reads=["oTh", ("nrs", r), "gnorm"], writes=[("t1", vc)])
                    mk.op("dve", lambda e, vc=vc, tsl=tsl, hh=hh: e.tensor_tensor(out=yT[:, hh * 2 + vc, tsl], in0=t1[vc][:], in1=sr[vc][:], op=ALU.mult),
                          reads=[("t1", vc), ("sr", vc)], writes=[("yT", hh * 2 + vc, tt)])
            if hh == 1:
                pair = h // 2
                if pair == 0:
                    for dc in range(NCH):
                        w = wo[dc % 2]
                        wkk = ("woB", dc % 2)
                        mk.dma("pool", lambda e, w=w, dc=dc, pair=pair: e.dma_start(out=w[:], in_=wo3[:, pair * 4:(pair + 1) * 4, dc * 128:(dc + 1) * 128]), slot=wkk, writes=[wkk])
                        for tt in range(4):
                            tsl = slice(tt * 512, (tt + 1) * 512)
                            b = self.bank()
                            for kc in range(4):
                                mk.op("pe", lambda e, b=b, kc=kc, w=w, tsl=tsl: e.matmul(ps[b][:, :], lhsT=w[:, kc, :], rhs=yT[:, kc, tsl], start=(kc == 0), stop=(kc == 3)),
                                      reads=[wkk], rhs=[("yT", kc, tt)], writes=[("ps", b)])
                            mk.op("dve", lambda e, b=b, dc=dc, tsl=tsl: e.tensor_tensor(out=xT[:, dc, tsl], in0=ps[b][:, :], in1=xT[:, dc, tsl], op=ALU.add),
                                  writes=[("ps", b), ("x", dc, tt)])
                else:
                    it = 0
                    for half in range(2):
                        for dc in range(NCH):
                            w = wo[it % 2]
                            wkk = ("woB", it % 2)
                            it += 1
                            mk.dma("pool", lambda e, w=w, dc=dc, pair=pair: e.dma_start(out=w[:], in_=wo3[:, pair * 4:(pair + 1) * 4, dc * 128:(dc + 1) * 128]), slot=wkk, writes=[wkk])
                            for tt in (2 * half, 2 * half + 1):
                                tsl = slice(tt * 512, (tt + 1) * 512)
                                b = self.bank()
                                for kc in range(4):
                                    mk.op("pe", lambda e, b=b, kc=kc, w=w, tsl=tsl: e.matmul(ps[b][:, :], lhsT=w[:, kc, :], rhs=yT[:, kc, tsl], start=(kc == 0), stop=(kc == 3)),
                                          reads=[wkk], rhs=[("yT", kc, tt)], writes=[("ps", b)])
                                mk.op("dve", lambda e, b=b, dc=dc, tsl=tsl: e.tensor_tensor(out=xT[:, dc, tsl], in0=ps[b][:, :], in1=xT[:, dc, tsl], op=ALU.add),
                                      writes=[("ps", b), ("x", dc, tt)])
                            if half == 1:
                                if dc == 0:
                                    self.norm_S(0)
                                elif dc == 2:
                                    self.norm_S(1)
                                elif dc == 4:
                                    self.norm_H(0)
                                elif dc == 6:
                                    self.norm_H(1)
                    self.defer_norm([("S", 2), ("S", 3), ("H", 2), ("H", 3)])

    def phase_F(self, s, l):
        mk, ps, xT, hT, cf, cb, convp = self.mk, self.ps, self.xT, self.hT, self.cf, self.cb, self.convp
        T = self.T
        pf = self.alloc_pf()
        yT = T("yF", [128, FGRP, S], BF16)
        asb = [T("asb%d" % i, [128, 514], F32) for i in range(2)]
        ct = [T("ct%d" % i, [128, 512], F32) for i in range(2)]
        sb = [T("sb%d" % i, [128, 512], BF16) for i in range(2)]
        win = [pf[:, 0:2048].rearrange("p (c n) -> p c n", c=NCH), pf[:, 2048:4096].rearrange("p (c n) -> p c n", c=NCH),
               T("win2", [128, NCH, 256], BF16)[:]]
        wdr = T("wdr", [128, NCH, FGRP, 128], BF16)
        wd = [wdr[:, i, :, :] for i in range(2)]

        win3 = self.f_w_in[l].rearrange("(c p) n -> p c n", p=128)
        wd3 = self.f_w_down[l].rearrange("(c p) n -> p c n", p=128)

        def cp(k, fc):
            o = (l * 4 + k) * NFC + fc
            return convp[:, o:o + 1]
        it = 0
        nt = 0

        def load_win(fc):
            w = win[fc % 3]
            wk = ("win", fc % 3)
            mk.dma("pool", lambda e, w=w, fc=fc: e.dma_start(out=w[:, :, 0:128], in_=win3[:, :, fc * 128:(fc + 1) * 128]), slot=wk, writes=[wk])
            mk.dma("pool", lambda e, w=w, fc=fc: e.dma_start(out=w[:, :, 128:256], in_=win3[:, :, FFN + fc * 128:FFN + (fc + 1) * 128]), slot=wk, writes=[wk])
        if not self.was_prefetched:
            load_win(0)
        load_win(1)
        self.flush_pending()
        for grp in range(2):
            for fi in range(FGRP):
                if grp == 0 and fi == 8:
                    for dc in range(2):
                        mk.dma("pool", lambda e, dc=dc: e.dma_start(out=wd[dc], in_=wd3[:, 0:FGRP, dc * 128:(dc + 1) * 128]), slot=("wd", dc), writes=[("wd", dc)])
                if grp == 1 and 1 <= fi <= 8:
                    dcl = fi - 1
                    mk.dma("pool", lambda e, dc=dcl: e.dma_start(out=wdr[:, dc, :, :], in_=wd3[:, FGRP:2 * FGRP, dc * 128:(dc + 1) * 128]), slot=("wdr", dcl), writes=[("wdr", dcl)] + ([("wd", dcl)] if dcl < 2 else []))
                fc = grp * FGRP + fi
                w = win[it % 3]
                wk = ("win", it % 3)
                it += 1
                if fc + 2 < NFC:
                    load_win(fc + 2)
                for tt in range(4):
                    tsl = slice(tt * 512, (tt + 1) * 512)
                    q = nt % 2
                    nt += 1
                    a_cur, a_prev = asb[q], asb[1 - q]
                    ba, bu = self.bank(), self.bank()
                    for c in range(NCH):
                        mk.op("pe", lambda e, ba=ba, c=c, w=w, tsl=tsl: e.matmul(ps[ba][:, :], lhsT=w[:, c, 0:128], rhs=hT[:, c, tsl], start=(c == 0), stop=(c == NCH - 1)),
                              reads=[wk, ("hT", c, tt)], writes=[("ps", ba)] + ([("ps", bu)] if c == 0 else []))
                    for c in range(NCH):
                        mk.op("pe", lambda e, bu=bu, c=c, w=w, tsl=tsl: e.matmul(ps[bu][:, :], lhsT=w[:, c, 128:256], rhs=hT[:, c, tsl], start=(c == 0), stop=(c == NCH - 1)),
                              reads=[wk, ("hT", c, tt)], writes=[("ps", bu)])
                    if tt == 0:
                        mk.op("pool", lambda e, a_cur=a_cur: e.memset(a_cur[:, 0:2], 0.0), writes=[("asb", q)])
                    else:
                        mk.op("pool", lambda e, a_cur=a_cur, a_prev=a_prev: e.tensor_copy(out=a_cur[:, 0:2], in_=a_prev[:, 512:514]),
                              reads=[("asb", 1 - q)], writes=[("asb", q)])
                    mk.op("act", lambda e, ba=ba, a_cur=a_cur: e.activation(out=a_cur[:, 2:514], in_=ps[ba][:, :], func=AF.Copy), writes=[("ps", ba), ("asb", q)])
                    mk.op("act", lambda e, ba=ba, q=q, fc=fc: e.activation(out=ct[q][:], in_=ps[ba][:, :], func=AF.Identity, scale=cp(2, fc), bias=cp(3, fc)),
                          reads=["convp"], writes=[("ps", ba), ("ct", q)])
                    mk.op("dve", lambda e, q=q, a_cur=a_cur, fc=fc: e.scalar_tensor_tensor(out=ct[q][:], in0=a_cur[:, 1:513], scalar=cp(1, fc), in1=ct[q][:], op0=ALU.mult, op1=ALU.add),
                          reads=[("asb", q), "convp"], writes=[("ct", q)])
                    mk.op("dve", lambda e, q=q, a_cur=a_cur, fc=fc: e.scalar_tensor_tensor(out=ct[q][:], in0=a_cur[:, 0:512], scalar=cp(0, fc), in1=ct[q][:], op0=ALU.mult, op1=ALU.add),
                          reads=[("asb", q), "convp"], writes=[("ct", q)])
                    mk.op("act", lambda e, q=q: e.activation(out=sb[q][:], in_=ct[q][:], func=AF.Silu), reads=[("ct", q)], writes=[("sb", q)])
                    mk.op("dve", lambda e, q=q, bu=bu, fi=fi, tsl=tsl: e.tensor_tensor(out=yT[:, fi, tsl], in0=ps[bu][:, :], in1=sb[q][:], op=ALU.mult),
                          reads=[("sb", q)], writes=[("ps", bu), ("yF", fi, tt)])
            if grp == 1:
                self.prefetch_next([("win", 0), ("win", 1)])
            if grp == 0:
                for dc in range(NCH):
                    w = wd[dc % 2]
                    wkk = ("wd", dc % 2)
                    if dc >= 2:
                        mk.dma("pool", lambda e, w=w, dc=dc, grp=grp: e.dma_start(out=w, in_=wd3[:, grp * FGRP:(grp + 1) * FGRP, dc * 128:(dc + 1) * 128]), slot=wkk, writes=[wkk])
                    for tt in range(4):
                        tsl = slice(tt * 512, (tt + 1) * 512)
                        b = self.bank()
                        for fi in range(FGRP):
                            mk.op("pe", lambda e, b=b, fi=fi, w=w, tsl=tsl: e.matmul(ps[b][:, :], lhsT=w[:, fi, :], rhs=yT[:, fi, tsl], start=(fi == 0), stop=(fi == FGRP - 1)),
                                  reads=[wkk], rhs=[("yF", fi, tt)], writes=[("ps", b)])
                        mk.op("dve", lambda e, b=b, dc=dc, tsl=tsl: e.tensor_tensor(out=xT[:, dc, tsl], in0=ps[b][:, :], in1=xT[:, dc, tsl], op=ALU.add),
                              writes=[("ps", b), ("x", dc, tt)])
            else:
                for tt in range(4):
                    tsl = slice(tt * 512, (tt + 1) * 512)
                    for dc in range(NCH):
                        b = self.bank()
                        for fi in range(FGRP):
                            mk.op("pe", lambda e, b=b, fi=fi, dc=dc, tsl=tsl: e.matmul(ps[b][:, :], lhsT=wdr[:, dc, fi, :], rhs=yT[:, fi, tsl], start=(fi == 0), stop=(fi == FGRP - 1)),
                                  reads=[("wdr", dc)], rhs=[("yF", fi, tt)], writes=[("ps", b)])
                        mk.op("dve", lambda e, b=b, dc=dc, tsl=tsl: e.tensor_tensor(out=xT[:, dc, tsl], in0=ps[b][:, :], in1=xT[:, dc, tsl], op=ALU.add),
                              writes=[("ps", b), ("x", dc, tt)])
                        if tt >= 1 and dc == 1:
                            self.norm_S(tt - 1)
                        if tt >= 1 and dc == 5:
                            self.norm_H(tt - 1)
                self.defer_norm([("S", 3), ("H", 3)])

    def phase_store(self, s):
        T = self.T
        self.flush_pending()
        of = [T("of%d" % i, [128, 512], F32) for i in range(2)]
        ost = [T("ost%d" % i, [128, 4, D], F32) for i in range(2)]
        toks = []
        for tt in range(4):
            self.store_tile(s, tt, of, ost, toks)
        self.mk.wait("sp", toks)

    def phase_storeload(self, s0, s1):
        T = self.T
        self.flush_pending()
        of = [T("of%d" % i, [128, 512], F32) for i in range(2)]
        ost = [T("ost%d" % i, [128, 4, D], F32) for i in range(2)]
        xs = [T("xs%d" % i, [128, D], F32) for i in range(3)]
        toks = []
        for tt in range(4):
            self.store_tile(s0, tt, of, ost, toks)
            self.load_tiles(s1, tt, xs, "pool")
        self.mk.wait("sp", toks)

    def store_tile(self, s, tt, of, ost, toks):
        mk, ps, xT, cf, cb, gains = self.mk, self.ps, self.xT, self.cf, self.cb, self.gains
        sq, rs = self.nsq, self.nrs
        if True:
            tsl = slice(tt * 512, (tt + 1) * 512)
            r = tt % 2
            if self.final_norm:
                b = self.bank()
                for c in range(NCH):
                    q = c % 2
                    mk.op("act", lambda e, c=c, q=q, tsl=tsl: e.activation(out=sq[q][:], in_=xT[:, c, tsl], func=AF.Square),
                          reads=[("x", c, tt)], writes=[("nsq", q)])
                    mk.op("pe", lambda e, b=b, c=c, q=q: e.matmul(ps[b][:, :], lhsT=cb[:, C_ONES:C_ONES + 128], rhs=sq[q][:], start=(c == 0), stop=(c == NCH - 1)),
                          reads=["cb"], rhs=[("nsq", q)], writes=[("ps", b)])
                mk.op("act", lambda e, b=b, r=r: e.activation(out=rs[r][:], in_=ps[b][:, :], func=AF.Ln, scale=1.0 / D, bias=cf[:, C_EPS:C_EPS + 1]),
                      reads=["cf"], writes=[("ps", b), ("nrs", r)])
                mk.op("act", lambda e, r=r: e.activation(out=rs[r][:], in_=rs[r][:], func=AF.Exp, scale=-0.5), writes=[("nrs", r)])
            o = ost[r]
            for c in range(NCH):
                q = c % 2
                if self.final_norm:
                    mk.op("dve", lambda e, c=c, q=q, r=r, tsl=tsl: e.scalar_tensor_tensor(
                        out=of[q][:], in0=xT[:, c, tsl], scalar=gains[:, 64 + c:65 + c], in1=rs[r][:], op0=ALU.mult, op1=ALU.mult),
                        reads=[("x", c, tt), ("nrs", r), "gains"], writes=[("of", q)])
                    src = of[q]
                    srck = [("of", q)]
                b2 = self.bank()
                for j in range(4):
                    if self.final_norm:
                        inp = src[:, j * 128:(j + 1) * 128]
                    else:
                        inp = xT[:, c, tt * 512 + j * 128:tt * 512 + (j + 1) * 128]
                        srck = [("x", c, tt)]
                    mk.op("pe", lambda e, b2=b2, j=j, inp=inp: e.transpose(out=ps[b2][:, j * 128:(j + 1) * 128], in_=inp, identity=cf[:, C_ID:C_ID + 128]),
                          reads=srck + ["cf"], writes=[("ps", b2)])
                mk.op("act", lambda e, b2=b2, c=c, o=o: e.activation(out=o[:, :, c * 128:(c + 1) * 128], in_=ps[b2][:, :].rearrange("p (a b) -> p a b", a=4), func=AF.Copy),
                      writes=[("ps", b2), ("ost", r)])
            toks.append(mk.dma("sp", lambda e, o=o, tsl=tsl: e.dma_start(out=self.y[s, tsl, :].rearrange("(j p) n -> p j n", p=128), in_=o[:]),
                               slot=("ost", r), reads=[("ost", r)]))


def make_consts():
    c = np.zeros((128, NCONST), np.float32)
    idx = np.arange(128)
    c[:, C_ID:C_ID + 128] = np.eye(128, dtype=np.float32)
    c[:, C_TRI:C_TRI + 128] = (idx[:, None] <= idx[None, :]).astype(np.float32)
    c[:, C_TRS:C_TRS + 128] = (idx[:, None] > idx[None, :]).astype(np.float32)
    mc = np.where(idx[:, None] <= idx[None, :], 0.0, NEG).astype(np.float32)
    mp = np.where(idx[:, None] >= idx[None, :], 0.0, NEG).astype(np.float32)
    c[:, C_MASK:C_MASK + 512] = np.concatenate([mc, mc, mp, mp], axis=1)
    c[:, C_ONES:C_ONES + 128] = 1.0
    fi = (idx % 32).astype(np.float64)
    inv = 10000.0 ** (-(2.0 * fi) / 64.0)
    c[:, C_INV] = (inv / TWO_PI).astype(np.float32)
    sc = TWO_PI * (1.0 - 1e-6)
    c[:, C_SGN] = np.where((idx % 64) < 32, -sc, sc).astype(np.float32)
    c[:, C_EPS] = 1e-6
    c[:, C_2PI] = sc
    c[:, C_ONE] = 1.0
    c[:, C_PERM:C_PERM + 128] = (idx[:, None] == (idx[None, :] ^ 32)).astype(np.float32)
    return c


def full_plan(nseq):
    plan = []
    for s in range(nseq):
        if s == 0:
            plan.append(("load", s))
        for l in range(DEPTH):
            plan.append(("A" if l % 2 == 0 else "B", s, l))
            plan.append(("F", s, l))
        if s + 1 < nseq:
            plan.append(("storeload", s, s + 1))
        else:
            plan.append(("store", s))
    return plan


def layout_params(inp):
    f32 = np.float32
    gains = np.zeros((128, 72), f32)
    gains[:, 0:32] = np.asarray(inp["norm_mix"], f32).reshape(4, 8, 128).transpose(2, 0, 1).reshape(128, 32)
    gains[:, 32:64] = np.asarray(inp["norm_ffn"], f32).reshape(4, 8, 128).transpose(2, 0, 1).reshape(128, 32)
    gains[:, 64:72] = np.asarray(inp["norm_final"], f32).reshape(8, 128).T
    cw = np.asarray(inp["f_conv_w"], f32).reshape(4, 3, NFC, 128)
    cbias = np.asarray(inp["f_conv_b"], f32).reshape(4, 1, NFC, 128)
    convp = np.concatenate([cw, cbias], axis=1).transpose(3, 0, 1, 2).reshape(128, 4 * 4 * NFC)
    gnorm = np.asarray(inp["b_g_norm"], f32).reshape(2, 2, 128).transpose(2, 0, 1).reshape(128, 4)
    wgu = np.zeros((32, 2, 512), f32)
    wgu[0:16] = np.asarray(inp["b_w_gate_up"], f32).transpose(1, 0, 2)
    wgu[16] = np.asarray(inp["b_b_gate_up"], f32)
    return dict(gains=np.ascontiguousarray(gains), convp=np.ascontiguousarray(convp),
                gnorm=np.ascontiguousarray(gnorm), wgu=np.ascontiguousarray(wgu.reshape(32, 1024)))


def kernel(**inputs):
    n = 8
    nseq = 2
    bld = Builder(nseq, full_plan(nseq))
    nc = bld.build()
    shared = layout_params(inputs)
    shared["cst"] = make_consts()
    for k in ("a_w_qkv", "a_w_o", "b_w_in", "b_w_o", "f_w_in", "f_w_down"):
        shared[k] = np.ascontiguousarray(np.asarray(inputs[k], np.float32))
    x = np.asarray(inputs["x"], np.float32)
    pos = np.asarray(inputs["positions"], np.int32)
    in_maps = []
    for i in range(n):
        m = dict(shared)
        m["x"] = np.ascontiguousarray(x[i * nseq:(i + 1) * nseq])
        m["pos"] = np.ascontiguousarray(pos[i * nseq:(i + 1) * nseq])
        in_maps.append(m)
    res = run_bass_kernel_spmd(nc, in_maps, core_ids=list(range(n)))
    return np.concatenate([r["y"] for r in res.results], axis=0)
```

```python
from contextlib import ExitStack
import numpy as np
import concourse.bass as bass
import concourse.mybir as mybir
from concourse.bass_utils import run_bass_kernel_spmd

F32 = mybir.dt.float32
BF16 = mybir.dt.bfloat16
I32 = mybir.dt.int32
AF = mybir.ActivationFunctionType
ALU = mybir.AluOpType

S = 2048
D = 1024
NCH = 8
DEPTH = 4
FFN = 2816
NFC = 22
FGRP = 11
NEG = -30000.0
TWO_PI = 6.283185307179586

ENGS = ["pe", "act", "dve", "pool", "sp"]
RECENT = 4
LA_ATT = 4
EMBED = True
PUSHBACK = False

C_ID = 0
C_TRI = 128
C_TRS = 256
C_MASK = 384
C_ONES = 896
C_INV = 1024
C_SGN = 1025
C_EPS = 1026
C_2PI = 1027
C_ONE = 1028
C_PERM = 1032
NCONST = 1160


class MK:
    def __init__(self, nc, semstack):
        self.nc = nc
        self.semstack = semstack
        self.streams = {e: [] for e in ENGS}
        self.nops = {e: 0 for e in ENGS}
        self.know = {e: {} for e in ENGS}
        self.pushable = set()
        self.tokclock = {}
        self.lastw = {}
        self.readers = {}
        self.dma_issued = {}
        self.sems = {}
        self.n_waits = 0
        self.n_instr = 0

    def _sem(self, key):
        s = self.sems.get(key)
        if s is None:
            s = self.semstack.enter_context(self.nc.semaphore("s%d" % len(self.sems)))
            self.sems[key] = s
        return s

    def _deps(self, reads, writes, rhs=()):
        deps = set()
        nonbank = set()
        lw = self.lastw
        for k in reads:
            t = lw.get(k)
            if t is not None:
                deps.add(t)
                nonbank.add(t)
        for k in rhs:
            t = lw.get(k)
            if t is not None:
                deps.add(t)
        for k in writes:
            isbank = isinstance(k, tuple) and k[0] == "ps"
            t = lw.get(k)
            if t is not None:
                deps.add(t)
                if not isbank:
                    nonbank.add(t)
            r = self.readers.get(k)
            if r:
                deps.update(r)
                if not isbank:
                    nonbank.update(r)
        self._nonbank = nonbank
        return deps

    def _reduce(self, eng, deps, noself=False):
        need = {}
        nb = getattr(self, "_nonbank", ())
        nbsem = set()
        for tok in deps:
            (semkey, val) = tok
            if semkey[0] == "dma":
                val = self.dma_issued[semkey[1]]
            elif semkey[1] == eng:
                if eng == "pe" or noself:
                    continue
                if self.nops[eng] - (val - 1) > RECENT:
                    continue
            if tok in nb:
                nbsem.add(semkey)
            if need.get(semkey, 0) < val:
                need[semkey] = val
        self._nbsem = nbsem
        k = self.know[eng]
        tc = self.tokclock
        items = [(sk, v) for sk, v in need.items() if k.get(sk, 0) < v]
        if len(items) > 1:
            items.sort(key=lambda kv: -sum(tc.get(kv, {}).values()))
        out = []
        for semkey, val in items:
            if k.get(semkey, 0) >= val:
                continue
            out.append((semkey, val))
            k[semkey] = val
            clk = tc.get((semkey, val))
            if clk:
                for s2, v2 in clk.items():
                    if k.get(s2, 0) < v2:
                        k[s2] = v2
        self.n_waits += len(out)
        return out

    def _commit(self, tok, reads, writes):
        rd = self.readers
        for k in reads:
            l = rd.get(k)
            if l is None:
                rd[k] = [tok]
            else:
                l.append(tok)
        for k in writes:
            self.lastw[k] = tok
            rd[k] = []

    def op(self, eng, fn, reads=(), writes=(), noself=False, rhs=()):
        waits = self._reduce(eng, self._deps(reads, writes, rhs), noself)
        emb = None
        if EMBED and waits:
            cand = list(waits)
            if cand:
                emb = cand[-1]
                waits = [w for w in waits if w is not emb]
        n = self.nops[eng]
        semkey = ("eng", eng)
        tok = (semkey, n + 1)
        clk = dict(self.know[eng])
        clk[semkey] = n + 1
        self.tokclock[tok] = clk
        if PUSHBACK and eng == "pe" and waits:
            for w in waits:
                if w[0] not in self._nbsem:
                    self.pushable.add((eng, n + 1, w))
        self.streams[eng].append((waits, fn, (semkey, 1), emb, n + 1))
        self.nops[eng] = n + 1
        self._commit(tok, tuple(reads) + tuple(rhs), writes)
        self.n_instr += 1
        return tok

    def dma(self, eng, fn, slot, reads=(), writes=()):
        waits = self._reduce(eng, self._deps(reads, writes))
        semkey = ("dma", slot)
        tot = self.dma_issued.get(slot, 0) + 16
        self.dma_issued[slot] = tot
        tok = (semkey, tot)
        clk = dict(self.know[eng])
        clk[semkey] = tot
        self.tokclock[tok] = clk
        self.streams[eng].append((waits, fn, (semkey, 16), None, None))
        self._commit(tok, reads, writes)
        self.n_instr += 1
        return tok

    def wait(self, eng, toks):
        waits = self._reduce(eng, set(toks))
        if waits:
            self.streams[eng].append((waits, None, None, None, None))

    def barrier(self):
        toks = [(("eng", e), self.nops[e]) for e in ENGS if self.nops[e] > 0]
        toks += [(("dma", s), v) for s, v in self.dma_issued.items()]
        for e in ENGS:
            self.wait(e, [t for t in toks if t[0] != ("eng", e)])

    def _pushback(self, name, st):
        me = ("eng", name)
        tc = self.tokclock
        for i in range(1, len(st)):
            waits, fn, inc, emb, idx = st[i]
            if not waits:
                continue
            keep = []
            for w in waits:
                clk = tc.get(w)
                placed = False
                if clk is not None and (name, idx, w) in self.pushable:
                    lim = clk.get(me, 0)
                    for j in range(i - 1, max(i - 4, -1), -1):
                        wj, fnj, incj, embj, idxj = st[j]
                        if fnj is None or idxj is None:
                            continue
                        if idxj <= lim:
                            break
                        if embj is None:
                            st[j] = (wj, fnj, incj, w, idxj)
                            placed = True
                            break
                if not placed:
                    keep.append(w)
            st[i] = (keep, fn, inc, emb, idx)
            self.n_pushed = getattr(self, "n_pushed", 0) + len(waits) - len(keep)

    def emit_block(self):
        nc = self.nc
        streams = self.streams
        if PUSHBACK:
            for name in ENGS:
                self._pushback(name, streams[name])
        for e in ENGS:
            for waits, fn, inc, emb, _idx in streams[e]:
                for semkey, _ in waits:
                    self._sem(semkey)
                if emb is not None:
                    self._sem(emb[0])
                if inc is not None:
                    self._sem(inc[0])
        sems = self.sems
        with nc.Block() as block:
            def run(name, e):
                for waits, fn, inc, emb, _idx in streams[name]:
                    for semkey, val in waits:
                        e.wait_ge(sems[semkey], val)
                    if fn is not None:
                        ins = fn(e)
                        if emb is not None:
                            ins._wait_ge(sems[emb[0]], emb[1])
                        ins.then_inc(sems[inc[0]], inc[1])

            @block.tensor
            def _(e):
                run("pe", e)

            @block.scalar
            def _(e):
                run("act", e)

            @block.vector
            def _(e):
                run("dve", e)

            @block.gpsimd
            def _(e):
                run("pool", e)

            @block.sync
            def _(e):
                run("sp", e)
        self.streams = {e: [] for e in ENGS}


class Builder:
    def __init__(self, nseq, plan, final_norm=True):
        self.nseq = nseq
        self.plan = plan
        self.final_norm = final_norm
        nc = bass.Bass("TRN2", target_bir_lowering=False)
        self.nc = nc
        dt = nc.dram_tensor
        self.x = dt("x", [nseq, S, D], F32, kind="ExternalInput").ap()
        self.pos = dt("pos", [nseq, S], I32, kind="ExternalInput").ap()
        self.a_w_qkv = dt("a_w_qkv", [2, D, 4608], F32, kind="ExternalInput").ap()
        self.a_w_o = dt("a_w_o", [2, 512, D], F32, kind="ExternalInput").ap()
        self.b_w_in = dt("b_w_in", [2, D, 3088], F32, kind="ExternalInput").ap()
        self.b_w_o = dt("b_w_o", [2, D, D], F32, kind="ExternalInput").ap()
        self.f_w_in = dt("f_w_in", [4, D, 2 * FFN], F32, kind="ExternalInput").ap()
        self.f_w_down = dt("f_w_down", [4, FFN, D], F32, kind="ExternalInput").ap()
        self.cst = dt("cst", [128, NCONST], F32, kind="ExternalInput").ap()
        self.gains_d = dt("gains", [128, 72], F32, kind="ExternalInput").ap()
        self.convp_d = dt("convp", [128, 4 * 4 * NFC], F32, kind="ExternalInput").ap()
        self.gnorm_d = dt("gnorm", [128, 4], F32, kind="ExternalInput").ap()
        self.wgu_d = dt("wgu", [32, 2 * 512], F32, kind="ExternalInput").ap()
        self.y = dt("y", [nseq, S, D], F32, kind="ExternalOutput").ap()
        self.nb = 0

    def bank(self):
        b = self.nb % 8
        self.nb += 1
        return b

    def build(self):
        nc = self.nc
        with ExitStack() as outer:
            mk = MK(nc, outer)
            self.mk = mk
            T = lambda name, shape, dt_: outer.enter_context(nc.sbuf_tensor(name, shape, dt_))
            self.xT = T("xT", [128, NCH, S], F32)
            self.hT = T("hT", [128, NCH, S], BF16)
            self.cf = T("cf", [128, NCONST], F32)
            self.cb = T("cb", [128, NCONST], BF16)
            self.gains = T("gains_s", [128, 72], F32)
            self.convp = T("convp_s", [128, 4 * 4 * NFC], F32)
            self.gnorm = T("gnorm_s", [128, 4], F32)
            self.wgu = T("wgu_s", [32, 2 * 512], F32)
            self.ps = [outer.enter_context(nc.psum_tensor("ps%d" % i, [128, 512], F32)) for i in range(8)]
            self.nsq = [T("nsq%d" % i, [128, 512], BF16) for i in range(2)]
            self.nrs = [T("nrs%d" % i, [128, 512], F32) for i in range(2)]
            self.nq = 0
            mk.dma("sp", lambda e: e.dma_start(out=self.cf[:], in_=self.cst), slot="c0", writes=["cf"])
            mk.dma("pool", lambda e: e.dma_start(out=self.cb[:], in_=self.cst), slot="c1", writes=["cb"])
            mk.dma("sp", lambda e: e.dma_start(out=self.gains[:], in_=self.gains_d), slot="c2", writes=["gains"])
            mk.dma("sp", lambda e: e.dma_start(out=self.convp[:], in_=self.convp_d), slot="c3", writes=["convp"])
            mk.dma("sp", lambda e: e.dma_start(out=self.gnorm[:], in_=self.gnorm_d), slot="c4", writes=["gnorm"])
            mk.dma("sp", lambda e: e.dma_start(out=self.wgu[:], in_=self.wgu_d), slot="c5", writes=["wgu"])
            mk.emit_block()
            for pi, ph in enumerate(self.plan):
                kind = ph[0]
                nxt = self.plan[pi + 1] if pi + 1 < len(self.plan) else None
                self.next_phase = nxt
                self.was_prefetched = getattr(self, "prefetched", False)
                self.prefetched = False
                if nxt is None or nxt[0] in ("store", "load"):
                    self.next_goff = None
                elif nxt[0] == "F":
                    self.next_goff = 32 + nxt[2] * 8
                else:
                    self.next_goff = nxt[2] * 8
                with ExitStack() as st:
                    self.st = st
                    mk.barrier()
                    if kind == "load":
                        self.phase_load(ph[1])
                    elif kind == "A":
                        self.phase_A(ph[1], ph[2])
                    elif kind == "B":
                        self.phase_B(ph[1], ph[2])
                    elif kind == "F":
                        self.phase_F(ph[1], ph[2])
                    elif kind == "store":
                        self.phase_store(ph[1])
                    mk.emit_block()
        return nc

    def alloc_pf(self):
        self.pf = self.T("pf", [128, 4096], BF16)
        return self.pf

    def prefetch_next(self, cur_keys):
        nxt = self.next_phase
        self.prefetched = False
        if nxt is None or nxt[0] not in ("A", "B", "F"):
            return
        mk, pf = self.mk, self.pf
        l = nxt[2]
        if nxt[0] == "A":
            w3 = self.a_w_qkv[l // 2].rearrange("(c p) n -> p c n", p=128)
            dst = pf[:, 0:3072].rearrange("p (c n) -> p c n", c=NCH)
            for kind in range(3):
                col0 = kind * 1536 + 2 * 512 + 0 * 128
                mk.dma("pool", lambda e, kind=kind, col0=col0: e.dma_start(out=dst[:, :, kind * 128:(kind + 1) * 128], in_=w3[:, :, col0:col0 + 128]),
                       slot=("wA", 0), writes=[("wA", 0)] + cur_keys)
        elif nxt[0] == "F":
            w3 = self.f_w_in[l].rearrange("(c p) n -> p c n", p=128)
            dst = pf[:, 0:2048].rearrange("p (c n) -> p c n", c=NCH)
            mk.dma("pool", lambda e: e.dma_start(out=dst[:, :, 0:128], in_=w3[:, :, 0:128]), slot=("win", 0), writes=[("win", 0)] + cur_keys)
            mk.dma("pool", lambda e: e.dma_start(out=dst[:, :, 128:256], in_=w3[:, :, FFN:FFN + 128]), slot=("win", 0), writes=[("win", 0)] + cur_keys)
        else:
            w3 = self.b_w_in[l // 2].rearrange("(c p) n -> p c n", p=128)
            dq = pf[:, 0:1024].rearrange("p (c n) -> p c n", c=NCH)
            dk = pf[:, 1024:2048].rearrange("p (c n) -> p c n", c=NCH)
            dv = pf[:, 2048:4096].rearrange("p (c n) -> p c n", c=NCH)
            mk.dma("pool", lambda e: e.dma_start(out=dq, in_=w3[:, :, 0:128]), slot=("wq", 0), writes=[("wq", 0)] + cur_keys)
            mk.dma("pool", lambda e: e.dma_start(out=dk, in_=w3[:, :, 512:640]), slot=("wk", 0), writes=[("wk", 0)] + cur_keys)
            mk.dma("pool", lambda e: e.dma_start(out=dv, in_=w3[:, :, 1024:1280]), slot="wv", writes=["wv"] + cur_keys)
        self.prefetched = True

    def T(self, name, shape, dt_):
        self.tcount = getattr(self, "tcount", 0) + 1
        return self.st.enter_context(self.nc.sbuf_tensor("%s_%d" % (name, self.tcount), shape, dt_))

    def phase_load(self, s):
        mk, ps, xT, cf = self.mk, self.ps, self.xT, self.cf
        xs = [self.T("xs%d" % i, [128, D], F32) for i in range(3)]
        for t16 in range(16):
            sl = t16 % 3
            mk.dma("sp", lambda e, t16=t16, sl=sl: e.dma_start(out=xs[sl][:], in_=self.x[s, t16 * 128:(t16 + 1) * 128, :]),
                   slot=("xs", sl), writes=[("xs", sl)])
            tt = t16 // 4
            for half in range(2):
                b = self.bank()
                for j in range(4):
                    c = half * 4 + j
                    mk.op("pe", lambda e, b=b, j=j, c=c, sl=sl: e.transpose(out=ps[b][:, j * 128:(j + 1) * 128], in_=xs[sl][:, c * 128:(c + 1) * 128], identity=cf[:, C_ID:C_ID + 128]),
                          reads=[("xs", sl), "cf"], writes=[("ps", b)])
                mk.op("act", lambda e, b=b, half=half, t16=t16: e.activation(
                    out=xT[:, half * 4:(half + 1) * 4, t16 * 128:(t16 + 1) * 128],
                    in_=ps[b][:, :].rearrange("p (a b) -> p a b", a=4), func=AF.Copy),
                    writes=[("ps", b)] + [("x", half * 4 + j, tt) for j in range(4)])
            if t16 % 4 == 3:
                self.norm_tile(tt)

    def norm_tile(self, tt):
        self.norm_S(tt)
        self.norm_H(tt)

    def defer_norm(self, stages):
        goff = self.next_goff
        if goff is None:
            return
        self.pending_norm = [(k, tt, goff) for (k, tt) in stages]

    def flush_pending(self):
        pend = getattr(self, "pending_norm", [])
        self.pending_norm = []
        for k, tt, goff in pend:
            if k == "S":
                self.norm_S(tt, goff)
            else:
                self.norm_H(tt, goff)

    def norm_S(self, tt, goff=-1):
        if goff == -1:
            goff = self.next_goff
        if goff is None:
            return
        mk, ps, xT, cf, cb = self.mk, self.ps, self.xT, self.cf, self.cb
        sq, rs = self.nsq, self.nrs
        tsl = slice(tt * 512, (tt + 1) * 512)
        b = self.bank()
        for c in range(NCH):
            q = self.nq % 2
            self.nq += 1
            mk.op("act", lambda e, c=c, q=q, tsl=tsl: e.activation(out=sq[q][:], in_=xT[:, c, tsl], func=AF.Square),
                  reads=[("x", c, tt)], writes=[("nsq", q)])
            mk.op("pe", lambda e, b=b, c=c, q=q: e.matmul(ps[b][:, :], lhsT=cb[:, C_ONES:C_ONES + 128], rhs=sq[q][:], start=(c == 0), stop=(c == NCH - 1)),
                  reads=["cb"], rhs=[("nsq", q)], writes=[("ps", b)])
        r = tt % 2
        mk.op("act", lambda e, b=b, r=r: e.activation(out=rs[r][:], in_=ps[b][:, :], func=AF.Ln, scale=1.0 / D, bias=cf[:, C_EPS:C_EPS + 1]),
              reads=["cf"], writes=[("ps", b), ("nrs", r)])
        mk.op("act", lambda e, r=r: e.activation(out=rs[r][:], in_=rs[r][:], func=AF.Exp, scale=-0.5), writes=[("nrs", r)])

    def norm_H(self, tt, goff=-1):
        if goff == -1:
            goff = self.next_goff
        if goff is None:
            return
        mk, xT, hT, gains, rs = self.mk, self.xT, self.hT, self.gains, self.nrs
        tsl = slice(tt * 512, (tt + 1) * 512)
        r = tt % 2
        for c in range(NCH):
            mk.op("dve", lambda e, c=c, r=r, tsl=tsl: e.scalar_tensor_tensor(
                out=hT[:, c, tsl], in0=xT[:, c, tsl], scalar=gains[:, goff + c:goff + c + 1], in1=rs[r][:], op0=ALU.mult, op1=ALU.mult),
                reads=[("x", c, tt), ("nrs", r), "gains"], writes=[("hT", c, tt)])

    def hT_reads(self, c):
        return [("hT", c, t) for t in range(4)]

    def phase_A(self, s, l):
        j = l // 2
        mk, ps, xT, hT, cf, cb = self.mk, self.ps, self.xT, self.hT, self.cf, self.cb
        T = self.T
        pf = self.alloc_pf()
        cosT = T("cosT", [128, S], F32)
        sinT = T("sinT", [128, S], F32)
        tA = [T("tA%d" % i, [128, 512], F32) for i in range(2)]
        tB = [T("tB%d" % i, [128, 512], F32) for i in range(2)]
        qd = T("qd", [128, S], BF16)
        kdA = T("kdA", [128, S], BF16)
        kdB = T("kdB", [128, S], BF16)
        vA = T("vA", [128, 16, 128], BF16)
        vB = T("vB", [128, 16, 128], BF16)
        accA = T("accA", [128, S], F32)
        accB = T("accB", [128, S], F32)
        mT = T("mT", [128, 4, S], BF16)
        PT = [T("PT%d" % i, [128, 512], BF16) for i in range(5)]
        ti = mT[:, 0, 0:1024].bitcast(I32)
        wsl = [pf[:, 0:3072].rearrange("p (c n) -> p c n", c=NCH)]
        qb = [T("qb%d" % i, [128, 512], BF16) for i in range(2)]
        wo = [T("woA%d" % i, [128, 4, 128], BF16) for i in range(2)]

        wv3 = self.a_w_qkv[j].rearrange("(c p) n -> p c n", p=128)
        ws = wsl[0]
        wkey = ("wA", 0)

        def load_wA(p_, g_):
            for kind in range(3):
                col0 = kind * 1536 + g_ * 512 + p_ * 128
                mk.dma("pool", lambda e, kind=kind, col0=col0: e.dma_start(out=ws[:, :, kind * 128:(kind + 1) * 128], in_=wv3[:, :, col0:col0 + 128]),
                       slot=wkey, writes=[wkey])
        if not self.was_prefetched:
            load_wA(0, 2)
        self.flush_pending()
        def gen_table(tt):
            tsl = slice(tt * 512, (tt + 1) * 512)
            mk.dma("sp", lambda e, tsl=tsl: e.dma_start(out=ti, in_=self.pos[s, tsl].partition_broadcast(128)),
                   slot="ti", writes=["ti"])
            u, kf = accA[:, 0:512], accB[:, 0:512]
            mk.op("dve", lambda e: e.tensor_copy(out=u, in_=ti), reads=["ti"], writes=[("accA", 0)])
            mk.op("dve", lambda e: e.tensor_scalar(out=u, in0=u, scalar1=cf[:, C_INV:C_INV + 1], scalar2=None, op0=ALU.mult),
                  reads=["cf"], writes=[("accA", 0)])
            for which in range(2):
                if which == 1:
                    mk.op("dve", lambda e: e.tensor_scalar(out=u, in0=u, scalar1=0.25, scalar2=None, op0=ALU.add), writes=[("accA", 0)])
                mk.op("dve", lambda e: e.tensor_copy(out=ti, in_=u), reads=[("accA", 0)], writes=["ti"])
                mk.op("dve", lambda e: e.tensor_copy(out=kf, in_=ti), reads=["ti"], writes=[("accB", 0)])
                mk.op("dve", lambda e: e.tensor_tensor(out=kf, in0=u, in1=kf, op=ALU.subtract), reads=[("accA", 0)], writes=[("accB", 0)])
                if which == 0:
                    mk.op("act", lambda e, tsl=tsl: e.activation(out=sinT[:, tsl], in_=kf, func=AF.Sin, scale=cf[:, C_SGN:C_SGN + 1]),
                          reads=[("accB", 0), "cf"], writes=[("sinT", tt)])
                else:
                    mk.op("act", lambda e, tsl=tsl: e.activation(out=cosT[:, tsl], in_=kf, func=AF.Sin, scale=cf[:, C_2PI:C_2PI + 1]),
                          reads=[("accB", 0), "cf"], writes=[("cosT", tt)])
        gen_table(0)
        mk.op("pool", lambda e: e.memset(kdA[64:128, :], 0.0), writes=[("kd", t_) for t_ in range(4)])
        mk.op("pool", lambda e: e.memset(kdB[0:64, :], 0.0), writes=[("kd", t_) for t_ in range(4)])
        mk.op("pool", lambda e: e.memset(vA[:, :, 64:128], 1.0), writes=["v"])
        mk.op("pool", lambda e: e.memset(vB[:, :, 0:64], 1.0), writes=["v"])


        def normalise_tile(p, tt):
            if True:
                tsl = slice(tt * 512, (tt + 1) * 512)
                r = tt % 2
                mk.op("act", lambda e, r=r, tsl=tsl: e.activation(out=tA[r][0:64, :], in_=accA[64:128, tsl], func=AF.Ln), reads=[("accA", tt)], writes=[("tA", r)])
                mk.op("act", lambda e, r=r: e.activation(out=tA[r][0:64, :], in_=tA[r][0:64, :], func=AF.Exp, scale=-1.0), writes=[("tA", r)])
                mk.op("dve", lambda e, r=r, tsl=tsl, p=p: e.tensor_tensor(out=mT[0:64, p, tsl], in0=accA[0:64, tsl], in1=tA[r][0:64, :], op=ALU.mult),
                      reads=[("accA", tt), ("tA", r)], writes=[("mT", p, tt)])
                mk.op("act", lambda e, r=r, tsl=tsl: e.activation(out=tB[r][64:128, :], in_=accB[0:64, tsl], func=AF.Ln), reads=[("accB", tt)], writes=[("tB", r)])
                mk.op("act", lambda e, r=r: e.activation(out=tB[r][64:128, :], in_=tB[r][64:128, :], func=AF.Exp, scale=-1.0), writes=[("tB", r)])
                mk.op("dve", lambda e, r=r, tsl=tsl, p=p: e.tensor_tensor(out=mT[64:128, p, tsl], in0=accB[64:128, tsl], in1=tB[r][64:128, :], op=ALU.mult),
                      reads=[("accB", tt), ("tB", r)], writes=[("mT", p, tt)])
        for p in range(4):
            for gi in range(3):
                g = 2 - gi
                dil = (1, 4, 16)[g]
                pending = []

                def post_tile(kind, tt, r):
                    tsl = slice(tt * 512, (tt + 1) * 512)
                    A, B = tA[r], tB[r]
                    b2 = self.bank()
                    mk.op("pe", lambda e, b2=b2, r=r: e.matmul(ps[b2][:, :], lhsT=cb[:, C_PERM:C_PERM + 128], rhs=qb[r][:], start=True, stop=True),
                          reads=["cb"], rhs=[("qb", r)], writes=[("ps", b2)])
                    mk.op("pool", lambda e, A=A, r=r, tsl=tsl: e.tensor_tensor(out=A[:], in0=qb[r][:], in1=cosT[:, tsl], op=ALU.mult),
                          reads=[("cosT", tt), ("qb", r)], writes=[("tA", r)])
                    mk.op("dve", lambda e, b2=b2, B=B, tsl=tsl: e.tensor_tensor(out=B[:], in0=ps[b2][:, :], in1=sinT[:, tsl], op=ALU.mult),
                          reads=[("sinT", tt)], writes=[("ps", b2), ("tB", r)])

                    def views(dst, Asrc, Bsrc, prt):
                        if dil == 1:
                            return dst[prt, tsl], Asrc[prt, :], Bsrc[prt, :]
                        n_r = dil
                        m = 512 // dil
                        o = dst[prt, :].rearrange("p (r m) -> p r m", r=n_r)[:, :, tt * m:(tt + 1) * m]
                        a = Asrc[prt, :].rearrange("p (m r) -> p r m", r=n_r)
                        bb = Bsrc[prt, :].rearrange("p (m r) -> p r m", r=n_r)
                        return o, a, bb
                    if kind == 0:
                        o, a, bb = views(qd, A, B, slice(0, 128))
                        mk.op("dve", lambda e, o=o, a=a, bb=bb: e.tensor_tensor(out=o, in0=a, in1=bb, op=ALU.add),
                              reads=[("tA", r), ("tB", r)], writes=[("qd", tt)])
                        if p == 0 and gi == 0 and tt + 1 < 4:
                            gen_table(tt + 1)
                    else:
                        for (dst, prt) in ((kdA, slice(0, 64)), (kdB, slice(64, 128))):
                            o, a, bb = views(dst, A, B, prt)
                            mk.op("dve", lambda e, o=o, a=a, bb=bb: e.tensor_tensor(out=o, in0=a, in1=bb, op=ALU.add),
                                  reads=[("tA", r), ("tB", r)], writes=[("kd", tt)])

                for kind in range(2):
                    for tt in range(4):
                        tsl = slice(tt * 512, (tt + 1) * 512)
                        b = self.bank()
                        for c in range(NCH):
                            mk.op("pe", lambda e, b=b, c=c, ws=ws, kind=kind, tsl=tsl: e.matmul(
                                ps[b][:, :], lhsT=ws[:, c, kind * 128:(kind + 1) * 128], rhs=hT[:, c, tsl], start=(c == 0), stop=(c == NCH - 1)),
                                reads=[wkey, ("hT", c, tt)], writes=[("ps", b)])
                        r = (kind * 4 + tt) % 2
                        mk.op("act", lambda e, b=b, r=r: e.activation(out=qb[r][:], in_=ps[b][:, :], func=AF.Copy),
                              writes=[("ps", b), ("qb", r)])
                        if pending:
                            post_tile(*pending.pop())
                        pending.append((kind, tt, r))
                def tok_slice(blk):
                    if dil == 1:
                        return slice(blk * 128, (blk + 1) * 128)
                    if dil == 4:
                        r_, n_ = blk // 4, blk % 4
                        st_ = r_ + 512 * n_
                        return slice(st_, st_ + 4 * 127 + 1, 4)
                    return slice(blk, S, 16)
                for b4 in range(4):
                    bk = self.bank()
                    for bi in range(4):
                        blk = b4 * 4 + bi
                        tk = tok_slice(blk)
                        for c in range(NCH):
                            mk.op("pe", lambda e, bk=bk, bi=bi, c=c, tk=tk, ws=ws: e.matmul(
                                ps[bk][:, bi * 128:(bi + 1) * 128], lhsT=hT[:, c, tk], rhs=ws[:, c, 256:384], start=(c == 0), stop=(c == NCH - 1)),
                                reads=[wkey] + self.hT_reads(c), writes=[("ps", bk)])
                    if pending:
                        post_tile(*pending.pop())
                    pv = ps[bk][:, :].rearrange("p (a b) -> p a b", a=4)
                    mk.op("act", lambda e, pv=pv, b4=b4: e.activation(out=vA[:, b4 * 4:(b4 + 1) * 4, 0:64], in_=pv[:, :, 0:64], func=AF.Copy),
                          writes=[("ps", bk), "v"])
                    mk.op("act", lambda e, pv=pv, b4=b4: e.activation(out=vB[:, b4 * 4:(b4 + 1) * 4, 64:128], in_=pv[:, :, 64:128], func=AF.Copy),
                          writes=[("ps", bk), "v"])
                nxt = p * 3 + gi + 1
                if nxt < 12:
                    load_wA(nxt // 3, 2 - (nxt % 3))
                else:
                    self.prefetch_next([("wA", 0)])
                LA = LA_ATT
                ctx = {}
                obanks = {}

                def tiles_of(blk):
                    if dil == 1:
                        return [blk // 4]
                    if dil == 4:
                        return [blk % 4]
                    return [0, 1, 2, 3]

                def qk_stage(blk):
                    n_ = blk if dil == 1 else (blk % 4 if dil == 4 else 0)
                    hp = n_ > 0
                    W = 512 if hp else 256
                    bs = self.bank()
                    bsl = slice(blk * 128, (blk + 1) * 128)
                    psl = slice((blk - 1) * 128, blk * 128)
                    mk.op("pe", lambda e, bs=bs, W=W: e.matmul(ps[bs][:, 0:W], lhsT=cb[:, C_ID:C_ID + 128], rhs=cb[:, C_MASK:C_MASK + W], start=True, stop=False),
                          reads=["cb"], writes=[("ps", bs)])
                    for h, kd in ((0, kdA), (1, kdB)):
                        mk.op("pe", lambda e, bs=bs, h=h, kd=kd, bsl=bsl: e.matmul(
                            ps[bs][:, h * 128:(h + 1) * 128], lhsT=kd[:, bsl], rhs=qd[:, bsl], start=False, stop=True, skip_group_check=True),
                            reads=[("kd", t_) for t_ in tiles_of(blk)], rhs=[("qd", t_) for t_ in tiles_of(blk)], writes=[("ps", bs)])
                        if hp:
                            mk.op("pe", lambda e, bs=bs, h=h, kd=kd, bsl=bsl, psl=psl: e.matmul(
                                ps[bs][:, 256 + h * 128:256 + (h + 1) * 128], lhsT=kd[:, psl], rhs=qd[:, bsl], start=False, stop=True, skip_group_check=True),
                                reads=[("kd", t_) for t_ in tiles_of(blk - 1)], rhs=[("qd", t_) for t_ in tiles_of(blk)], writes=[("ps", bs)])
                    pq = blk % 5
                    mk.op("act", lambda e, bs=bs, W=W, pq=pq: e.activation(out=PT[pq][:, 0:W], in_=ps[bs][:, 0:W], func=AF.Exp, scale=0.125),
                          writes=[("ps", bs), ("PT", pq)])
                    ctx[blk] = (hp, pq)

                def pv_stage(blk):
                    hp, pq = ctx[blk]
                    q4 = blk % 4
                    if q4 == 0:
                        obanks[0], obanks[1] = self.bank(), self.bank()
                    oa, ob = obanks[0], obanks[1]
                    osl = slice(q4 * 128, (q4 + 1) * 128)
                    for ob_, vv, c0 in ((oa, vA, 0), (ob, vB, 128)):
                        mk.op("pe", lambda e, ob_=ob_, vv=vv, c0=c0, osl=osl, blk=blk, pq=pq, hp=hp: e.matmul(
                            ps[ob_][:, osl], lhsT=vv[:, blk, :], rhs=PT[pq][:, c0:c0 + 128], start=True, stop=(not hp)),
                            reads=["v"], rhs=[("PT", pq)], writes=[("ps", ob_)])
                        if hp:
                            mk.op("pe", lambda e, ob_=ob_, vv=vv, c0=c0, osl=osl, blk=blk, pq=pq: e.matmul(
                                ps[ob_][:, osl], lhsT=vv[:, blk - 1, :], rhs=PT[pq][:, 256 + c0:256 + c0 + 128], start=False, stop=True),
                                reads=["v"], rhs=[("PT", pq)], writes=[("ps", ob_)])
                    if q4 == 3:
                        b0 = blk - 3
                        for ob_, acc, akey in ((oa, accA, "accA"), (ob, accB, "accB")):
                            if dil == 1:
                                o = acc[:, b0 * 128:b0 * 128 + 512]
                                src = ps[ob_][:, :]
                            elif dil == 4:
                                o = acc[:, slice(b0 // 4, S, 4)]
                                src = ps[ob_][:, :]
                            else:
                                o = acc[:, :].rearrange("p (m r) -> p r m", r=16)[:, b0:b0 + 4, :]
                                src = ps[ob_][:, :].rearrange("p (a b) -> p a b", a=4)
                            akeys = [(akey, b0 // 4)] if dil == 1 else [(akey, t) for t in range(4)]
                            if gi == 0:
                                mk.op("act", lambda e, o=o, src=src: e.activation(out=o, in_=src, func=AF.Copy),
                                      writes=[("ps", ob_)] + akeys)
                            else:
                                mk.op("dve", lambda e, o=o, src=src: e.tensor_tensor(out=o, in0=src, in1=o, op=ALU.add),
                                      writes=[("ps", ob_)] + akeys)
                        if gi == 2:
                            normalise_tile(p, b0 // 4)

                for step in range(16 + LA):
                    if step < 16:
                        qk_stage(step)
                    if step >= LA:
                        pv_stage(step - LA)
        wo3 = self.a_w_o[j].rearrange("(c p) n -> p c n", p=128)
        it = 0
        for half in range(2):
            for dc in range(NCH):
                w = wo[it % 2]
                wk = ("woA", it % 2)
                it += 1
                mk.dma("pool", lambda e, w=w, dc=dc: e.dma_start(out=w[:], in_=wo3[:, :, dc * 128:(dc + 1) * 128]), slot=wk, writes=[wk])
                for tt in (2 * half, 2 * half + 1):
                    tsl = slice(tt * 512, (tt + 1) * 512)
                    b = self.bank()
                    for kc in range(4):
                        mk.op("pe", lambda e, b=b, kc=kc, w=w, tsl=tsl: e.matmul(ps[b][:, :], lhsT=w[:, kc, :], rhs=mT[:, kc, tsl], start=(kc == 0), stop=(kc == 3)),
                              reads=[wk], rhs=[("mT", kc, tt)], writes=[("ps", b)])
                    mk.op("dve", lambda e, b=b, dc=dc, tsl=tsl: e.tensor_tensor(out=xT[:, dc, tsl], in0=ps[b][:, :], in1=xT[:, dc, tsl], op=ALU.add),
                          writes=[("ps", b), ("x", dc, tt)])
                if half == 1:
                    if dc == 0:
                        self.norm_S(0)
                    elif dc == 2:
                        self.norm_S(1)
                    elif dc == 4:
                        self.norm_H(0)
                    elif dc == 6:
                        self.norm_H(1)
        self.defer_norm([("S", 2), ("S", 3), ("H", 2), ("H", 3)])

    def phase_B(self, s, l):
        jb = l // 2
        mk, ps, xT, hT, cf, cb = self.mk, self.ps, self.xT, self.hT, self.cf, self.cb
        gnorm, wgu = self.gnorm, self.wgu
        T = self.T
        pf = self.alloc_pf()
        gd = T("gd", [32, S], F32)
        Lt = T("Lt", [128, 512], F32)
        EB = T("EB", [128, 512], F32)
        EBi = T("EBi", [128, 512], F32)
        Er = T("Er", [128, 512], F32)
        dec = T("dec", [128, 16], F32)
        qi = T("qi", [128, S], BF16)
        kinv = T("kinv", [128, S], BF16)
        ks = T("ks", [128, 16, 128], BF16)
        vv = T("vv", [128, 16, 256], BF16)
        oTh = T("oTh", [128, 2, S], BF16)
        S32 = T("S32", [128, 256], F32)
        Sbf = [T("Sbf%d" % i, [128, 256], BF16) for i in range(2)]
        aTm = [T("aTm%d" % i, [128, 128], BF16) for i in range(4)]
        sq, rs = self.nsq, self.nrs
        sr = [T("sr%d" % i, [128, 512], BF16) for i in range(2)]
        t1 = [T("t1%d" % i, [128, 512], F32) for i in range(2)]
        yT = T("yT", [128, 4, S], BF16)
        wq = [pf[:, 0:1024].rearrange("p (c n) -> p c n", c=NCH), T("wq1", [128, NCH, 128], BF16)[:]]
        wk_ = [pf[:, 1024:2048].rearrange("p (c n) -> p c n", c=NCH), T("wk1", [128, NCH, 128], BF16)[:]]
        wv = pf[:, 2048:4096].rearrange("p (c n) -> p c n", c=NCH)
        wr = [T("wr%d" % i, [128, NCH, 128], BF16) for i in range(2)]
        wg = T("wg", [128, NCH, 16], BF16)
        wo = [T("woB%d" % i, [128, 4, 128], BF16) for i in range(2)]

        win3 = self.b_w_in[jb].rearrange("(c p) n -> p c n", p=128)
        wo3 = self.b_w_o[jb].rearrange("(c p) n -> p c n", p=128)
        mk.dma("pool", lambda e: e.dma_start(out=wg[:], in_=win3[:, :, 3072:3088]), slot="wg", writes=["wg"])
        if not self.was_prefetched:
            mk.dma("pool", lambda e: e.dma_start(out=wq[0], in_=win3[:, :, 0:128]), slot=("wq", 0), writes=[("wq", 0)])
            mk.dma("pool", lambda e: e.dma_start(out=wk_[0], in_=win3[:, :, 512:640]), slot=("wk", 0), writes=[("wk", 0)])
            mk.dma("pool", lambda e: e.dma_start(out=wv, in_=win3[:, :, 1024:1280]), slot="wv", writes=["wv"])
        self.flush_pending()
        mk.op("pool", lambda e: e.memset(gd[:, :], 1.0), writes=["gd"])
        for tt in range(4):
            tsl = slice(tt * 512, (tt + 1) * 512)
            b = self.bank()
            for c in range(NCH):
                mk.op("pe", lambda e, b=b, c=c, tsl=tsl: e.matmul(ps[b][0:16, :], lhsT=wg[:, c, :], rhs=hT[:, c, tsl], start=(c == 0), stop=(c == NCH - 1)),
                      reads=["wg", ("hT", c, tt)], writes=[("ps", b)])
            mk.op("act", lambda e, b=b, tsl=tsl: e.activation(out=gd[0:16, tsl], in_=ps[b][0:16, :], func=AF.Copy), writes=[("ps", b), "gd"])

        for h in range(4):
            hs = h % 2
            if h > 0:
                mk.dma("pool", lambda e, h=h, hs=hs: e.dma_start(out=wq[hs], in_=win3[:, :, h * 128:(h + 1) * 128]), slot=("wq", hs), writes=[("wq", hs)])
                mk.dma("pool", lambda e, h=h, hs=hs: e.dma_start(out=wk_[hs], in_=win3[:, :, 512 + h * 128:512 + (h + 1) * 128]), slot=("wk", hs), writes=[("wk", hs)])
                mk.dma("pool", lambda e, h=h: e.dma_start(out=wv, in_=win3[:, :, 1024 + h * 256:1024 + (h + 1) * 256]), slot="wv", writes=["wv"])
            for tt in range(4):
                tsl = slice(tt * 512, (tt + 1) * 512)
                bl = self.bank()
                for j4 in range(4):
                    t16 = tt * 4 + j4
                    mk.op("pe", lambda e, bl=bl, j4=j4, t16=t16, h=h: e.matmul(
                        ps[bl][:, j4 * 128:(j4 + 1) * 128], lhsT=gd[0:17, t16 * 128:(t16 + 1) * 128],
                        rhs=wgu[0:17, jb * 512 + h * 128:jb * 512 + (h + 1) * 128], start=True, stop=True),
                        reads=["gd", "wgu"], writes=[("ps", bl)])
                mk.op("act", lambda e, bl=bl: e.activation(out=Lt[:], in_=ps[bl][:, :], func=AF.Exp, scale=-1.0), writes=[("ps", bl), "Lt"])
                mk.op("act", lambda e: e.activation(out=Lt[:], in_=Lt[:], func=AF.Ln, bias=cf[:, C_ONE:C_ONE + 1]), reads=["cf"], writes=["Lt"])
                bq = self.bank()
                for c in range(NCH):
                    mk.op("pe", lambda e, bq=bq, c=c, hs=hs, tsl=tsl: e.matmul(ps[bq][:, :], lhsT=wq[hs][:, c, :], rhs=hT[:, c, tsl], start=(c == 0), stop=(c == NCH - 1)),
                          reads=[("wq", hs), ("hT", c, tt)], writes=[("ps", bq)])
                bk = self.bank()
                for c in range(NCH):
                    mk.op("pe", lambda e, bk=bk, c=c, hs=hs, tsl=tsl: e.matmul(ps[bk][:, :], lhsT=wk_[hs][:, c, :], rhs=hT[:, c, tsl], start=(c == 0), stop=(c == NCH - 1)),
                          reads=[("wk", hs), ("hT", c, tt)], writes=[("ps", bk)])
                bc, br = self.bank(), self.bank()
                for j4 in range(4):
                    jsl = slice(j4 * 128, (j4 + 1) * 128)
                    mk.op("pe", lambda e, bc=bc, jsl=jsl: e.matmul(ps[bc][:, jsl], lhsT=Lt[:, jsl], rhs=cf[:, C_TRI:C_TRI + 128], start=True, stop=True),
                          reads=["Lt", "cf"], writes=[("ps", bc)])
                    mk.op("pe", lambda e, br=br, jsl=jsl: e.matmul(ps[br][:, jsl], lhsT=cf[:, C_TRS:C_TRS + 128], rhs=Lt[:, jsl], start=True, stop=True),
                          reads=["cf"], rhs=["Lt"], writes=[("ps", br)])
                mk.op("act", lambda e, bc=bc: e.activation(out=EB[:], in_=ps[bc][:, :], func=AF.Exp, scale=-1.0 / 16), writes=[("ps", bc), "EB"])
                mk.op("act", lambda e, bc=bc: e.activation(out=EBi[:], in_=ps[bc][:, :], func=AF.Exp, scale=1.0 / 16), writes=[("ps", bc), "EBi"])
                mk.op("act", lambda e, br=br: e.activation(out=Er[:], in_=ps[br][:, :], func=AF.Exp, scale=-1.0 / 16), writes=[("ps", br), "Er"])
                mk.op("dve", lambda e, tt=tt: e.tensor_copy(out=dec[:, tt * 4:(tt + 1) * 4], in_=EB[:, slice(127, 512, 128)]), reads=["EB"], writes=["dec"])
                mk.op("dve", lambda e, bq=bq, tsl=tsl: e.scalar_tensor_tensor(out=qi[:, tsl], in0=ps[bq][:, :], scalar=128.0 ** -0.5, in1=EB[:], op0=ALU.mult, op1=ALU.mult),
                      reads=["EB"], writes=[("ps", bq), "qi"])
                mk.op("dve", lambda e, bk=bk, tsl=tsl: e.tensor_tensor(out=kinv[:, tsl], in0=ps[bk][:, :], in1=EBi[:], op=ALU.mult),
                      reads=["EBi"], writes=[("ps", bk), "kinv"])
                bkt = self.bank()
                for j4 in range(4):
                    t16 = tt * 4 + j4
                    for c in range(NCH):
                        mk.op("pe", lambda e, bkt=bkt, j4=j4, t16=t16, c=c, hs=hs: e.matmul(
                            ps[bkt][:, j4 * 128:(j4 + 1) * 128], lhsT=hT[:, c, t16 * 128:(t16 + 1) * 128], rhs=wk_[hs][:, c, :], start=(c == 0), stop=(c == NCH - 1)),
                            reads=[("wk", hs), ("hT", c, tt)], writes=[("ps", bkt)])
                mk.op("dve", lambda e, bkt=bkt, tt=tt: e.tensor_tensor(
                    out=ks[:, tt * 4:(tt + 1) * 4, :], in0=ps[bkt][:, :].rearrange("p (a b) -> p a b", a=4), in1=Er[:, :].rearrange("p (a b) -> p a b", a=4), op=ALU.mult),
                    reads=["Er"], writes=[("ps", bkt), "ks"])
                for half in range(2):
                    bv = self.bank()
                    for jj in range(2):
                        t16 = tt * 4 + half * 2 + jj
                        for c in range(NCH):
                            mk.op("pe", lambda e, bv=bv, jj=jj, t16=t16, c=c: e.matmul(
                                ps[bv][:, jj * 256:(jj + 1) * 256], lhsT=hT[:, c, t16 * 128:(t16 + 1) * 128], rhs=wv[:, c, :], start=(c == 0), stop=(c == NCH - 1)),
                                reads=["wv", ("hT", c, tt)], writes=[("ps", bv)])
                    t0 = tt * 4 + half * 2
                    mk.op("act", lambda e, bv=bv, t0=t0: e.activation(out=vv[:, t0:t0 + 2, :], in_=ps[bv][:, :].rearrange("p (a b) -> p a b", a=2), func=AF.Copy),
                          writes=[("ps", bv), "vv"])
            if h == 3:
                self.prefetch_next([("wq", 0), ("wk", 0), "wv"])
            LA = 2
            b1s = {}

            def rec_stage1(c):
                csl = slice(c * 128, (c + 1) * 128)
                b1 = self.bank()
                b1s[c] = b1
                mk.op("pe", lambda e, b1=b1, csl=csl: e.matmul(ps[b1][:, 0:128], lhsT=kinv[:, csl], rhs=qi[:, csl], start=True, stop=True),
                      reads=["kinv", "qi"], writes=[("ps", b1)])
                if c < 15:
                    mk.op("pe", lambda e, b1=b1, c=c: e.matmul(ps[b1][:, 128:384], lhsT=ks[:, c, :], rhs=vv[:, c, :], start=True, stop=True),
                          reads=["ks", "vv"], writes=[("ps", b1)])
                aq = c % 4
                mk.op("dve", lambda e, b1=b1, aq=aq: e.tensor_tensor(out=aTm[aq][:], in0=ps[b1][:, 0:128], in1=cf[:, C_TRI:C_TRI + 128], op=ALU.mult),
                      reads=["cf"], writes=[("ps", b1), ("aTm", aq)])

            def rec_stage2(c):
                csl = slice(c * 128, (c + 1) * 128)
                b1 = b1s[c]
                aq = c % 4
                bo = self.bank()
                sqn = c % 2
                for vc in range(2):
                    vsl = slice(vc * 128, (vc + 1) * 128)
                    if c > 0:
                        mk.op("pe", lambda e, bo=bo, vsl=vsl, sqn=sqn, csl=csl: e.matmul(ps[bo][:, vsl], lhsT=Sbf[sqn][:, vsl], rhs=qi[:, csl], start=True, stop=False),
                              reads=[("Sbf", sqn), "qi"], writes=[("ps", bo)])
                    mk.op("pe", lambda e, bo=bo, vsl=vsl, aq=aq, c=c: e.matmul(ps[bo][:, vsl], lhsT=vv[:, c, vsl], rhs=aTm[aq][:], start=(c == 0), stop=True),
                          reads=["vv"], rhs=[("aTm", aq)], writes=[("ps", bo)])
                if c < 15:
                    if c == 0:
                        mk.op("dve", lambda e, b1=b1: e.tensor_copy(out=S32[:], in_=ps[b1][:, 128:384]), writes=[("ps", b1), "S32"])
                    else:
                        mk.op("dve", lambda e, b1=b1, c=c: e.scalar_tensor_tensor(out=S32[:], in0=S32[:], scalar=dec[:, c:c + 1], in1=ps[b1][:, 128:384], op0=ALU.mult, op1=ALU.add),
                              reads=["dec"], writes=[("ps", b1), "S32"])
                    nq = (c + 1) % 2
                    mk.op("act", lambda e, nq=nq: e.activation(out=Sbf[nq][:], in_=S32[:], func=AF.Copy), reads=["S32"], writes=[("Sbf", nq)])
                mk.op("act", lambda e, bo=bo, csl=csl: e.activation(out=oTh[:, :, csl], in_=ps[bo][:, 0:256].rearrange("p (a b) -> p a b", a=2), func=AF.Copy),
                      writes=[("ps", bo), "oTh"])

            for step in range(16 + LA):
                if step < 16:
                    rec_stage1(step)
                if step >= LA:
                    rec_stage2(step - LA)
            hh = h % 2
            for vc in range(2):
                mk.dma("pool", lambda e, h=h, vc=vc: e.dma_start(out=wr[vc][:], in_=win3[:, :, 2048 + h * 256 + vc * 128:2048 + h * 256 + (vc + 1) * 128]),
                       slot=("wr", vc), writes=[("wr", vc)])
            for tt in range(4):
                tsl = slice(tt * 512, (tt + 1) * 512)
                bn = self.bank()
                brrs = [self.bank(), self.bank()]
                for vc in range(2):
                    mk.op("dve", lambda e, vc=vc, tsl=tsl: e.tensor_tensor(out=sq[vc][:], in0=oTh[:, vc, tsl], in1=oTh[:, vc, tsl], op=ALU.mult),
                          reads=["oTh"], writes=[("nsq", vc)])
                for vc in range(2):
                    brr = brrs[vc]
                    for c in range(NCH):
                        mk.op("pe", lambda e, brr=brr, c=c, vc=vc, tsl=tsl: e.matmul(ps[brr][:, :], lhsT=wr[vc][:, c, :], rhs=hT[:, c, tsl], start=(c == 0), stop=(c == NCH - 1)),
                              reads=[("wr", vc), ("hT", c, tt)], writes=[("ps", brr)])
                for vc in range(2):
                    mk.op("pe", lambda e, bn=bn, vc=vc: e.matmul(ps[bn][:, :], lhsT=cb[:, C_ONES:C_ONES + 128], rhs=sq[vc][:], start=(vc == 0), stop=(vc == 1)),
                          reads=["cb"], rhs=[("nsq", vc)], writes=[("ps", bn)])
                r = tt % 2
                mk.op("act", lambda e, bn=bn, r=r: e.activation(out=rs[r][:], in_=ps[bn][:, :], func=AF.Ln, scale=1.0 / 256, bias=cf[:, C_EPS:C_EPS + 1]),
                      reads=["cf"], writes=[("ps", bn), ("nrs", r)])
                mk.op("act", lambda e, r=r: e.activation(out=rs[r][:], in_=rs[r][:], func=AF.Exp, scale=-0.5), writes=[("nrs", r)])
                for vc in range(2):
                    brr = brrs[vc]
                    mk.op("act", lambda e, brr=brr, vc=vc: e.activation(out=sr[vc][:], in_=ps[brr][:, :], func=AF.Silu), writes=[("ps", brr), ("sr", vc)])
                    mk.op("dve", lambda e, vc=vc, r=r, tsl=tsl: e.scalar_tensor_tensor(
                        out=t1[vc][:], in0=oTh[:, vc, tsl], scalar=gnorm[:, jb * 2 + vc:jb * 2 + vc + 1], in1=rs[r][:], op0=ALU.mult, op1=ALU.mult),
                        reads=["oTh", ("nrs", r), "gnorm"], writes=[("t1", vc)])
                    mk.op("dve", lambda e, vc=vc, tsl=tsl, hh=hh: e.tensor_tensor(out=yT[:, hh * 2 + vc, tsl], in0=t1[vc][:], in1=sr[vc][:], op=ALU.mult),
                          reads=[("t1", vc), ("sr", vc)], writes=[("yT", hh * 2 + vc, tt)])
            if hh == 1:
                pair = h // 2
                if pair == 0:
                    for dc in range(NCH):
                        w = wo[dc % 2]
                        wkk = ("woB", dc % 2)
                        mk.dma("pool", lambda e, w=w, dc=dc, pair=pair: e.dma_start(out=w[:], in_=wo3[:, pair * 4:(pair + 1) * 4, dc * 128:(dc + 1) * 128]), slot=wkk, writes=[wkk])
                        for tt in range(4):
                            tsl = slice(tt * 512, (tt + 1) * 512)
                            b = self.bank()
                            for kc in range(4):
                                mk.op("pe", lambda e, b=b, kc=kc, w=w, tsl=tsl: e.matmul(ps[b][:, :], lhsT=w[:, kc, :], rhs=yT[:, kc, tsl], start=(kc == 0), stop=(kc == 3)),
                                      reads=[wkk], rhs=[("yT", kc, tt)], writes=[("ps", b)])
                            mk.op("dve", lambda e, b=b, dc=dc, tsl=tsl: e.tensor_tensor(out=xT[:, dc, tsl], in0=ps[b][:, :], in1=xT[:, dc, tsl], op=ALU.add),
                                  writes=[("ps", b), ("x", dc, tt)])
                else:
                    it = 0
                    for half in range(2):
                        for dc in range(NCH):
                            w = wo[it % 2]
                            wkk = ("woB", it % 2)
                            it += 1
                            mk.dma("pool", lambda e, w=w, dc=dc, pair=pair: e.dma_start(out=w[:], in_=wo3[:, pair * 4:(pair + 1) * 4, dc * 128:(dc + 1) * 128]), slot=wkk, writes=[wkk])
                            for tt in (2 * half, 2 * half + 1):
                                tsl = slice(tt * 512, (tt + 1) * 512)
                                b = self.bank()
                                for kc in range(4):
                                    mk.op("pe", lambda e, b=b, kc=kc, w=w, tsl=tsl: e.matmul(ps[b][:, :], lhsT=w[:, kc, :], rhs=yT[:, kc, tsl], start=(kc == 0), stop=(kc == 3)),
                                          reads=[wkk], rhs=[("yT", kc, tt)], writes=[("ps", b)])
                                mk.op("dve", lambda e, b=b, dc=dc, tsl=tsl: e.tensor_tensor(out=xT[:, dc, tsl], in0=ps[b][:, :], in1=xT[:, dc, tsl], op=ALU.add),
                                      writes=[("ps", b), ("x", dc, tt)])
                            if half == 1:
                                if dc == 0:
                                    self.norm_S(0)
                                elif dc == 2:
                                    self.norm_S(1)
                                elif dc == 4:
                                    self.norm_H(0)
                                elif dc == 6:
                                    self.norm_H(1)
                    self.defer_norm([("S", 2), ("S", 3), ("H", 2), ("H", 3)])

    def phase_F(self, s, l):
        mk, ps, xT, hT, cf, cb, convp = self.mk, self.ps, self.xT, self.hT, self.cf, self.cb, self.convp
        T = self.T
        pf = self.alloc_pf()
        yT = T("yF", [128, FGRP, S], BF16)
        asb = [T("asb%d" % i, [128, 514], F32) for i in range(2)]
        ct = [T("ct%d" % i, [128, 512], F32) for i in range(2)]
        sb = [T("sb%d" % i, [128, 512], BF16) for i in range(2)]
        win = [pf[:, 0:2048].rearrange("p (c n) -> p c n", c=NCH), pf[:, 2048:4096].rearrange("p (c n) -> p c n", c=NCH),
               T("win2", [128, NCH, 256], BF16)[:]]
        wdr = T("wdr", [128, NCH, FGRP, 128], BF16)
        wd = [wdr[:, i, :, :] for i in range(2)]

        win3 = self.f_w_in[l].rearrange("(c p) n -> p c n", p=128)
        wd3 = self.f_w_down[l].rearrange("(c p) n -> p c n", p=128)

        def cp(k, fc):
            o = (l * 4 + k) * NFC + fc
            return convp[:, o:o + 1]
        it = 0
        nt = 0

        def load_win(fc):
            w = win[fc % 3]
            wk = ("win", fc % 3)
            mk.dma("pool", lambda e, w=w, fc=fc: e.dma_start(out=w[:, :, 0:128], in_=win3[:, :, fc * 128:(fc + 1) * 128]), slot=wk, writes=[wk])
            mk.dma("pool", lambda e, w=w, fc=fc: e.dma_start(out=w[:, :, 128:256], in_=win3[:, :, FFN + fc * 128:FFN + (fc + 1) * 128]), slot=wk, writes=[wk])
        if not self.was_prefetched:
            load_win(0)
        load_win(1)
        self.flush_pending()
        for grp in range(2):
            for fi in range(FGRP):
                if grp == 0 and fi == 8:
                    for dc in range(2):
                        mk.dma("pool", lambda e, dc=dc: e.dma_start(out=wd[dc], in_=wd3[:, 0:FGRP, dc * 128:(dc + 1) * 128]), slot=("wd", dc), writes=[("wd", dc)])
                if grp == 1 and 1 <= fi <= 8:
                    dcl = fi - 1
                    mk.dma("pool", lambda e, dc=dcl: e.dma_start(out=wdr[:, dc, :, :], in_=wd3[:, FGRP:2 * FGRP, dc * 128:(dc + 1) * 128]), slot=("wdr", dcl), writes=[("wdr", dcl)] + ([("wd", dcl)] if dcl < 2 else []))
                fc = grp * FGRP + fi
                w = win[it % 3]
                wk = ("win", it % 3)
                it += 1
                if fc + 2 < NFC:
                    load_win(fc + 2)
                for tt in range(4):
                    tsl = slice(tt * 512, (tt + 1) * 512)
                    q = nt % 2
                    nt += 1
                    a_cur, a_prev = asb[q], asb[1 - q]
                    ba, bu = self.bank(), self.bank()
                    for c in range(NCH):
                        mk.op("pe", lambda e, ba=ba, c=c, w=w, tsl=tsl: e.matmul(ps[ba][:, :], lhsT=w[:, c, 0:128], rhs=hT[:, c, tsl], start=(c == 0), stop=(c == NCH - 1)),
                              reads=[wk, ("hT", c, tt)], writes=[("ps", ba)])
                    for c in range(NCH):
                        mk.op("pe", lambda e, bu=bu, c=c, w=w, tsl=tsl: e.matmul(ps[bu][:, :], lhsT=w[:, c, 128:256], rhs=hT[:, c, tsl], start=(c == 0), stop=(c == NCH - 1)),
                              reads=[wk, ("hT", c, tt)], writes=[("ps", bu)])
                    if tt == 0:
                        mk.op("pool", lambda e, a_cur=a_cur: e.memset(a_cur[:, 0:2], 0.0), writes=[("asb", q)])
                    else:
                        mk.op("pool", lambda e, a_cur=a_cur, a_prev=a_prev: e.tensor_copy(out=a_cur[:, 0:2], in_=a_prev[:, 512:514]),
                              reads=[("asb", 1 - q)], writes=[("asb", q)])
                    mk.op("act", lambda e, ba=ba, a_cur=a_cur: e.activation(out=a_cur[:, 2:514], in_=ps[ba][:, :], func=AF.Copy), writes=[("ps", ba), ("asb", q)])
                    mk.op("act", lambda e, ba=ba, q=q, fc=fc: e.activation(out=ct[q][:], in_=ps[ba][:, :], func=AF.Identity, scale=cp(2, fc), bias=cp(3, fc)),
                          reads=["convp"], writes=[("ps", ba), ("ct", q)])
                    mk.op("dve", lambda e, q=q, a_cur=a_cur, fc=fc: e.scalar_tensor_tensor(out=ct[q][:], in0=a_cur[:, 1:513], scalar=cp(1, fc), in1=ct[q][:], op0=ALU.mult, op1=ALU.add),
                          reads=[("asb", q), "convp"], writes=[("ct", q)])
                    mk.op("dve", lambda e, q=q, a_cur=a_cur, fc=fc: e.scalar_tensor_tensor(out=ct[q][:], in0=a_cur[:, 0:512], scalar=cp(0, fc), in1=ct[q][:], op0=ALU.mult, op1=ALU.add),
                          reads=[("asb", q), "convp"], writes=[("ct", q)])
                    mk.op("act", lambda e, q=q: e.activation(out=sb[q][:], in_=ct[q][:], func=AF.Silu), reads=[("ct", q)], writes=[("sb", q)])
                    mk.op("dve", lambda e, q=q, bu=bu, fi=fi, tsl=tsl: e.tensor_tensor(out=yT[:, fi, tsl], in0=ps[bu][:, :], in1=sb[q][:], op=ALU.mult),
                          reads=[("sb", q)], writes=[("ps", bu), ("yF", fi, tt)])
            if grp == 1:
                self.prefetch_next([("win", 0), ("win", 1)])
            if grp == 0:
                for dc in range(NCH):
                    w = wd[dc % 2]
                    wkk = ("wd", dc % 2)
                    if dc >= 2:
                        mk.dma("pool", lambda e, w=w, dc=dc, grp=grp: e.dma_start(out=w, in_=wd3[:, grp * FGRP:(grp + 1) * FGRP, dc * 128:(dc + 1) * 128]), slot=wkk, writes=[wkk])
                    for tt in range(4):
                        tsl = slice(tt * 512, (tt + 1) * 512)
                        b = self.bank()
                        for fi in range(FGRP):
                            mk.op("pe", lambda e, b=b, fi=fi, w=w, tsl=tsl: e.matmul(ps[b][:, :], lhsT=w[:, fi, :], rhs=yT[:, fi, tsl], start=(fi == 0), stop=(fi == FGRP - 1)),
                                  reads=[wkk], rhs=[("yF", fi, tt)], writes=[("ps", b)])
                        mk.op("dve", lambda e, b=b, dc=dc, tsl=tsl: e.tensor_tensor(out=xT[:, dc, tsl], in0=ps[b][:, :], in1=xT[:, dc, tsl], op=ALU.add),
                              writes=[("ps", b), ("x", dc, tt)])
            else:
                for tt in range(4):
                    tsl = slice(tt * 512, (tt + 1) * 512)
                    for dc in range(NCH):
                        b = self.bank()
                        for fi in range(FGRP):
                            mk.op("pe", lambda e, b=b, fi=fi, dc=dc, tsl=tsl: e.matmul(ps[b][:, :], lhsT=wdr[:, dc, fi, :], rhs=yT[:, fi, tsl], start=(fi == 0), stop=(fi == FGRP - 1)),
                                  reads=[("wdr", dc)], rhs=[("yF", fi, tt)], writes=[("ps", b)])
                        mk.op("dve", lambda e, b=b, dc=dc, tsl=tsl: e.tensor_tensor(out=xT[:, dc, tsl], in0=ps[b][:, :], in1=xT[:, dc, tsl], op=ALU.add),
                              writes=[("ps", b), ("x", dc, tt)])
                        if tt >= 1 and dc == 1:
                            self.norm_S(tt - 1)
                        if tt >= 1 and dc == 5:
                            self.norm_H(tt - 1)
                self.defer_norm([("S", 3), ("H", 3)])

    def phase_store(self, s):
        mk, ps, xT, cf, cb, gains = self.mk, self.ps, self.xT, self.cf, self.cb, self.gains
        T = self.T
        self.flush_pending()
        sq, rs = self.nsq, self.nrs
        of = [T("of%d" % i, [128, 512], F32) for i in range(2)]
        ost = [T("ost%d" % i, [128, 4, D], F32) for i in range(2)]
        toks = []
        for tt in range(4):
            tsl = slice(tt * 512, (tt + 1) * 512)
            r = tt % 2
            if self.final_norm:
                b = self.bank()
                for c in range(NCH):
                    q = c % 2
                    mk.op("act", lambda e, c=c, q=q, tsl=tsl: e.activation(out=sq[q][:], in_=xT[:, c, tsl], func=AF.Square),
                          reads=[("x", c, tt)], writes=[("nsq", q)])
                    mk.op("pe", lambda e, b=b, c=c, q=q: e.matmul(ps[b][:, :], lhsT=cb[:, C_ONES:C_ONES + 128], rhs=sq[q][:], start=(c == 0), stop=(c == NCH - 1)),
                          reads=["cb"], rhs=[("nsq", q)], writes=[("ps", b)])
                mk.op("act", lambda e, b=b, r=r: e.activation(out=rs[r][:], in_=ps[b][:, :], func=AF.Ln, scale=1.0 / D, bias=cf[:, C_EPS:C_EPS + 1]),
                      reads=["cf"], writes=[("ps", b), ("nrs", r)])
                mk.op("act", lambda e, r=r: e.activation(out=rs[r][:], in_=rs[r][:], func=AF.Exp, scale=-0.5), writes=[("nrs", r)])
            o = ost[r]
            for c in range(NCH):
                q = c % 2
                if self.final_norm:
                    mk.op("dve", lambda e, c=c, q=q, r=r, tsl=tsl: e.scalar_tensor_tensor(
                        out=of[q][:], in0=xT[:, c, tsl], scalar=gains[:, 64 + c:65 + c], in1=rs[r][:], op0=ALU.mult, op1=ALU.mult),
                        reads=[("x", c, tt), ("nrs", r), "gains"], writes=[("of", q)])
                    src = of[q]
                    srck = [("of", q)]
                b2 = self.bank()
                for j in range(4):
                    if self.final_norm:
                        inp = src[:, j * 128:(j + 1) * 128]
                    else:
                        inp = xT[:, c, tt * 512 + j * 128:tt * 512 + (j + 1) * 128]
                        srck = [("x", c, tt)]
                    mk.op("pe", lambda e, b2=b2, j=j, inp=inp: e.transpose(out=ps[b2][:, j * 128:(j + 1) * 128], in_=inp, identity=cf[:, C_ID:C_ID + 128]),
                          reads=srck + ["cf"], writes=[("ps", b2)])
                mk.op("act", lambda e, b2=b2, c=c, o=o: e.activation(out=o[:, :, c * 128:(c + 1) * 128], in_=ps[b2][:, :].rearrange("p (a b) -> p a b", a=4), func=AF.Copy),
                      writes=[("ps", b2), ("ost", r)])
            toks.append(mk.dma("sp", lambda e, o=o, tsl=tsl: e.dma_start(out=self.y[s, tsl, :].rearrange("(j p) n -> p j n", p=128), in_=o[:]),
                               slot=("ost", r), reads=[("ost", r)]))
        mk.wait("sp", toks)


def make_consts():
    c = np.zeros((128, NCONST), np.float32)
    idx = np.arange(128)
    c[:, C_ID:C_ID + 128] = np.eye(128, dtype=np.float32)
    c[:, C_TRI:C_TRI + 128] = (idx[:, None] <= idx[None, :]).astype(np.float32)
    c[:, C_TRS:C_TRS + 128] = (idx[:, None] > idx[None, :]).astype(np.float32)
    mc = np.where(idx[:, None] <= idx[None, :], 0.0, NEG).astype(np.float32)
    mp = np.where(idx[:, None] >= idx[None, :], 0.0, NEG).astype(np.float32)
    c[:, C_MASK:C_MASK + 512] = np.concatenate([mc, mc, mp, mp], axis=1)
    c[:, C_ONES:C_ONES + 128] = 1.0
    fi = (idx % 32).astype(np.float64)
    inv = 10000.0 ** (-(2.0 * fi) / 64.0)
    c[:, C_INV] = (inv / TWO_PI).astype(np.float32)
    sc = TWO_PI * (1.0 - 1e-6)
    c[:, C_SGN] = np.where((idx % 64) < 32, -sc, sc).astype(np.float32)
    c[:, C_EPS] = 1e-6
    c[:, C_2PI] = sc
    c[:, C_ONE] = 1.0
    c[:, C_PERM:C_PERM + 128] = (idx[:, None] == (idx[None, :] ^ 32)).astype(np.float32)
    return c


def full_plan(nseq):
    plan = []
    for s in range(nseq):
        plan.append(("load", s))
        for l in range(DEPTH):
            plan.append(("A" if l % 2 == 0 else "B", s, l))
            plan.append(("F", s, l))
        plan.append(("store", s))
    return plan


def layout_params(inp):
    f32 = np.float32
    gains = np.zeros((128, 72), f32)
    gains[:, 0:32] = np.asarray(inp["norm_mix"], f32).reshape(4, 8, 128).transpose(2, 0, 1).reshape(128, 32)
    gains[:, 32:64] = np.asarray(inp["norm_ffn"], f32).reshape(4, 8, 128).transpose(2, 0, 1).reshape(128, 32)
    gains[:, 64:72] = np.asarray(inp["norm_final"], f32).reshape(8, 128).T
    cw = np.asarray(inp["f_conv_w"], f32).reshape(4, 3, NFC, 128)
    cbias = np.asarray(inp["f_conv_b"], f32).reshape(4, 1, NFC, 128)
    convp = np.concatenate([cw, cbias], axis=1).transpose(3, 0, 1, 2).reshape(128, 4 * 4 * NFC)
    gnorm = np.asarray(inp["b_g_norm"], f32).reshape(2, 2, 128).transpose(2, 0, 1).reshape(128, 4)
    wgu = np.zeros((32, 2, 512), f32)
    wgu[0:16] = np.asarray(inp["b_w_gate_up"], f32).transpose(1, 0, 2)
    wgu[16] = np.asarray(inp["b_b_gate_up"], f32)
    return dict(gains=np.ascontiguousarray(gains), convp=np.ascontiguousarray(convp),
                gnorm=np.ascontiguousarray(gnorm), wgu=np.ascontiguousarray(wgu.reshape(32, 1024)))


def kernel(**inputs):
    n = 8
    nseq = 2
    bld = Builder(nseq, full_plan(nseq))
    nc = bld.build()
    shared = layout_params(inputs)
    shared["cst"] = make_consts()
    for k in ("a_w_qkv", "a_w_o", "b_w_in", "b_w_o", "f_w_in", "f_w_down"):
        shared[k] = np.ascontiguousarray(np.asarray(inputs[k], np.float32))
    x = np.asarray(inputs["x"], np.float32)
    pos = np.asarray(inputs["positions"], np.int32)
    in_maps = []
    for i in range(n):
        m = dict(shared)
        m["x"] = np.ascontiguousarray(x[i * nseq:(i + 1) * nseq])
        m["pos"] = np.ascontiguousarray(pos[i * nseq:(i + 1) * nseq])
        in_maps.append(m)
    res = run_bass_kernel_spmd(nc, in_maps, core_ids=list(range(n)))
    return np.concatenate([r["y"] for r in res.results], axis=0)
```
